# Optimizing a Trainium2 kernel written in Bass

```python
import jax, jax.numpy as jnp
from jax import lax
import numpy as np


D_MODEL = 1024
BATCH = 8
SEQ = 2048
DEPTH = 4

GRID_W = 64
CTX_LEN = 256
N_MIXERS = 4
EPS = 1e-6
NEG = -1e30
ROPE_THETA = 10000.0
BLOCK = 128
D_FF = 2816
FFN_RESIDUAL_WEIGHT = 0.5
N_MOD = 9
N_A = (DEPTH + 3) // 4
N_B = (DEPTH + 2) // 4
N_C = (DEPTH + 1) // 4
N_D = DEPTH // 4
NA_HEADS = 16
NA_HEAD_DIM = D_MODEL // NA_HEADS
NA_WIN_R = 8
NA_WIN_C = 16
NA_Q_BLK_C = 16
NA_KEY_BLK_C = 32
SW_HEADS = 16
SW_KV_HEADS = 2
SW_HEAD_DIM = 64
SW_WINDOW = 128
SW_QKV_DIM = (SW_HEADS + 2 * SW_KV_HEADS) * SW_HEAD_DIM
CONV_WIDTH = 31
GA_HEADS = 8
GA_KV_HEADS = 2
GA_HEAD_DIM = 128
GA_QKV_DIM = (GA_HEADS + 2 * GA_KV_HEADS) * GA_HEAD_DIM

kernel_name = 'hybrid_interleaved_diffusion_backbone'


def rmsnorm(x, g):
    xf = x.astype(jnp.float32)
    y = xf * lax.rsqrt(jnp.mean(xf * xf, axis=-1, keepdims=True) + EPS)
    return (y * g.astype(jnp.float32)).astype(x.dtype)


def layernorm(x, g, b):
    xf = x.astype(jnp.float32)
    mu = jnp.mean(xf, axis=-1, keepdims=True)
    var = jnp.mean(jnp.square(xf - mu), axis=-1, keepdims=True)
    y = (xf - mu) * lax.rsqrt(var + EPS)
    return (y * g.astype(jnp.float32) + b.astype(jnp.float32)).astype(x.dtype)


def modulated_norm(x, g, shift, scale):
    return rmsnorm(x, g) * (1.0 + scale) + shift


def gated_residual(x, y, g_post, gate, weight):
    return x + weight * gate * rmsnorm(y, g_post)


def swiglu(h, w_gate, w_up, w_down):
    return (jax.nn.silu(h @ w_gate) * (h @ w_up)) @ w_down


def macaron_half_ffn(x, m, slot, g_pre, g_post, w_gate, w_up, w_down):
    h = modulated_norm(x, g_pre, m[:, :, 3 * slot], m[:, :, 3 * slot + 1])
    y = swiglu(h, w_gate, w_up, w_down)
    return gated_residual(x, y, g_post, m[:, :, 3 * slot + 2], FFN_RESIDUAL_WEIGHT)


def axial_rope_angles(n_tokens, head_dim):
    t = jnp.arange(n_tokens)
    row = (t // GRID_W).astype(jnp.float32)
    col = (t % GRID_W).astype(jnp.float32)
    n_pairs_axis = head_dim // 4
    freq = ROPE_THETA ** (-jnp.arange(n_pairs_axis, dtype=jnp.float32) / n_pairs_axis)
    ang = jnp.concatenate([row[:, None] * freq, col[:, None] * freq], axis=-1)
    return jnp.cos(ang), jnp.sin(ang)


def apply_rope(x, cos, sin):
    xf = x.astype(jnp.float32).reshape(x.shape[:-1] + (-1, 2))
    x1, x2 = xf[..., 0], xf[..., 1]
    c = cos[None, :, None, :]
    s = sin[None, :, None, :]
    out = jnp.stack([x1 * c - x2 * s, x1 * s + x2 * c], axis=-1).reshape(x.shape)
    return out.astype(x.dtype)


def softmax_with_sink(s, sink):
    sk = jnp.broadcast_to(sink.astype(jnp.float32).reshape(1, SW_KV_HEADS, -1, 1, 1), s.shape[:-1] + (1,))
    return jax.nn.softmax(jnp.concatenate([sk, s], axis=-1), axis=-1)[..., 1:]


def neighborhood_attention(h_lat, h_ctx, w_qkv, w_o, rpb, with_ctx_out):
    B, S, _ = h_lat.shape
    C = h_ctx.shape[1]
    H, dh = NA_HEADS, NA_HEAD_DIM
    qd = H * dh
    rows = S // GRID_W
    kr = min(NA_WIN_R, rows)
    scale = dh ** -0.5
    n_cb = GRID_W // NA_Q_BLK_C
    qkv = (h_lat @ w_qkv).reshape(B, rows, GRID_W, 3, H, dh)
    q, k, v = qkv[:, :, :, 0], qkv[:, :, :, 1], qkv[:, :, :, 2]
    kv_c = (h_ctx @ w_qkv[:, qd:]).reshape(B, C, 2, H, dh)
    k_c, v_c = kv_c[:, :, 0], kv_c[:, :, 1]
    jcol = np.arange(GRID_W)
    qcol = jcol.reshape(n_cb, NA_Q_BLK_C)
    win_start = np.clip(jcol - NA_WIN_C // 2, 0, GRID_W - NA_WIN_C).reshape(n_cb, NA_Q_BLK_C)
    blk_start = np.clip(qcol[:, 0] - NA_WIN_C // 2, 0, GRID_W - NA_KEY_BLK_C)
    key_cols = blk_start[:, None] + np.arange(NA_KEY_BLK_C)
    kc3 = key_cols[:, None, :]
    col_valid = (kc3 >= win_start[:, :, None]) & (kc3 < win_start[:, :, None] + NA_WIN_C)
    col_idx = np.clip(kc3 - qcol[:, :, None] + NA_WIN_C - 1, 0, 2 * NA_WIN_C - 2)
    n_nb = kr * NA_KEY_BLK_C

    def row_fn(r):
        rs = jnp.clip(r - NA_WIN_R // 2, 0, rows - kr)
        k_blk = lax.dynamic_slice_in_dim(k, rs, kr, axis=1)[:, :, key_cols]
        v_blk = lax.dynamic_slice_in_dim(v, rs, kr, axis=1)[:, :, key_cols]
        q_r = lax.dynamic_index_in_dim(q, r, axis=1, keepdims=False).reshape(B, n_cb, NA_Q_BLK_C, H, dh)
        row_idx = rs + jnp.arange(kr) - r + NA_WIN_R - 1
        bias = rpb[:, row_idx][:, :, col_idx].transpose(2, 0, 3, 1, 4)
        s_nb = jnp.einsum('bnqhd,brnkhd->bnhqrk', q_r, k_blk, preferred_element_type=jnp.float32) * scale
        s_nb = jnp.where(col_valid[None, :, None, :, None, :], s_nb + bias.astype(jnp.float32), NEG)
        s_c = jnp.einsum('bnqhd,bchd->bnhqc', q_r, k_c, preferred_element_type=jnp.float32) * scale
        s = jnp.concatenate([s_nb.reshape(B, n_cb, H, NA_Q_BLK_C, n_nb), s_c], axis=-1)
        p = jax.nn.softmax(s, axis=-1).astype(v.dtype)
        p_nb = p[..., :n_nb].reshape(B, n_cb, H, NA_Q_BLK_C, kr, NA_KEY_BLK_C)
        o = (jnp.einsum('bnhqrk,brnkhd->bnqhd', p_nb, v_blk)
             + jnp.einsum('bnhqc,bchd->bnqhd', p[..., n_nb:], v_c))
        return o.reshape(B, GRID_W, qd)

    o = lax.map(row_fn, jnp.arange(rows))
    y_lat = o.transpose(1, 0, 2, 3).reshape(B, S, qd) @ w_o
    y_ctx = None
    if with_ctx_out:
        q_c = (h_ctx @ w_qkv[:, :qd]).reshape(B, C, H, dh)
        s = jnp.einsum('bqhd,bkhd->bhqk', q_c, k_c, preferred_element_type=jnp.float32) * scale
        p = jax.nn.softmax(s, axis=-1).astype(v_c.dtype)
        y_ctx = jnp.einsum('bhqk,bkhd->bqhd', p, v_c).reshape(B, C, qd) @ w_o
    return y_lat, y_ctx


def window_gqa_sink(h_lat, h_ctx, w_qkv, w_o, sink, cos, sin, with_ctx_out):
    B, S, _ = h_lat.shape
    C = h_ctx.shape[1]
    H, Hkv, dh = SW_HEADS, SW_KV_HEADS, SW_HEAD_DIM
    G = H // Hkv
    qd, kvd = H * dh, Hkv * dh
    scale = dh ** -0.5
    proj = h_lat @ w_qkv
    q = apply_rope(proj[..., :qd].reshape(B, S, H, dh), cos, sin).reshape(B, S, Hkv, G, dh)
    k = apply_rope(proj[..., qd:qd + kvd].reshape(B, S, Hkv, dh), cos, sin)
    v = proj[..., qd + kvd:].reshape(B, S, Hkv, dh)
    proj_c = h_ctx @ w_qkv[:, qd:]
    k_c = proj_c[..., :kvd].reshape(B, C, Hkv, dh)
    v_c = proj_c[..., kvd:].reshape(B, C, Hkv, dh)
    pad = ((0, 0), (BLOCK, BLOCK), (0, 0), (0, 0))
    k_pad, v_pad = jnp.pad(k, pad), jnp.pad(v, pad)
    nb = S // BLOCK
    q_blocks = q.reshape(B, nb, BLOCK, Hkv, G, dh).swapaxes(0, 1)

    def blk_fn(args):
        n, q_b = args
        start = n * BLOCK
        k_b = lax.dynamic_slice_in_dim(k_pad, start, 3 * BLOCK, axis=1)
        v_b = lax.dynamic_slice_in_dim(v_pad, start, 3 * BLOCK, axis=1)
        qpos = start + jnp.arange(BLOCK)
        kpos = start - BLOCK + jnp.arange(3 * BLOCK)
        valid = ((jnp.abs(kpos[None, :] - qpos[:, None]) <= SW_WINDOW)
                 & (kpos >= 0)[None, :] & (kpos < S)[None, :])
        s_w = jnp.einsum('bqkgd,bjkd->bkgqj', q_b, k_b, preferred_element_type=jnp.float32) * scale
        s_w = jnp.where(valid, s_w, NEG)
        s_c = jnp.einsum('bqkgd,bjkd->bkgqj', q_b, k_c, preferred_element_type=jnp.float32) * scale
        p = softmax_with_sink(jnp.concatenate([s_w, s_c], axis=-1), sink).astype(v.dtype)
        o = (jnp.einsum('bkgqj,bjkd->bqkgd', p[..., :3 * BLOCK], v_b)
             + jnp.einsum('bkgqj,bjkd->bqkgd', p[..., 3 * BLOCK:], v_c))
        return o.reshape(B, BLOCK, qd)

    o = lax.map(blk_fn, (jnp.arange(nb), q_blocks))
    y_lat = o.swapaxes(0, 1).reshape(B, S, qd) @ w_o
    y_ctx = None
    if with_ctx_out:
        q_c = (h_ctx @ w_qkv[:, :qd]).reshape(B, C, Hkv, G, dh)
        s = jnp.einsum('bqkgd,bjkd->bkgqj', q_c, k_c, preferred_element_type=jnp.float32) * scale
        p = softmax_with_sink(s, sink).astype(v_c.dtype)
        y_ctx = jnp.einsum('bkgqj,bjkd->bqkgd', p, v_c).reshape(B, C, qd) @ w_o
    return y_lat, y_ctx


def conformer_conv(h, w_pw1, b_pw1, w_dw, b_dw, ln_g, ln_b, w_pw2, b_pw2):
    D = h.shape[-1]
    u = h @ w_pw1 + b_pw1
    u = u[..., :D] * jax.nn.sigmoid(u[..., D:])
    u = lax.conv_general_dilated(u, w_dw[:, None, :], window_strides=(1,),
                                 padding=[(CONV_WIDTH // 2, CONV_WIDTH // 2)],
                                 dimension_numbers=('NWC', 'WIO', 'NWC'),
                                 feature_group_count=D) + b_dw
    u = jax.nn.silu(layernorm(u, ln_g, ln_b))
    return u @ w_pw2 + b_pw2


def global_gqa(h_lat, h_ctx, w_qkv, w_o, q_norm, k_norm, cos, sin, with_ctx_out):
    B, S, _ = h_lat.shape
    C = h_ctx.shape[1]
    H, Hkv, dh = GA_HEADS, GA_KV_HEADS, GA_HEAD_DIM
    G = H // Hkv
    qd, kvd = H * dh, Hkv * dh
    scale = dh ** -0.5
    proj = h_lat @ w_qkv
    q = apply_rope(rmsnorm(proj[..., :qd].reshape(B, S, H, dh), q_norm), cos, sin)
    k = apply_rope(rmsnorm(proj[..., qd:qd + kvd].reshape(B, S, Hkv, dh), k_norm), cos, sin)
    v = proj[..., qd + kvd:].reshape(B, S, Hkv, dh)
    proj_c = h_ctx @ w_qkv[:, qd:]
    k_c = rmsnorm(proj_c[..., :kvd].reshape(B, C, Hkv, dh), k_norm)
    v_c = proj_c[..., kvd:].reshape(B, C, Hkv, dh)
    k_all = jnp.concatenate([k, k_c], axis=1)
    v_all = jnp.concatenate([v, v_c], axis=1)
    nb = S // BLOCK
    q_blocks = q.reshape(B, nb, BLOCK, Hkv, G, dh).swapaxes(0, 1)

    def blk_fn(q_b):
        s = jnp.einsum('bqkgd,bjkd->bkgqj', q_b, k_all, preferred_element_type=jnp.float32) * scale
        p = jax.nn.softmax(s, axis=-1).astype(v_all.dtype)
        return jnp.einsum('bkgqj,bjkd->bqkgd', p, v_all).reshape(B, BLOCK, qd)

    o = lax.map(blk_fn, q_blocks)
    y_lat = o.swapaxes(0, 1).reshape(B, S, qd) @ w_o
    y_ctx = None
    if with_ctx_out:
        q_c = rmsnorm((h_ctx @ w_qkv[:, :qd]).reshape(B, C, H, dh), q_norm).reshape(B, C, Hkv, G, dh)
        s = jnp.einsum('bqkgd,bjkd->bkgqj', q_c, k_c, preferred_element_type=jnp.float32) * scale
        p = jax.nn.softmax(s, axis=-1).astype(v_c.dtype)
        y_ctx = jnp.einsum('bkgqj,bjkd->bqkgd', p, v_c).reshape(B, C, qd) @ w_o
    return y_lat, y_ctx


def setup_inputs(seed: int = 0) -> dict:
    key = jax.random.key(seed)
    ks = iter(jax.random.split(key, 32))
    D = D_MODEL

    def nrm(shape, s):
        return jax.random.normal(next(ks), shape, jnp.float32) * s

    return {
        'x': nrm((BATCH, SEQ, D), 1.0),
        'c': nrm((BATCH, D), 1.0),
        'ctx': nrm((BATCH, CTX_LEN, D), 1.0),
        'c_ctx': nrm((D,), 1.0),
        'mod_w': nrm((DEPTH, D, N_MOD * D), 0.5 * D ** -0.5),
        'mod_b': nrm((DEPTH, N_MOD * D), 0.02),
        'norm_g': 1.0 + nrm((DEPTH, 6, D), 0.02),
        'ffn_w_gate': nrm((DEPTH, 2, D, D_FF), D ** -0.5),
        'ffn_w_up': nrm((DEPTH, 2, D, D_FF), D ** -0.5),
        'ffn_w_down': nrm((DEPTH, 2, D_FF, D), D_FF ** -0.5),
        'na_w_qkv': nrm((N_A, D, 3 * NA_HEADS * NA_HEAD_DIM), D ** -0.5),
        'na_w_o': nrm((N_A, NA_HEADS * NA_HEAD_DIM, D), (NA_HEADS * NA_HEAD_DIM) ** -0.5),
        'na_rpb': nrm((N_A, NA_HEADS, 2 * NA_WIN_R - 1, 2 * NA_WIN_C - 1), 0.1),
        'sw_w_qkv': nrm((N_B, D, SW_QKV_DIM), D ** -0.5),
        'sw_w_o': nrm((N_B, SW_HEADS * SW_HEAD_DIM, D), (SW_HEADS * SW_HEAD_DIM) ** -0.5),
        'sw_sink': nrm((N_B, SW_HEADS), 0.5),
        'cv_w_pw1': nrm((N_C, D, 2 * D), D ** -0.5),
        'cv_b_pw1': nrm((N_C, 2 * D), 0.02),
        'cv_w_dw': nrm((N_C, CONV_WIDTH, D), CONV_WIDTH ** -0.5),
        'cv_b_dw': nrm((N_C, D), 0.02),
        'cv_ln_g': 1.0 + nrm((N_C, D), 0.02),
        'cv_ln_b': nrm((N_C, D), 0.02),
        'cv_w_pw2': nrm((N_C, D, D), D ** -0.5),
        'cv_b_pw2': nrm((N_C, D), 0.02),
        'ga_w_qkv': nrm((N_D, D, GA_QKV_DIM), D ** -0.5),
        'ga_w_o': nrm((N_D, GA_HEADS * GA_HEAD_DIM, D), (GA_HEADS * GA_HEAD_DIM) ** -0.5),
        'ga_q_norm': 1.0 + nrm((N_D, GA_HEAD_DIM), 0.02),
        'ga_k_norm': 1.0 + nrm((N_D, GA_HEAD_DIM), 0.02),
    }


def reference(x, c, ctx, c_ctx, mod_w, mod_b, norm_g, ffn_w_gate, ffn_w_up, ffn_w_down,
              na_w_qkv, na_w_o, na_rpb, sw_w_qkv, sw_w_o, sw_sink,
              cv_w_pw1, cv_b_pw1, cv_w_dw, cv_b_dw, cv_ln_g, cv_ln_b, cv_w_pw2, cv_b_pw2,
              ga_w_qkv, ga_w_o, ga_q_norm, ga_k_norm):
    B, S, D = x.shape
    cos_sw, sin_sw = axial_rope_angles(S, SW_HEAD_DIM)
    cos_ga, sin_ga = axial_rope_angles(S, GA_HEAD_DIM)
    xl, xc = x, ctx
    for i in range(DEPTH):
        kind, j = i % N_MIXERS, i // N_MIXERS
        last = i == DEPTH - 1
        ctx_in = not (last and kind == 2)
        ctx_out = not last
        ml = (jax.nn.silu(c) @ mod_w[i] + mod_b[i]).reshape(B, 1, N_MOD, D)
        mc = (jax.nn.silu(c_ctx) @ mod_w[i] + mod_b[i]).reshape(1, 1, N_MOD, D)
        g = norm_g[i]
        xl = macaron_half_ffn(xl, ml, 0, g[0], g[1], ffn_w_gate[i, 0], ffn_w_up[i, 0], ffn_w_down[i, 0])
        if ctx_in:
            xc = macaron_half_ffn(xc, mc, 0, g[0], g[1], ffn_w_gate[i, 0], ffn_w_up[i, 0], ffn_w_down[i, 0])
        hl = modulated_norm(xl, g[2], ml[:, :, 3], ml[:, :, 4])
        hc = modulated_norm(xc, g[2], mc[:, :, 3], mc[:, :, 4]) if ctx_in else None
        if kind == 0:
            yl, yc = neighborhood_attention(hl, hc, na_w_qkv[j], na_w_o[j], na_rpb[j], ctx_out)
        elif kind == 1:
            yl, yc = window_gqa_sink(hl, hc, sw_w_qkv[j], sw_w_o[j], sw_sink[j], cos_sw, sin_sw, ctx_out)
        elif kind == 2:
            yl = conformer_conv(hl, cv_w_pw1[j], cv_b_pw1[j], cv_w_dw[j], cv_b_dw[j],
                                cv_ln_g[j], cv_ln_b[j], cv_w_pw2[j], cv_b_pw2[j])
            yc = conformer_conv(hc, cv_w_pw1[j], cv_b_pw1[j], cv_w_dw[j], cv_b_dw[j],
                                cv_ln_g[j], cv_ln_b[j], cv_w_pw2[j], cv_b_pw2[j]) if ctx_out else None
        else:
            yl, yc = global_gqa(hl, hc, ga_w_qkv[j], ga_w_o[j], ga_q_norm[j], ga_k_norm[j],
                                cos_ga, sin_ga, ctx_out)
        xl = gated_residual(xl, yl, g[3], ml[:, :, 5], 1.0)
        xl = macaron_half_ffn(xl, ml, 2, g[4], g[5], ffn_w_gate[i, 1], ffn_w_up[i, 1], ffn_w_down[i, 1])
        if ctx_out:
            xc = gated_residual(xc, yc, g[3], mc[:, :, 5], 1.0)
            xc = macaron_half_ffn(xc, mc, 2, g[4], g[5], ffn_w_gate[i, 1], ffn_w_up[i, 1], ffn_w_down[i, 1])
    return xl
```

```python
import numpy as np
from contextlib import ExitStack
import concourse.bass as bass
import concourse.mybir as mybir
from concourse.bass_utils import run_bass_kernel_spmd

F32 = mybir.dt.float32
BF16 = mybir.dt.bfloat16
AF = mybir.ActivationFunctionType
ALU = mybir.AluOpType
AX = mybir.AxisListType

D = 1024
S = 2048
C = 256
NTOK = S + C
DFF = 2816
EPS = 1e-6
KC = D // 128
GRID_W = 64


class Prog:
    ENGS = ["tensor", "vector", "scalar", "gpsimd", "sync"]

    def __init__(self, nc, stack, n_dma_sems=48):
        self.nc = nc
        self.ops = {e: [] for e in self.ENGS}
        self.count = {e: 0 for e in self.ENGS}
        self.waited = {e: {} for e in self.ENGS}
        self.last_w = {}
        self.readers = {}
        self.n_dma_sems = n_dma_sems
        self.dma_i = 0
        self.dma_j = 0
        self.dma_sem_use = [0] * n_dma_sems
        self.sems = {e: stack.enter_context(nc.semaphore("s_" + e)) for e in self.ENGS}
        self.dsems = [stack.enter_context(nc.semaphore("d_%d" % i)) for i in range(n_dma_sems)]
        self.out_evs = []

    def _need(self, eng, ev, waits):
        if ev is None:
            return
        if ev[0] == "e" and ev[1] == eng and eng == "tensor":
            return
        key = (ev[0], ev[1])
        if self.waited[eng].get(key, 0) >= ev[2]:
            return
        self.waited[eng][key] = ev[2]
        waits[key] = max(waits.get(key, 0), ev[2])

    def op(self, eng, fn, reads=(), writes=(), dma=False):
        writes = list(writes) + [r for r in reads if r.startswith("PSUM_")]
        reads = [r for r in reads if not r.startswith("PSUM_")]
        waits = {}
        for r in reads:
            self._need(eng, self.last_w.get(r), waits)
        for w in writes:
            self._need(eng, self.last_w.get(w), waits)
            for ev in self.readers.get(w, ()):
                self._need(eng, ev, waits)
        if dma:
            half = self.n_dma_sems // 2
            if eng == "gpsimd":
                si = half + self.dma_j % half
                self.dma_j += 1
            else:
                si = self.dma_i % half
                self.dma_i += 1
            if self.dma_sem_use[si] > 0:
                self._need(eng, ("d", si, 16 * self.dma_sem_use[si]), waits)
            self.dma_sem_use[si] += 1
            ev = ("d", si, 16 * self.dma_sem_use[si])
        else:
            self.count[eng] += 1
            ev = ("e", eng, self.count[eng])
        self.ops[eng].append((fn, list(waits.items()), ev))
        for r in reads:
            self.readers.setdefault(r, []).append(ev)
        for w in writes:
            self.last_w[w] = ev
            self.readers[w] = []
        return ev

    def barrier(self):
        evs = [("e", e, self.count[e]) for e in self.ENGS if self.count[e] > 0]
        evs += [("d", si, 16 * u) for si, u in enumerate(self.dma_sem_use) if u > 0]
        for e in self.ENGS:
            waits = {}
            for ev in evs:
                if ev[0] == "e" and ev[1] == e:
                    continue
                self._need(e, ev, waits)
            if waits:
                self.ops[e].append((None, list(waits.items()), None))
        self.last_w = {}
        self.readers = {}

    def flush(self):
        nc = self.nc
        with nc.Block() as block:
            def run(engname):
                def body(eng):
                    for fn, waits, ev in self.ops[engname]:
                        for (kind, k), val in waits:
                            sem = self.sems[k] if kind == "e" else self.dsems[k]
                            eng.wait_ge(sem, val)
                        if fn is None:
                            continue
                        ins = fn(eng)
                        if ev[0] == "e":
                            ins.then_inc(self.sems[ev[1]], 1)
                        else:
                            ins.then_inc(self.dsems[ev[1]], 16)
                return body
            block.tensor(run("tensor"))
            block.vector(run("vector"))
            block.scalar(run("scalar"))
            block.gpsimd(run("gpsimd"))
            block.sync(run("sync"))
        self.ops = {e: [] for e in self.ENGS}


class Ring:
    def __init__(self, items):
        self.items = items
        self.i = 0

    def next(self):
        it = self.items[self.i % len(self.items)]
        self.i += 1
        return it


class KB:
    def __init__(self, nc, st):
        self.nc = nc
        self.st = st
        self.P = Prog(nc, st)
        self.uid = 0

    def sb(self, st, name, shape, dt):
        self.uid += 1
        nm = "%s_%d" % (name, self.uid)
        return st.enter_context(self.nc.sbuf_tensor(nm, list(shape), dt)), nm

    def ps(self, st, name, shape, dt):
        self.uid += 1
        nm = "PSUM_%s_%d" % (name, self.uid)
        return st.enter_context(self.nc.psum_tensor(nm, list(shape), dt)), nm

    def ring(self, st, name, shape, dt, n, psum=False):
        return Ring([(self.ps if psum else self.sb)(st, name, shape, dt) for _ in range(n)])


def tok_macros(t0, ntok, width=512):
    out = []
    o = 0
    while o < ntok:
        n = min(width, ntok - o)
        out.append((t0 + o, n))
        o += n
    return out


def stage_consts(kb, st):
    nc, P = kb.nc, kb.P
    identf, n_if = kb.sb(st, "identf", [128, 128], F32)
    ident, n_i = kb.sb(st, "ident", [128, 128], BF16)
    nh, n_nh = kb.sb(st, "neghalf", [128, 8], F32)
    P.op("gpsimd", lambda e: e.memset(identf[:], 1.0), writes=[n_if])
    P.op("gpsimd", lambda e: e.affine_select(out=identf[:], in_=identf[:], pattern=[[-1, 128]],
                                              compare_op=ALU.is_equal, fill=0.0, base=0, channel_multiplier=1),
         reads=[n_if], writes=[n_if])
    P.op("vector", lambda e: e.tensor_copy(out=ident[:], in_=identf[:]), reads=[n_if], writes=[n_i])
    P.op("gpsimd", lambda e: e.memset(nh[:], -0.5), writes=[n_nh])
    kb.ident, kb.n_ident = ident, n_i
    kb.identf, kb.n_identf = identf, n_if
    kb.nh, kb.n_nh = nh, n_nh
    NEGB = -30000.0
    for nm, pat, cm in (("masklo", [[-1, 128]], 1), ("maskhi", [[1, 128]], -1)):
        mf, n_mf = kb.sb(st, nm + "f", [128, 128], F32)
        mb, n_mb = kb.sb(st, nm, [128, 128], BF16)
        P.op("gpsimd", lambda e, mf=mf: e.memset(mf[:], 0.0), writes=[n_mf])
        P.op("gpsimd", lambda e, mf=mf, pat=pat, cm=cm: e.affine_select(out=mf[:], in_=mf[:], pattern=pat, compare_op=ALU.is_ge,
                                                                        fill=NEGB, base=0, channel_multiplier=cm),
             reads=[n_mf], writes=[n_mf])
        P.op("vector", lambda e, mf=mf, mb=mb: e.tensor_copy(out=mb[:], in_=mf[:]), reads=[n_mf], writes=[n_mb])
        setattr(kb, nm, mb)
        setattr(kb, "n_" + nm, n_mb)
    nh2, n_nh2 = kb.sb(st, "neghalf2", [128, 16], F32)
    P.op("gpsimd", lambda e: e.memset(nh2[:], -0.5), writes=[n_nh2])
    kb.nh2, kb.n_nh2 = nh2, n_nh2


def rstd_from_ssq(kb, ssq, n_ssq, ncols, inv_n):
    P = kb.P
    P.op("vector", lambda e: e.tensor_scalar(out=ssq[:, 0:ncols], in0=ssq[:, 0:ncols], scalar1=inv_n, scalar2=EPS,
                                             op0=ALU.mult, op1=ALU.add), reads=[n_ssq], writes=[n_ssq])
    P.op("gpsimd", lambda e: e.tensor_tensor(out=ssq[:, 0:ncols], in0=ssq[:, 0:ncols], in1=kb.nh[:, 0:ncols],
                                             op=ALU.pow), reads=[n_ssq, kb.n_nh], writes=[n_ssq])


def stage_mod(kb, st, T, li, lst):
    nc, P = kb.nc, kb.P
    Afm, n_A = kb.sb(lst, "Afm", [128, 3, 2, 8], F32)
    Bfm, n_B = kb.sb(lst, "Bfm", [128, 3, 2, 8], F32)
    Gbc, n_G = kb.sb(lst, "Gbc", [128, 3, 2, 1024], F32)
    cf, n_cf = kb.sb(st, "cf", [128, 8, 2], F32)
    sc, n_sc = kb.sb(st, "sc", [128, 8, 2], F32)
    sbc, n_sbc = kb.sb(st, "sbc", [128, 2, 8, 128], F32)
    mbfm, n_mbfm = kb.sb(st, "mbfm", [128, 72], F32)
    gfm, n_gfm = kb.sb(st, "gfm", [128, 48], F32)
    modfm, n_modfm = kb.sb(st, "modfm", [128, 9, 8, 2], F32)
    wring = kb.ring(st, "modw", [128, 8, 512], F32, 2)
    bring = kb.ring(st, "modbb", [128, 512], F32, 2)
    gring = kb.ring(st, "gpost", [128, 512], F32, 2)
    tring = kb.ring(st, "modtmp", [128, 512], F32, 2)
    pfm, n_pfm = kb.ps(st, "pfm", [128, 9, 8, 2], F32)
    pbr = kb.ring(st, "pbc", [128, 512], F32, 2, psum=True)

    P.op("sync", lambda e: e.dma_start(out=cf[:], in_=T["cfm"][:, :, :]), writes=[n_cf], dma=True)
    P.op("sync", lambda e: e.dma_start(out=mbfm[:], in_=T["mod_b_fm"][li]), writes=[n_mbfm], dma=True)
    P.op("sync", lambda e: e.dma_start(out=gfm[:], in_=T["norm_g_fm"][li]), writes=[n_gfm], dma=True)
    P.op("scalar", lambda e: e.activation(out=sc[:], in_=cf[:], func=AF.Silu), reads=[n_cf], writes=[n_sc])
    for s in range(2):
        for kc in range(8):
            P.op("vector", lambda e, s=s, kc=kc: e.tensor_copy(out=sbc[:, s, kc, :],
                                                               in_=sc[:, kc, s:s + 1].to_broadcast([128, 128])),
                 reads=[n_sc], writes=[n_sbc])
    mw = T["mod_w"][li].rearrange("(kc p) n -> p kc n", p=128)
    wsub = [0.5, 1.0, 0.5]
    for slot in range(9):
        sub = slot // 3
        for half in range(2):
            col0 = slot * 1024 + half * 512
            (wt, n_wt) = wring.next()
            P.op("sync", lambda e, wt=wt, col0=col0: e.dma_start(out=wt[:], in_=mw[:, :, col0:col0 + 512]),
                 writes=[n_wt], dma=True)
            if slot % 3 != 2:
                def mm(e, wt=wt, slot=slot, half=half):
                    ins = None
                    for oc in range(4):
                        for kc in range(8):
                            ins = e.matmul(pfm[:, slot, half * 4 + oc, :], lhsT=wt[:, kc, oc * 128:(oc + 1) * 128],
                                           rhs=sc[:, kc, :], start=(kc == 0), stop=(kc == 7))
                    return ins
                P.op("tensor", mm, reads=[n_wt, n_sc], writes=[n_pfm])
            else:
                (bt, n_bt) = bring.next()
                (gt, n_gt) = gring.next()
                P.op("sync", lambda e, bt=bt, col0=col0: e.dma_start(
                    out=bt[:], in_=T["mod_b"][li, col0:col0 + 512].partition_broadcast(128)), writes=[n_bt], dma=True)
                P.op("sync", lambda e, gt=gt, sub=sub, half=half: e.dma_start(
                    out=gt[:], in_=T["norm_g"][li, 2 * sub + 1, half * 512:(half + 1) * 512].partition_broadcast(128)),
                    writes=[n_gt], dma=True)
                for s in range(2):
                    (pb, n_pb) = pbr.next()
                    (tt, n_tt) = tring.next()

                    def mm(e, wt=wt, s=s, pb=pb):
                        ins = None
                        for kc in range(8):
                            ins = e.matmul(pb[:], lhsT=sbc[:, s, kc, :], rhs=wt[:, kc, :], start=(kc == 0), stop=(kc == 7))
                        return ins
                    P.op("tensor", mm, reads=[n_wt, n_sbc], writes=[n_pb])
                    P.op("vector", lambda e, pb=pb, bt=bt, tt=tt: e.tensor_tensor(out=tt[:], in0=pb[:], in1=bt[:], op=ALU.add),
                         reads=[n_pb, n_bt], writes=[n_tt])
                    P.op("vector", lambda e, tt=tt, gt=gt, sub=sub, s=s, half=half: e.scalar_tensor_tensor(
                        out=Gbc[:, sub, s, half * 512:(half + 1) * 512], in0=tt[:], scalar=wsub[sub], in1=gt[:],
                        op0=ALU.mult, op1=ALU.mult), reads=[n_tt, n_gt], writes=[n_G])
    for slot in range(9):
        if slot % 3 == 2:
            continue
        for s in range(2):
            P.op("vector", lambda e, s=s, slot=slot: e.tensor_tensor(out=modfm[:, slot, :, s], in0=pfm[:, slot, :, s],
                                                                     in1=mbfm[:, slot * 8:slot * 8 + 8], op=ALU.add),
                 reads=[n_pfm, n_mbfm], writes=[n_modfm])
    for sub in range(3):
        for s in range(2):
            P.op("vector", lambda e, sub=sub, s=s: e.scalar_tensor_tensor(
                out=Afm[:, sub, s, :], in0=modfm[:, 3 * sub + 1, :, s], scalar=1.0,
                in1=gfm[:, (2 * sub) * 8:(2 * sub) * 8 + 8], op0=ALU.add, op1=ALU.mult),
                reads=[n_modfm, n_gfm], writes=[n_A])
            P.op("vector", lambda e, sub=sub, s=s: e.tensor_copy(out=Bfm[:, sub, s, :], in_=modfm[:, 3 * sub, :, s]),
                 reads=[n_modfm], writes=[n_B])
    return dict(Afm=Afm, n_A=n_A, Bfm=Bfm, n_B=n_B, Gbc=Gbc, n_G=n_G)


def stage_norm_T(kb, st, xsrc, tiles, M, sub, HT, n_HT, psum_ring=None, rings=(6, 2, 8)):
    nc, P = kb.nc, kb.P
    xr = kb.ring(st, "nx", [128, 1024], F32, rings[0])
    jr = kb.ring(st, "njunk", [128, 1024], BF16, rings[1])
    xnr = kb.ring(st, "nxn", [128, 1024], BF16, rings[2])
    sr = kb.ring(st, "nssq", [128, 8], F32, 2)
    ptr = psum_ring or kb.ring(st, "nptr", [128, 512], BF16, 2, psum=True)
    ev = 0
    for m0 in range(0, len(tiles), 4):
        grp = tiles[m0:m0 + 4]
        n = len(grp)
        s = 0 if grp[0] < 16 else 1
        assert all((t < 16) == (grp[0] < 16) for t in grp)
        (ssq, n_ssq) = sr.next()
        xts = []
        for j, tt in enumerate(grp):
            (xt, n_xt) = xr.next()
            (jk, n_jk) = jr.next()
            src, n_src = xsrc(tt)
            P.op("sync", lambda e, xt=xt, src=src: e.dma_start(out=xt[:], in_=src), reads=[n_src], writes=[n_xt], dma=True)
            P.op("scalar", lambda e, xt=xt, jk=jk, ssq=ssq, j=j: e.activation(out=jk[:], in_=xt[:], func=AF.Square,
                                                                               accum_out=ssq[:, j:j + 1]),
                 reads=[n_xt], writes=[n_jk, n_ssq])
            xts.append((xt, n_xt))
        rstd_from_ssq(kb, ssq, n_ssq, n, 1.0 / D)
        xns = []
        for j, tt in enumerate(grp):
            (xn, n_xn) = xnr.next()
            xt, n_xt = xts[j]
            P.op("gpsimd" if j % 2 else "vector", lambda e, xn=xn, xt=xt, ssq=ssq, j=j: e.tensor_scalar(
                out=xn[:], in0=xt[:], scalar1=ssq[:, j:j + 1], scalar2=None, op0=ALU.mult),
                reads=[n_xt, n_ssq], writes=[n_xn])
            xns.append((xn, n_xn))
        for kc in range(8):
            (pt, n_pt) = ptr.next()

            def tr(e, pt=pt, kc=kc, xns=xns):
                ins = None
                for j, (xn, _) in enumerate(xns):
                    ins = e.transpose(out=pt[:, j * 128:(j + 1) * 128], in_=xn[:, kc * 128:(kc + 1) * 128],
                                      identity=kb.ident[:])
                return ins
            P.op("tensor", tr, reads=[nm for _, nm in xns] + [kb.n_ident], writes=[n_pt])
            c0 = m0 * 128
            if ev % 2 == 0:
                P.op("scalar", lambda e, pt=pt, kc=kc, c0=c0, n=n, s=s: e.activation(
                    out=HT[:, kc, c0:c0 + n * 128], in_=pt[:, 0:n * 128], func=AF.Identity,
                    scale=M["Afm"][:, sub, s, kc:kc + 1], bias=M["Bfm"][:, sub, s, kc:kc + 1]),
                    reads=[n_pt, M["n_A"], M["n_B"]], writes=[n_HT])
            else:
                P.op("vector", lambda e, pt=pt, kc=kc, c0=c0, n=n, s=s: e.tensor_scalar(
                    out=HT[:, kc, c0:c0 + n * 128], in0=pt[:, 0:n * 128], scalar1=M["Afm"][:, sub, s, kc:kc + 1],
                    scalar2=M["Bfm"][:, sub, s, kc:kc + 1], op0=ALU.mult, op1=ALU.add),
                    reads=[n_pt, M["n_A"], M["n_B"]], writes=[n_HT])
            ev += 1


class PostStage:
    def __init__(self, kb, st, bufs=2):
        self.kb = kb
        self.xr = kb.ring(st, "px", [128, 1024], F32, bufs)
        self.jr = kb.ring(st, "pjunk", [128, 1024], BF16, bufs)
        self.tr = kb.ring(st, "ptmp", [128, 1024], F32, bufs)
        self.orr = kb.ring(st, "pout", [128, 1024], F32, bufs)
        self.sr = kb.ring(st, "pssq", [128, 8], F32, 4)

    def run(self, y_ap_halves, y_names, tt, M, sub, xsrc, xdst, is_out=False):
        kb = self.kb
        P = kb.P
        s = 0 if tt < 16 else 1
        (xt, n_xt) = self.xr.next()
        (jk, n_jk) = self.jr.next()
        (tm, n_tm) = self.tr.next()
        (xo, n_xo) = self.orr.next()
        (ssq, n_ssq) = self.sr.next()
        src, n_src = xsrc(tt)
        dst, n_dst = xdst(tt)
        P.op("sync", lambda e: e.dma_start(out=xt[:], in_=src), reads=[n_src], writes=[n_xt], dma=True)
        for h, yh in enumerate(y_ap_halves):
            P.op("scalar", lambda e, h=h, yh=yh: e.activation(out=jk[:, h * 512:(h + 1) * 512], in_=yh, func=AF.Square,
                                                             accum_out=ssq[:, h:h + 1]),
                 reads=[y_names[h]], writes=[n_jk, n_ssq])
        P.op("vector", lambda e: e.tensor_tensor(out=ssq[:, 2:3], in0=ssq[:, 0:1], in1=ssq[:, 1:2], op=ALU.add),
             reads=[n_ssq], writes=[n_ssq])
        P.op("vector", lambda e: e.tensor_scalar(out=ssq[:, 3:4], in0=ssq[:, 2:3], scalar1=1.0 / D, scalar2=EPS,
                                                 op0=ALU.mult, op1=ALU.add), reads=[n_ssq], writes=[n_ssq])
        P.op("gpsimd", lambda e: e.tensor_tensor(out=ssq[:, 4:5], in0=ssq[:, 3:4], in1=kb.nh[:, 0:1], op=ALU.pow),
             reads=[n_ssq, kb.n_nh], writes=[n_ssq])
        for h, yh in enumerate(y_ap_halves):
            P.op("vector", lambda e, h=h, yh=yh: e.scalar_tensor_tensor(
                out=tm[:, h * 512:(h + 1) * 512], in0=yh, scalar=ssq[:, 4:5],
                in1=M["Gbc"][:, sub, s, h * 512:(h + 1) * 512], op0=ALU.mult, op1=ALU.mult),
                reads=[y_names[h], n_ssq, M["n_G"]], writes=[n_tm])
        P.op("gpsimd", lambda e: e.tensor_tensor(out=xo[:], in0=tm[:], in1=xt[:], op=ALU.add),
             reads=[n_tm, n_xt], writes=[n_xo])
        ev = P.op("sync", lambda e: e.dma_start(out=dst, in_=xo[:]), reads=[n_xo], writes=[n_dst], dma=True)
        if is_out:
            P.out_evs.append(ev)


def stage_ffn(kb, st, T, li, which, tiles, M, sub, xsrc, xdst, is_out=False):
    nc, P = kb.nc, kb.P
    nt = len(tiles)
    ntok = nt * 128
    HT, n_HT = kb.sb(st, "HT", [128, 8, ntok], BF16)
    Y, n_Y = kb.sb(st, "Y", [128, nt, 1024], F32)
    with ExitStack() as st2:
        stage_norm_T(kb, st2, xsrc, tiles, M, sub, HT, n_HT)
        P.barrier()
        P.flush()
    with ExitStack() as st3:
        FG = 256
        ngrp = DFF // FG
        wgr = kb.ring(st3, "wg", [128, 8, FG], BF16, 2)
        wur = kb.ring(st3, "wu", [128, 8, FG], BF16, 2)
        wdr = kb.ring(st3, "wd", [128, FG // 128, 1024], BF16, 2)
        actr = kb.ring(st3, "actT", [128, FG // 128, ntok], BF16, 2)
        sgr = kb.ring(st3, "sg", [128, 512], F32, 3)
        pgr = kb.ring(st3, "pg", [128, 512], F32, 2, psum=True)
        pur = kb.ring(st3, "pu", [128, 512], F32, 2, psum=True)
        pdr = kb.ring(st3, "pd", [128, 512], F32, 3, psum=True)
        wgd = T["ffn_w_gate"][li, which].rearrange("(kc p) n -> p kc n", p=128)
        wud = T["ffn_w_up"][li, which].rearrange("(kc p) n -> p kc n", p=128)
        wdd = T["ffn_w_down"][li, which].rearrange("(fc p) n -> p fc n", p=128)
        macros = tok_macros(0, ntok)
        for g in range(ngrp):
            f0 = g * FG
            (wg, n_wg) = wgr.next()
            (wu, n_wu) = wur.next()
            (wd, n_wd) = wdr.next()
            (act, n_act) = actr.next()
            P.op("gpsimd", lambda e, wg=wg, f0=f0: e.dma_start(out=wg[:], in_=wgd[:, :, f0:f0 + FG]), writes=[n_wg], dma=True)
            P.op("gpsimd", lambda e, wu=wu, f0=f0: e.dma_start(out=wu[:], in_=wud[:, :, f0:f0 + FG]), writes=[n_wu], dma=True)
            P.op("gpsimd", lambda e, wd=wd, f0=f0: e.dma_start(out=wd[:], in_=wdd[:, f0 // 128:(f0 + FG) // 128, :]),
                 writes=[n_wd], dma=True)
            for fc in range(FG // 128):
                for (t0, n) in macros:
                    (pg, n_pg) = pgr.next()
                    (pu, n_pu) = pur.next()
                    (sg, n_sg) = sgr.next()

                    def mm(e, w=wg, p=pg, fc=fc, t0=t0, n=n):
                        ins = None
                        for kc in range(8):
                            ins = e.matmul(p[:, 0:n], lhsT=w[:, kc, fc * 128:(fc + 1) * 128], rhs=HT[:, kc, t0:t0 + n],
                                           start=(kc == 0), stop=(kc == 7))
                        return ins
                    P.op("tensor", mm, reads=[n_wg, n_HT], writes=[n_pg])

                    def mm2(e, w=wu, p=pu, fc=fc, t0=t0, n=n):
                        ins = None
                        for kc in range(8):
                            ins = e.matmul(p[:, 0:n], lhsT=w[:, kc, fc * 128:(fc + 1) * 128], rhs=HT[:, kc, t0:t0 + n],
                                           start=(kc == 0), stop=(kc == 7))
                        return ins
                    P.op("tensor", mm2, reads=[n_wu, n_HT], writes=[n_pu])
                    P.op("scalar", lambda e, sg=sg, pg=pg, n=n: e.activation(out=sg[:, 0:n], in_=pg[:, 0:n], func=AF.Silu),
                         reads=[n_pg], writes=[n_sg])
                    P.op("vector", lambda e, act=act, sg=sg, pu=pu, fc=fc, t0=t0, n=n: e.tensor_tensor(
                        out=act[:, fc, t0:t0 + n], in0=sg[:, 0:n], in1=pu[:, 0:n], op=ALU.mult),
                        reads=[n_sg, n_pu], writes=[n_act + "_%d" % (t0 // 512)])
            for j in range(nt):
                for h in range(2):
                    (pd, n_pd) = pdr.next()

                    def mmd(e, pd=pd, act=act, wd=wd, j=j, h=h):
                        ins = None
                        nfc = FG // 128
                        for fc in range(nfc):
                            ins = e.matmul(pd[:], lhsT=act[:, fc, j * 128:(j + 1) * 128], rhs=wd[:, fc, h * 512:(h + 1) * 512],
                                           start=(fc == 0), stop=(fc == nfc - 1))
                        return ins
                    P.op("tensor", mmd, reads=[n_act + "_%d" % (j // 4), n_wd], writes=[n_pd])
                    yname = n_Y + "_%d_%d" % (j, h)
                    if g == 0:
                        P.op("scalar", lambda e, pd=pd, j=j, h=h: e.activation(out=Y[:, j, h * 512:(h + 1) * 512], in_=pd[:],
                                                                                func=AF.Identity),
                             reads=[n_pd], writes=[yname])
                    else:
                        P.op("vector", lambda e, pd=pd, j=j, h=h: e.tensor_tensor(
                            out=Y[:, j, h * 512:(h + 1) * 512], in0=Y[:, j, h * 512:(h + 1) * 512], in1=pd[:], op=ALU.add),
                            reads=[n_pd, yname], writes=[yname])
        P.barrier()
        P.flush()
    post = PostStage(kb, st)
    for j, tt in enumerate(tiles):
        post.run([Y[:, j, 0:512], Y[:, j, 512:1024]], [n_Y + "_%d_0" % j, n_Y + "_%d_1" % j], tt, M, sub, xsrc, xdst,
                 is_out=is_out)
    P.barrier()
    P.flush()


class AttnCore:
    def __init__(self, kb, st, n_ps=3, n_pt=3):
        self.kb = kb
        self.psr = kb.ring(st, "pS", [128, 512], F32, n_ps, psum=True)
        self.ptr = kb.ring(st, "PT", [128, 512], BF16, n_pt)

    def bank(self, blocks, exp_scale):
        kb = self.kb
        P = kb.P
        (pS, n_pS) = self.psr.next()
        (PT, n_PT) = self.ptr.next()
        nb = len(blocks)
        assert 1 <= nb <= 4

        def mm(e):
            ins = None
            for i, b in enumerate(blocks):
                o = pS[:, i * 128:(i + 1) * 128]
                if b.get("bias") is not None:
                    e.matmul(o, lhsT=kb.ident[:], rhs=b["bias"], start=True, stop=False)
                    ins = e.matmul(o, lhsT=b["kT"], rhs=b["qT"], start=False, stop=True)
                else:
                    ins = e.matmul(o, lhsT=b["kT"], rhs=b["qT"], start=True, stop=True)
            return ins
        rd = [kb.n_ident]
        for b in blocks:
            rd += list(b["rd"])
        P.op("tensor", mm, reads=rd, writes=[n_pS])
        P.op("scalar", lambda e: e.activation(out=PT[:, 0:nb * 128], in_=pS[:, 0:nb * 128], func=AF.Exp, scale=exp_scale),
             reads=[n_pS], writes=[n_PT])

        def pv(e):
            ins = None
            for i, b in enumerate(blocks):
                ins = e.matmul(b["po"], lhsT=PT[:, i * 128:(i + 1) * 128], rhs=b["v"], start=b["start"], stop=b["stop"])
            return ins
        rd = [n_PT]
        wr = []
        for b in blocks:
            rd += list(b["rdv"])
            wr.append(b["n_po"])
        P.op("tensor", pv, reads=rd, writes=wr)


class OutProj:
    def __init__(self, kb, st, T, wo_dram, M, xsrc, xdst, post_bufs=1):
        self.kb = kb
        P = kb.P
        self.M = M
        self.xsrc, self.xdst = xsrc, xdst
        self.wo, self.n_wo = kb.sb(st, "wo", [128, 8, 1024], BF16)
        P.op("gpsimd", lambda e: e.dma_start(out=self.wo[:], in_=wo_dram.rearrange("(kc p) n -> p kc n", p=128)),
             writes=[self.n_wo], dma=True)
        self.otr = kb.ring(st, "OT", [128, 8, 128], BF16, 2)
        self.ptr = kb.ring(st, "pOT", [128, 1024], BF16, 1, psum=True)
        self.pyr = kb.ring(st, "pY", [128, 512], F32, 2, psum=True)
        self.post = PostStage(kb, st, bufs=post_bufs)

    def run(self, ocat_ap, n_ocat, tt, is_out=False):
        kb = self.kb
        P = kb.P
        (OT, n_OT) = self.otr.next()
        (pt, n_pt) = self.ptr.next()

        def tr(e):
            ins = None
            for c in range(8):
                ins = e.transpose(out=pt[:, c * 128:(c + 1) * 128], in_=ocat_ap[:, c * 128:(c + 1) * 128], identity=kb.ident[:])
            return ins
        P.op("tensor", tr, reads=[n_ocat, kb.n_ident], writes=[n_pt])
        P.op("vector", lambda e: e.tensor_copy(out=OT[:].rearrange("p c t -> p (c t)"), in_=pt[:]), reads=[n_pt], writes=[n_OT])
        halves = []
        names = []
        for h in range(2):
            (py, n_py) = self.pyr.next()

            def mm(e, py=py, h=h):
                ins = None
                for c in range(8):
                    ins = e.matmul(py[:], lhsT=OT[:, c, :], rhs=self.wo[:, c, h * 512:(h + 1) * 512], start=(c == 0), stop=(c == 7))
                return ins
            P.op("tensor", mm, reads=[n_OT, self.n_wo], writes=[n_py])
            halves.append(py[:])
            names.append(n_py)
        self.post.run(halves, names, tt, self.M, 1, self.xsrc, self.xdst, is_out=is_out)


def load_w_bf16(kb, st, name, dram_ap_pkn, ncols, piece=512):
    P = kb.P
    w, n_w = kb.sb(st, name, [128, 8, ncols], BF16)
    for c0 in range(0, ncols, piece):
        n = min(piece, ncols - c0)
        P.op("gpsimd", lambda e, c0=c0, n=n: e.dma_start(out=w[:, :, c0:c0 + n], in_=dram_ap_pkn[:, :, c0:c0 + n]),
             writes=[n_w + "_%d" % (c0 // piece)], dma=True)
    return w, n_w


def stage_mixer_d(kb, st, T, M, xsrc, xdst):
    nc, P = kb.nc, kb.P
    H, HKV, DH = 8, 2, 128
    QT, n_QT = kb.sb(st, "QT", [128, H, S], BF16)
    KT, n_KT = kb.sb(st, "KT", [128, HKV, NTOK], BF16)
    VA, n_VA = kb.sb(st, "VA", [128, 18, HKV, DH + 2], BF16)
    P.op("gpsimd", lambda e: e.memset(VA[:, :, :, DH:DH + 1], 1.0), writes=[n_VA])
    with ExitStack() as st2:
        HT, n_HT = kb.sb(st2, "HT", [128, 8, NTOK], BF16)
        with ExitStack() as st3:
            stage_norm_T(kb, st3, xsrc, list(range(18)), M, 1, HT, n_HT)
            P.barrier()
            P.flush()
        wq, n_wq = load_w_bf16(kb, st2, "wqkv", T["ga_w_qkv_p"].rearrange("(kc p) n -> p kc n", p=128), 1536)
        gain, n_gain = kb.sb(st2, "gain", [128, 10, 128], F32)
        cos, n_cos = kb.sb(st2, "cos", [128, 16, 64], F32)
        sin, n_sin = kb.sb(st2, "sin", [128, 16, 64], F32)
        P.op("sync", lambda e: e.dma_start(out=gain[:].rearrange("p a b -> p (a b)"),
                                           in_=T["ga_gain"][:].partition_broadcast(128)), writes=[n_gain], dma=True)
        P.op("sync", lambda e: e.dma_start(out=cos[:], in_=T["ga_cos"].rearrange("(t p) d -> p t d", p=128)), writes=[n_cos], dma=True)
        P.op("sync", lambda e: e.dma_start(out=sin[:], in_=T["ga_sin"].rearrange("(t p) d -> p t d", p=128)), writes=[n_sin], dma=True)
        ppr = kb.ring(st2, "pproj", [128, 512], F32, 3, psum=True)
        ptq = kb.ring(st2, "ptq", [128, 1024], BF16, 2, psum=True)
        qfr = kb.ring(st2, "qf", [128, 10, 128], F32, 2)
        sqr = kb.ring(st2, "qsq", [128, 10, 128], F32, 1)
        ssr = kb.ring(st2, "qss", [128, 16], F32, 2)
        rar = kb.ring(st2, "ra", [128, 10, 64], F32, 2)
        rbr = kb.ring(st2, "rb", [128, 10, 64], F32, 2)
        qrr = kb.ring(st2, "qr", [128, 10, 128], BF16, 2)
        import os
        for tt in range(18 if int(os.environ.get("MIX_CUT", "99")) >= 0 else 0):
            lat = tt < 16
            (qf, n_qf) = qfr.next()
            (sq, n_sq) = sqr.next()
            (ss, n_ss) = ssr.next()
            (qr, n_qr) = qrr.next()
            pieces = ([(0, 0), (512, 4)] if lat else []) + [(1024, 8)]
            for (c0, h0) in pieces:
                (pp, n_pp) = ppr.next()

                def mm(e, pp=pp, c0=c0, tt=tt):
                    ins = None
                    for kc in range(8):
                        ins = e.matmul(pp[:], lhsT=HT[:, kc, tt * 128:(tt + 1) * 128], rhs=wq[:, kc, c0:c0 + 512],
                                       start=(kc == 0), stop=(kc == 7))
                    return ins
                P.op("tensor", mm, reads=[n_HT, n_wq + "_%d" % (c0 // 512)], writes=[n_pp])
                if c0 < 1024:
                    P.op("scalar", lambda e, pp=pp, qf=qf, h0=h0: e.activation(
                        out=qf[:, h0:h0 + 4, :].rearrange("p a b -> p (a b)"), in_=pp[:], func=AF.Identity),
                        reads=[n_pp], writes=[n_qf])
                else:
                    P.op("scalar", lambda e, pp=pp, qf=qf: e.activation(
                        out=qf[:, 8:10, :].rearrange("p a b -> p (a b)"), in_=pp[:, 0:256], func=AF.Identity),
                        reads=[n_pp], writes=[n_qf])
                    P.op("vector", lambda e, pp=pp, tt=tt: e.tensor_copy(
                        out=VA[:, tt, :, 0:DH], in_=pp[:, 256:512].rearrange("p (a b) -> p a b", b=DH)),
                        reads=[n_pp], writes=[n_VA])
            h_lo = 0 if lat else 8
            nh_ = 10 - h_lo
            import os
            CUT = int(os.environ.get("MIX_CUT", "99"))
            if CUT < 1:
                continue
            P.op("vector", lambda e, qf=qf, sq=sq, h_lo=h_lo: e.tensor_tensor(out=sq[:, h_lo:10, :], in0=qf[:, h_lo:10, :],
                                                                            in1=qf[:, h_lo:10, :], op=ALU.mult),
                 reads=[n_qf], writes=[n_sq])
            P.op("vector", lambda e, sq=sq, ss=ss, h_lo=h_lo: e.tensor_reduce(out=ss[:, h_lo:10], in_=sq[:, h_lo:10, :],
                                                                            axis=AX.X, op=ALU.add),
                 reads=[n_sq], writes=[n_ss])
            P.op("vector", lambda e, ss=ss: e.tensor_scalar(out=ss[:, 0:10], in0=ss[:, 0:10], scalar1=1.0 / DH, scalar2=EPS,
                                                            op0=ALU.mult, op1=ALU.add), reads=[n_ss], writes=[n_ss])
            P.op("gpsimd", lambda e, ss=ss: e.tensor_tensor(out=ss[:, 0:10], in0=ss[:, 0:10], in1=kb.nh2[:, 0:10], op=ALU.pow),
                 reads=[n_ss, kb.n_nh2], writes=[n_ss])
            P.op("vector", lambda e, qf=qf, ss=ss, h_lo=h_lo, nh_=nh_: e.tensor_tensor(
                out=qf[:, h_lo:10, :], in0=qf[:, h_lo:10, :], in1=ss[:, h_lo:10].unsqueeze(2).to_broadcast([128, nh_, 128]),
                op=ALU.mult), reads=[n_qf, n_ss], writes=[n_qf])
            if CUT < 2:
                continue
            if lat:
                P.op("gpsimd", lambda e, qf=qf: e.tensor_tensor(out=qf[:], in0=qf[:], in1=gain[:], op=ALU.mult),
                     reads=[n_qf, n_gain], writes=[n_qf])
                (ra, n_ra) = rar.next()
                (rb, n_rb) = rbr.next()
                cb = cos[:, tt, :].unsqueeze(1).to_broadcast([128, 10, 64])
                sb_ = sin[:, tt, :].unsqueeze(1).to_broadcast([128, 10, 64])
                x1 = qf[:, :, 0:64]
                x2 = qf[:, :, 64:128]
                P.op("vector", lambda e, ra=ra, x1=x1, cb=cb: e.tensor_tensor(out=ra[:], in0=x1, in1=cb, op=ALU.mult),
                     reads=[n_qf, n_cos], writes=[n_ra])
                P.op("gpsimd", lambda e, rb=rb, x2=x2, sb_=sb_: e.tensor_tensor(out=rb[:], in0=x2, in1=sb_, op=ALU.mult),
                     reads=[n_qf, n_sin], writes=[n_rb])
                P.op("vector", lambda e, qr=qr, ra=ra, rb=rb: e.tensor_tensor(out=qr[:, :, 0:64], in0=ra[:], in1=rb[:], op=ALU.subtract),
                     reads=[n_ra, n_rb], writes=[n_qr])
                (ra2, n_ra2) = rar.next()
                (rb2, n_rb2) = rbr.next()
                P.op("vector", lambda e, ra2=ra2, x1=x1, sb_=sb_: e.tensor_tensor(out=ra2[:], in0=x1, in1=sb_, op=ALU.mult),
                     reads=[n_qf, n_sin], writes=[n_ra2])
                P.op("gpsimd", lambda e, rb2=rb2, x2=x2, cb=cb: e.tensor_tensor(out=rb2[:], in0=x2, in1=cb, op=ALU.mult),
                     reads=[n_qf, n_cos], writes=[n_rb2])
                P.op("vector", lambda e, qr=qr, ra2=ra2, rb2=rb2: e.tensor_tensor(out=qr[:, :, 64:128], in0=ra2[:], in1=rb2[:], op=ALU.add),
                     reads=[n_ra2, n_rb2], writes=[n_qr])
            else:
                P.op("gpsimd", lambda e, qf=qf, qr=qr: e.tensor_tensor(out=qr[:, 8:10, :], in0=qf[:, 8:10, :], in1=gain[:, 8:10, :],
                                                                     op=ALU.mult), reads=[n_qf, n_gain], writes=[n_qr])
            if CUT < 3:
                continue
            groups = ([(0, 8, "q")] if lat else []) + [(8, 2, "k")]
            for (h0, n, kind) in groups:
                (pt, n_pt) = ptq.next()

                def tr(e, pt=pt, qr=qr, h0=h0, n=n):
                    ins = None
                    for j in range(n):
                        ins = e.transpose(out=pt[:, j * 128:(j + 1) * 128], in_=qr[:, h0 + j, :], identity=kb.ident[:])
                    return ins
                P.op("tensor", tr, reads=[n_qr, kb.n_ident], writes=[n_pt])
                if kind == "q":
                    P.op("scalar", lambda e, pt=pt, tt=tt: e.activation(
                        out=QT[:, :, tt * 128:(tt + 1) * 128], in_=pt[:].rearrange("p (a b) -> p a b", b=128), func=AF.Identity),
                        reads=[n_pt], writes=[n_QT])
                else:
                    P.op("vector", lambda e, pt=pt, tt=tt: e.tensor_copy(
                        out=KT[:, :, tt * 128:(tt + 1) * 128], in_=pt[:, 0:256].rearrange("p (a b) -> p a b", b=128)),
                        reads=[n_pt], writes=[n_KT])
        P.barrier()
        P.flush()
    import os
    if os.environ.get("MIX_STOP") == "1":
        return
    with ExitStack() as st2:
        core = AttnCore(kb, st2)
        op_ = OutProj(kb, st2, T, T["ga_w_o"][0], M, xsrc, xdst)
        ocr = kb.ring(st2, "Ocat", [128, 4, 1024], BF16, 2)
        por = kb.ring(st2, "pO", [128, 512], F32, 2, psum=True)
        rcr = kb.ring(st2, "rc", [128, 4], F32, 4)
        scale = DH ** -0.5
        for mq in range(4):
            (oc, n_oc) = ocr.next()
            for j in range(4):
                q0 = (mq * 4 + j) * 128
                for h in range(H):
                    g = h // (H // HKV)
                    (po, n_po) = por.next()
                    for k0 in range(0, 18, 4):
                        blocks = []
                        for kt in range(k0, min(k0 + 4, 18)):
                            blocks.append(dict(kT=KT[:, g, kt * 128:(kt + 1) * 128], qT=QT[:, h, q0:q0 + 128], rd=[n_KT, n_QT],
                                               v=VA[:, kt, g, 0:DH + 1], rdv=[n_VA], po=po[:, 0:DH + 1], n_po=n_po,
                                               start=(kt == 0), stop=(kt == 17)))
                        core.bank(blocks, scale)
                    (rc, n_rc) = rcr.next()
                    P.op("vector", lambda e, rc=rc, po=po: e.reciprocal(out=rc[:, 0:1], in_=po[:, DH:DH + 1]),
                         reads=[n_po], writes=[n_rc])
                    P.op("vector", lambda e, oc=oc, po=po, rc=rc, h=h, j=j: e.tensor_scalar(
                        out=oc[:, j, h * DH:(h + 1) * DH], in0=po[:, 0:DH], scalar1=rc[:, 0:1], scalar2=None, op0=ALU.mult),
                        reads=[n_po, n_rc], writes=[n_oc + "_%d" % j])
            for j in range(4):
                op_.run(oc[:, j, :], n_oc + "_%d" % j, mq * 4 + j)
        P.barrier()
        P.flush()


def normalize_head(kb, po, n_po, dv, oc_ap, n_oc, rcr, extra=None):
    P = kb.P
    (rc, n_rc) = rcr.next()
    if extra is not None:
        ex_ap, n_ex = extra
        P.op("vector", lambda e: e.tensor_tensor(out=rc[:, 1:2], in0=po[:, dv:dv + 1], in1=ex_ap, op=ALU.add),
             reads=[n_po, n_ex], writes=[n_rc])
        P.op("vector", lambda e: e.reciprocal(out=rc[:, 0:1], in_=rc[:, 1:2]), reads=[n_rc], writes=[n_rc])
    else:
        P.op("vector", lambda e: e.reciprocal(out=rc[:, 0:1], in_=po[:, dv:dv + 1]), reads=[n_po], writes=[n_rc])
    P.op("vector", lambda e: e.tensor_scalar(out=oc_ap, in0=po[:, 0:dv], scalar1=rc[:, 0:1], scalar2=None, op0=ALU.mult),
         reads=[n_po, n_rc], writes=[n_oc])


def stage_mixer_b(kb, st, T, M, xsrc, xdst):
    nc, P = kb.nc, kb.P
    H, HKV, DH = 16, 2, 64
    QT, n_QT = kb.sb(st, "QT2", [128, 8, NTOK], BF16)
    KT, n_KT = kb.sb(st, "KT2", [128, HKV, NTOK], BF16)
    VA, n_VA = kb.sb(st, "VA", [128, 18, HKV, DH + 2], BF16)
    P.op("gpsimd", lambda e: e.memset(VA[:, :, :, DH:DH + 1], 1.0), writes=[n_VA])
    with ExitStack() as st2:
        HT, n_HT = kb.sb(st2, "HT", [128, 8, NTOK], BF16)
        with ExitStack() as st3:
            stage_norm_T(kb, st3, xsrc, list(range(18)), M, 1, HT, n_HT)
            P.barrier()
            P.flush()
        wq, n_wq = load_w_bf16(kb, st2, "wqkv", T["sw_w_qkv_p"].rearrange("(kc p) n -> p kc n", p=128), 1280, piece=256)
        cos, n_cos = kb.sb(st2, "cos", [128, 16, 32], F32)
        sin, n_sin = kb.sb(st2, "sin", [128, 16, 32], F32)
        P.op("sync", lambda e: e.dma_start(out=cos[:], in_=T["sw_cos"].rearrange("(t p) d -> p t d", p=128)), writes=[n_cos], dma=True)
        P.op("sync", lambda e: e.dma_start(out=sin[:], in_=T["sw_sin"].rearrange("(t p) d -> p t d", p=128)), writes=[n_sin], dma=True)
        ppr = kb.ring(st2, "pproj", [128, 512], F32, 3, psum=True)
        ptq = kb.ring(st2, "ptq", [128, 1024], BF16, 2, psum=True)
        qfr = kb.ring(st2, "qf", [128, 18, 64], F32, 2)
        rar = kb.ring(st2, "ra", [128, 18, 32], F32, 2)
        rbr = kb.ring(st2, "rb", [128, 18, 32], F32, 2)
        qrr = kb.ring(st2, "qr", [128, 18, 64], BF16, 2)
        kdr = kb.ring(st2, "kd", [128, 2, 2, 64], BF16, 2)
        for tt in range(18):
            lat = tt < 16
            (qf, n_qf) = qfr.next()
            (qr, n_qr) = qrr.next()
            (kd, n_kd) = kdr.next()
            for (c0, ncol, h0) in [(0, 512, 0), (512, 512, 8), (1024, 256, 16)]:
                (pp, n_pp) = ppr.next()

                def mm(e, pp=pp, c0=c0, ncol=ncol, tt=tt):
                    ins = None
                    for kc in range(8):
                        ins = e.matmul(pp[:, 0:ncol], lhsT=HT[:, kc, tt * 128:(tt + 1) * 128], rhs=wq[:, kc, c0:c0 + ncol],
                                       start=(kc == 0), stop=(kc == 7))
                    return ins
                P.op("tensor", mm, reads=[n_HT] + [n_wq + "_%d" % i for i in range(c0 // 256, (c0 + ncol) // 256)], writes=[n_pp])
                if c0 < 1024:
                    P.op("scalar", lambda e, pp=pp, qf=qf, h0=h0: e.activation(
                        out=qf[:, h0:h0 + 8, :].rearrange("p a b -> p (a b)"), in_=pp[:], func=AF.Identity),
                        reads=[n_pp], writes=[n_qf])
                else:
                    P.op("scalar", lambda e, pp=pp, qf=qf: e.activation(
                        out=qf[:, 16:18, :].rearrange("p a b -> p (a b)"), in_=pp[:, 0:128], func=AF.Identity),
                        reads=[n_pp], writes=[n_qf])
                    P.op("scalar", lambda e, pp=pp, tt=tt: e.activation(
                        out=VA[:, tt, :, 0:DH], in_=pp[:, 128:256].rearrange("p (a b) -> p a b", b=DH), func=AF.Identity),
                        reads=[n_pp], writes=[n_VA])
            if lat:
                (ra, n_ra) = rar.next()
                (rb, n_rb) = rbr.next()
                cb = cos[:, tt, :].unsqueeze(1).to_broadcast([128, 18, 32])
                sb_ = sin[:, tt, :].unsqueeze(1).to_broadcast([128, 18, 32])
                x1 = qf[:, :, 0:32]
                x2 = qf[:, :, 32:64]
                P.op("vector", lambda e, ra=ra, x1=x1, cb=cb: e.tensor_tensor(out=ra[:], in0=x1, in1=cb, op=ALU.mult),
                     reads=[n_qf, n_cos], writes=[n_ra])
                P.op("gpsimd", lambda e, rb=rb, x2=x2, sb_=sb_: e.tensor_tensor(out=rb[:], in0=x2, in1=sb_, op=ALU.mult),
                     reads=[n_qf, n_sin], writes=[n_rb])
                P.op("vector", lambda e, qr=qr, ra=ra, rb=rb: e.tensor_tensor(out=qr[:, :, 0:32], in0=ra[:], in1=rb[:], op=ALU.subtract),
                     reads=[n_ra, n_rb], writes=[n_qr])
                (ra2, n_ra2) = rar.next()
                (rb2, n_rb2) = rbr.next()
                P.op("vector", lambda e, ra2=ra2, x1=x1, sb_=sb_: e.tensor_tensor(out=ra2[:], in0=x1, in1=sb_, op=ALU.mult),
                     reads=[n_qf, n_sin], writes=[n_ra2])
                P.op("gpsimd", lambda e, rb2=rb2, x2=x2, cb=cb: e.tensor_tensor(out=rb2[:], in0=x2, in1=cb, op=ALU.mult),
                     reads=[n_qf, n_cos], writes=[n_rb2])
                P.op("vector", lambda e, qr=qr, ra2=ra2, rb2=rb2: e.tensor_tensor(out=qr[:, :, 32:64], in0=ra2[:], in1=rb2[:], op=ALU.add),
                     reads=[n_ra2, n_rb2], writes=[n_qr])
            else:
                P.op("vector", lambda e, qr=qr, qf=qf: e.tensor_copy(out=qr[:], in_=qf[:]), reads=[n_qf], writes=[n_qr])
            for dup in range(2):
                P.op("gpsimd", lambda e, kd=kd, qr=qr, dup=dup: e.tensor_copy(out=kd[:, :, dup, :], in_=qr[:, 16:18, :]),
                     reads=[n_qr], writes=[n_kd])
            (pt, n_pt) = ptq.next()

            def tr(e, pt=pt, qr=qr):
                ins = None
                for p_ in range(8):
                    ins = e.transpose(out=pt[:, p_ * 128:(p_ + 1) * 128],
                                      in_=qr[:, 2 * p_:2 * p_ + 2, :].rearrange("p a b -> p (a b)"), identity=kb.ident[:])
                return ins
            P.op("tensor", tr, reads=[n_qr, kb.n_ident], writes=[n_pt])
            P.op("scalar", lambda e, pt=pt, tt=tt: e.activation(
                out=QT[:, :, tt * 128:(tt + 1) * 128], in_=pt[:].rearrange("p (a b) -> p a b", b=128), func=AF.Identity),
                reads=[n_pt], writes=[n_QT])
            (pt2, n_pt2) = ptq.next()

            def tr2(e, pt2=pt2, kd=kd):
                ins = None
                for g in range(2):
                    ins = e.transpose(out=pt2[:, g * 128:(g + 1) * 128], in_=kd[:, g, :, :].rearrange("p a b -> p (a b)"),
                                      identity=kb.ident[:])
                return ins
            P.op("tensor", tr2, reads=[n_kd, kb.n_ident], writes=[n_pt2])
            P.op("vector", lambda e, pt2=pt2, tt=tt: e.tensor_copy(
                out=KT[:, :, tt * 128:(tt + 1) * 128], in_=pt2[:, 0:256].rearrange("p (a b) -> p a b", b=128)),
                reads=[n_pt2], writes=[n_KT])
        P.barrier()
        P.flush()
    with ExitStack() as st2:
        core = AttnCore(kb, st2)
        op_ = OutProj(kb, st2, T, T["sw_w_o"][0], M, xsrc, xdst)
        ocr = kb.ring(st2, "Ocat", [128, 1024], BF16, 2)
        por = kb.ring(st2, "pO", [128, 512], F32, 2, psum=True)
        rcr = kb.ring(st2, "rc", [128, 4], F32, 4)
        esk, n_esk = kb.sb(st2, "esink", [128, 16], F32)
        P.op("sync", lambda e: e.dma_start(out=esk[:], in_=T["sw_sink"][0, :].partition_broadcast(128)), writes=[n_esk], dma=True)
        P.op("scalar", lambda e: e.activation(out=esk[:], in_=esk[:], func=AF.Exp), reads=[n_esk], writes=[n_esk])
        scale = DH ** -0.5
        for tt in range(18):
            (oc, n_oc) = ocr.next()
            if tt < 16:
                kl = []
                if tt - 1 >= 0:
                    kl.append((tt - 1, kb.masklo[:], kb.n_masklo))
                kl.append((tt, None, None))
                if tt + 1 <= 15:
                    kl.append((tt + 1, kb.maskhi[:], kb.n_maskhi))
                kl += [(16, None, None), (17, None, None)]
            else:
                kl = [(16, None, None), (17, None, None)]
            for h in range(H):
                g = h // (H // HKV)
                b0 = (h % 2) * 64
                pr = h // 2
                (po, n_po) = por.next()
                for k0 in range(0, len(kl), 4):
                    blocks = []
                    for idx in range(k0, min(k0 + 4, len(kl))):
                        kt, bias, n_bias = kl[idx]
                        blocks.append(dict(kT=KT[b0:b0 + 64, g, kt * 128:(kt + 1) * 128], qT=QT[b0:b0 + 64, pr, tt * 128:(tt + 1) * 128],
                                           bias=bias, rd=[n_KT, n_QT] + ([n_bias] if n_bias else []),
                                           v=VA[:, kt, g, 0:DH + 1], rdv=[n_VA], po=po[:, 0:DH + 1], n_po=n_po,
                                           start=(idx == 0), stop=(idx == len(kl) - 1)))
                    core.bank(blocks, scale)
                normalize_head(kb, po, n_po, DH, oc[:, h * DH:(h + 1) * DH], n_oc, rcr, extra=(esk[:, h:h + 1], n_esk))
            op_.run(oc[:], n_oc, tt)
        P.barrier()
        P.flush()


NEG_BIAS = -30000.0


def na_structure():
    rows = S // GRID_W
    p = np.arange(128)
    combos = []
    keyl = []
    sig = {}
    for m in range(16):
        r = 2 * m + p // 64
        j = p % 64
        rs = np.clip(r - 4, 0, rows - 8)
        ws = np.clip(j - 8, 0, GRID_W - 16)
        lst = []
        for kt in range(16):
            kr = 2 * kt + p // 64
            kc = p % 64
            valid = ((kr[:, None] >= rs[None, :]) & (kr[:, None] < rs[None, :] + 8)
                     & (kc[:, None] >= ws[None, :]) & (kc[:, None] < ws[None, :] + 16))
            if not valid.any():
                continue
            ridx = np.clip(kr[:, None] - r[None, :] + 7, 0, 14)
            cidx = np.clip(kc[:, None] - j[None, :] + 15, 0, 30)
            ridx = np.where(valid, ridx, 0)
            cidx = np.where(valid, cidx, 0)
            key = (ridx.tobytes(), cidx.tobytes(), valid.tobytes())
            if key not in sig:
                sig[key] = len(combos)
                combos.append((ridx, cidx, valid))
            lst.append((kt, sig[key]))
        keyl.append(lst)
    return keyl, combos


def na_bias_host(rpb):
    keyl, combos = na_structure()
    rpb = np.asarray(rpb, dtype=np.float32)[0]
    out = np.empty((16, 128, len(combos), 128), dtype=np.float32)
    for ci, (ridx, cidx, valid) in enumerate(combos):
        g = rpb[:, ridx, cidx]
        out[:, :, ci, :] = np.where(valid[None], g, np.float32(NEG_BIAS))
    return out


def stage_mixer_a(kb, st, T, M, xsrc, xdst):
    nc, P = kb.nc, kb.P
    H, DH = 16, 64
    keyl, combos = na_structure()
    NCMB = len(combos)
    BIG, n_BIG = kb.sb(st, "HT_OC", [128, 8 * NTOK], BF16)
    HT, n_HT = BIG[:].rearrange("p (c t) -> p c t", c=8), n_BIG + "_ht"
    OC, n_OC = BIG[:].rearrange("p (t d) -> p t d", t=18), n_BIG + "_oc"
    stA = ExitStack()
    QT, n_QT = kb.sb(stA, "QT2", [128, 8, NTOK], BF16)
    KT, n_KT = kb.sb(stA, "KT2", [128, 8, NTOK], BF16)
    VA, n_VA = kb.sb(stA, "VA", [128, 18, H, DH + 2], BF16)
    P.op("gpsimd", lambda e: e.memset(VA[:, :, :, DH:DH + 1], 1.0), writes=[n_VA])
    scale = DH ** -0.5
    with ExitStack() as st2:
        with ExitStack() as st3:
            stage_norm_T(kb, st3, xsrc, list(range(18)), M, 1, HT, n_HT, rings=(4, 1, 4))
            P.barrier()
            P.flush()
        wr = kb.ring(st2, "wqkv", [128, 8, 512], BF16, 2)
        ppr = kb.ring(st2, "pproj", [128, 512], F32, 4, psum=True)
        wd = T["na_w_qkv"][0].rearrange("(kc p) n -> p kc n", p=128)
        macros = tok_macros(0, NTOK)
        ev = 0
        for piece in range(6):
            (w, n_w) = wr.next()
            P.op("gpsimd", lambda e, w=w, piece=piece: e.dma_start(out=w[:], in_=wd[:, :, piece * 512:(piece + 1) * 512]),
                 writes=[n_w], dma=True)
            if piece < 4:
                dst, n_dst = (QT, n_QT) if piece < 2 else (KT, n_KT)
                for pl in range(4):
                    pr = (piece % 2) * 4 + pl
                    for (t0, n) in macros:
                        (pp, n_pp) = ppr.next()

                        def mm(e, pp=pp, w=w, pl=pl, t0=t0, n=n):
                            ins = None
                            for kc in range(8):
                                ins = e.matmul(pp[:, 0:n], lhsT=w[:, kc, pl * 128:(pl + 1) * 128], rhs=HT[:, kc, t0:t0 + n],
                                               start=(kc == 0), stop=(kc == 7))
                            return ins
                        P.op("tensor", mm, reads=[n_w, n_HT], writes=[n_pp])
                        if piece < 2:
                            P.op("scalar", lambda e, pp=pp, pr=pr, t0=t0, n=n: e.activation(
                                out=QT[:, pr, t0:t0 + n], in_=pp[:, 0:n], func=AF.Identity, scale=scale),
                                reads=[n_pp], writes=[n_QT])
                        else:
                            P.op("vector", lambda e, pp=pp, pr=pr, t0=t0, n=n: e.tensor_copy(out=KT[:, pr, t0:t0 + n], in_=pp[:, 0:n]),
                                 reads=[n_pp], writes=[n_KT])
            else:
                h0 = (piece - 4) * 8
                for tt in range(18):
                    (pp, n_pp) = ppr.next()

                    def mm(e, pp=pp, w=w, tt=tt):
                        ins = None
                        for kc in range(8):
                            ins = e.matmul(pp[:], lhsT=HT[:, kc, tt * 128:(tt + 1) * 128], rhs=w[:, kc, :], start=(kc == 0), stop=(kc == 7))
                        return ins
                    P.op("tensor", mm, reads=[n_w, n_HT], writes=[n_pp])
                    if ev % 2 == 0:
                        P.op("scalar", lambda e, pp=pp, tt=tt, h0=h0: e.activation(
                            out=VA[:, tt, h0:h0 + 8, 0:DH], in_=pp[:].rearrange("p (a b) -> p a b", b=DH), func=AF.Identity),
                            reads=[n_pp], writes=[n_VA])
                    else:
                        P.op("vector", lambda e, pp=pp, tt=tt, h0=h0: e.tensor_copy(
                            out=VA[:, tt, h0:h0 + 8, 0:DH], in_=pp[:].rearrange("p (a b) -> p a b", b=DH)),
                            reads=[n_pp], writes=[n_VA])
                    ev += 1
        P.barrier()
        P.flush()
    with ExitStack() as st2:
        core = AttnCore(kb, st2, n_ps=4, n_pt=4)
        por = kb.ring(st2, "pO", [128, 512], F32, 2, psum=True)
        rcr = kb.ring(st2, "rc", [128, 4], F32, 4)
        br = kb.ring(st2, "nabias", [128, NCMB, 128], BF16, 2)
        for h in range(H):
            (bt, n_bt) = br.next()
            P.op("gpsimd", lambda e, bt=bt, h=h: e.dma_start(out=bt[:], in_=T["na_bias"][h]), writes=[n_bt], dma=True)
            b0 = (h % 2) * 64
            pr = h // 2
            for tt in range(18):
                if tt < 16:
                    kl = [(kt, bt[:, ci, :], n_bt) for (kt, ci) in keyl[tt]] + [(16, None, None), (17, None, None)]
                else:
                    kl = [(16, None, None), (17, None, None)]
                (po, n_po) = por.next()
                for k0 in range(0, len(kl), 4):
                    blocks = []
                    for idx in range(k0, min(k0 + 4, len(kl))):
                        kt, bias, n_bias = kl[idx]
                        blocks.append(dict(kT=KT[b0:b0 + 64, pr, kt * 128:(kt + 1) * 128], qT=QT[b0:b0 + 64, pr, tt * 128:(tt + 1) * 128],
                                           bias=bias, rd=[n_KT, n_QT] + ([n_bias] if n_bias else []),
                                           v=VA[:, kt, h, 0:DH + 1], rdv=[n_VA], po=po[:, 0:DH + 1], n_po=n_po,
                                           start=(idx == 0), stop=(idx == len(kl) - 1)))
                    core.bank(blocks, 1.0)
                normalize_head(kb, po, n_po, DH, OC[:, tt, h * DH:(h + 1) * DH], n_OC + "_%d" % tt, rcr)
        P.barrier()
        P.flush()
    stA.close()
    with ExitStack() as st2:
        op_ = OutProj(kb, st2, T, T["na_w_o"][0], M, xsrc, xdst, post_bufs=2)
        for tt in range(18):
            op_.run(OC[:, tt, :], n_OC + "_%d" % tt, tt)
        P.barrier()
        P.flush()


def stage_mixer_c(kb, st, T, M, xsrc, xdst):
    nc, P = kb.nc, kb.P
    CW = 31
    HW_ = CW // 2
    HT, n_HT = kb.sb(st, "HT", [128, 8, NTOK], BF16)
    with ExitStack() as st3:
        stage_norm_T(kb, st3, xsrc, list(range(18)), M, 1, HT, n_HT, rings=(4, 1, 4))
        P.barrier()
        P.flush()
    w1, n_w1 = load_w_bf16(kb, st, "w1", T["cv_w_pw1"][0].rearrange("(kc p) n -> p kc n", p=128), 2048)
    w2, n_w2 = load_w_bf16(kb, st, "w2", T["cv_w_pw2"][0].rearrange("(kc p) n -> p kc n", p=128), 1024)
    cvf, n_cvf = kb.sb(st, "cvf", [128, 8, 36], F32)
    b2, n_b2 = kb.sb(st, "b2", [128, 1024], F32)
    onesm, n_ones = kb.sb(st, "onesm", [128, 128], F32)
    U, n_U = kb.sb(st, "U", [128, 8, 512 + 2 * HW_], F32)
    V, n_V = kb.sb(st, "V", [128, 8, 512], F32)
    Z, n_Z = kb.sb(st, "Z", [128, 8, 512], BF16)
    sgr = kb.ring(st, "sig", [128, 512 + 2 * HW_], F32, 2)
    msq, n_msq = kb.sb(st, "msq", [128, 512], F32)
    rstd, n_rstd = kb.sb(st, "rstd", [128, 512], F32)
    ysr = kb.ring(st, "Ysb", [128, 1024], F32, 2)
    pA, n_pA = kb.ps(st, "pA", [128, 1024], F32)
    pB, n_pB = kb.ps(st, "pB", [128, 1024], F32)
    pM, n_pM = kb.ps(st, "pM", [128, 512], F32)
    pQ, n_pQ = kb.ps(st, "pQ", [128, 512], F32)
    pyr = kb.ring(st, "pY", [128, 512], F32, 2, psum=True)
    post = PostStage(kb, st, bufs=1)
    P.op("sync", lambda e: e.dma_start(out=cvf[:], in_=T["cv_fm"][:, :, :]), writes=[n_cvf], dma=True)
    P.op("sync", lambda e: e.dma_start(out=b2[:], in_=T["cv_b_pw2"][0, :].partition_broadcast(128)), writes=[n_b2], dma=True)
    P.op("gpsimd", lambda e: e.memset(onesm[:], 1.0 / D), writes=[n_ones])
    segs = [(t0, 512, 0, S) for t0 in range(0, S, 512)] + [(S, C, S, S + C)]
    for (t0, n, seq0, seq1) in segs:
        lo = max(t0 - HW_, seq0)
        hi = min(t0 + n + HW_, seq1)
        w = hi - lo
        off = lo - (t0 - HW_)
        if off > 0 or off + w < n + 2 * HW_:
            P.op("gpsimd", lambda e, n=n: e.memset(U[:, :, 0:n + 2 * HW_], 0.0), writes=[n_U])
        pieces = [(0, min(w, 512))] + ([(512, w - 512)] if w > 512 else [])
        for c in range(8):
            (sg, n_sg) = sgr.next()
            for (pp, n_pp, cbase) in ((pA, n_pA, 0), (pB, n_pB, 1024)):
                def mm(e, pp=pp, cbase=cbase, c=c, lo=lo, pieces=pieces):
                    ins = None
                    for (a, ln) in pieces:
                        for kc in range(8):
                            ins = e.matmul(pp[:, a:a + ln], lhsT=w1[:, kc, cbase + c * 128:cbase + (c + 1) * 128],
                                           rhs=HT[:, kc, lo + a:lo + a + ln], start=(kc == 0), stop=(kc == 7))
                    return ins
                P.op("tensor", mm, reads=[n_HT, n_w1 + "_%d" % ((cbase + c * 128) // 512)], writes=[n_pp])
            P.op("scalar", lambda e, sg=sg, c=c, w=w: e.activation(out=sg[:, 0:w], in_=pB[:, 0:w], func=AF.Sigmoid,
                                                                 bias=cvf[:, c, 1:2], scale=1.0),
                 reads=[n_pB, n_cvf], writes=[n_sg])
            P.op("vector", lambda e, sg=sg, c=c, w=w, off=off: e.scalar_tensor_tensor(
                out=U[:, c, off:off + w], in0=pA[:, 0:w], scalar=cvf[:, c, 0:1], in1=sg[:, 0:w], op0=ALU.add, op1=ALU.mult),
                reads=[n_pA, n_cvf, n_sg], writes=[n_U])
            P.op("vector", lambda e, c=c, n=n: e.tensor_scalar(out=V[:, c, 0:n], in0=U[:, c, 0:n], scalar1=cvf[:, c, 5:6],
                                                               scalar2=cvf[:, c, 2:3], op0=ALU.mult, op1=ALU.add),
                 reads=[n_U, n_cvf], writes=[n_V])
            for j in range(1, CW):
                P.op("vector", lambda e, c=c, n=n, j=j: e.scalar_tensor_tensor(
                    out=V[:, c, 0:n], in0=U[:, c, j:j + n], scalar=cvf[:, c, 5 + j:6 + j], in1=V[:, c, 0:n],
                    op0=ALU.mult, op1=ALU.add), reads=[n_U, n_cvf, n_V], writes=[n_V])
        def mmM(e, n=n):
            ins = None
            for c in range(8):
                ins = e.matmul(pM[:, 0:n], lhsT=onesm[:], rhs=V[:, c, 0:n], start=(c == 0), stop=(c == 7))
            return ins
        P.op("tensor", mmM, reads=[n_ones, n_V], writes=[n_pM])
        for c in range(8):
            P.op("scalar", lambda e, c=c, n=n: e.activation(out=U[:, c, 0:n], in_=V[:, c, 0:n], func=AF.Square),
                 reads=[n_V], writes=[n_U])

        def mmQ(e, n=n):
            ins = None
            for c in range(8):
                ins = e.matmul(pQ[:, 0:n], lhsT=onesm[:], rhs=U[:, c, 0:n], start=(c == 0), stop=(c == 7))
            return ins
        P.op("tensor", mmQ, reads=[n_ones, n_U], writes=[n_pQ])
        P.op("scalar", lambda e, n=n: e.activation(out=msq[:, 0:n], in_=pM[:, 0:n], func=AF.Square), reads=[n_pM], writes=[n_msq])
        P.op("vector", lambda e, n=n: e.tensor_tensor(out=rstd[:, 0:n], in0=pQ[:, 0:n], in1=msq[:, 0:n], op=ALU.subtract),
             reads=[n_pQ, n_msq], writes=[n_rstd])
        P.op("vector", lambda e, n=n: e.tensor_scalar(out=rstd[:, 0:n], in0=rstd[:, 0:n], scalar1=EPS, scalar2=None, op0=ALU.add),
             reads=[n_rstd], writes=[n_rstd])
        P.op("scalar", lambda e, n=n: e.activation(out=rstd[:, 0:n], in_=rstd[:, 0:n], func=AF.Sqrt), reads=[n_rstd], writes=[n_rstd])
        P.op("vector", lambda e, n=n: e.reciprocal(out=rstd[:, 0:n], in_=rstd[:, 0:n]), reads=[n_rstd], writes=[n_rstd])
        for c in range(8):
            P.op("vector", lambda e, c=c, n=n: e.tensor_tensor(out=V[:, c, 0:n], in0=V[:, c, 0:n], in1=pM[:, 0:n], op=ALU.subtract),
                 reads=[n_V, n_pM], writes=[n_V])
            P.op("gpsimd", lambda e, c=c, n=n: e.tensor_tensor(out=V[:, c, 0:n], in0=V[:, c, 0:n], in1=rstd[:, 0:n], op=ALU.mult),
                 reads=[n_V, n_rstd], writes=[n_V])
            P.op("scalar", lambda e, c=c, n=n: e.activation(out=Z[:, c, 0:n], in_=V[:, c, 0:n], func=AF.Silu,
                                                           scale=cvf[:, c, 3:4], bias=cvf[:, c, 4:5]),
                 reads=[n_V, n_cvf], writes=[n_Z])
        for jt in range(n // 128):
            (ys, n_ys) = ysr.next()
            for h in range(2):
                (py, n_py) = pyr.next()

                def mmy(e, py=py, jt=jt, h=h):
                    ins = None
                    for c in range(8):
                        ins = e.matmul(py[:], lhsT=Z[:, c, jt * 128:(jt + 1) * 128], rhs=w2[:, c, h * 512:(h + 1) * 512],
                                       start=(c == 0), stop=(c == 7))
                    return ins
                P.op("tensor", mmy, reads=[n_Z, n_w2 + "_%d" % h], writes=[n_py])
                P.op("vector", lambda e, ys=ys, py=py, h=h: e.tensor_tensor(out=ys[:, h * 512:(h + 1) * 512], in0=py[:],
                                                                           in1=b2[:, h * 512:(h + 1) * 512], op=ALU.add),
                     reads=[n_py, n_b2], writes=[n_ys + "_%d" % h])
            post.run([ys[:, 0:512], ys[:, 512:1024]], [n_ys + "_0", n_ys + "_1"], t0 // 128 + jt, M, 1, xsrc, xdst)
    P.barrier()
    P.flush()


COMMON_SPECS = {
    "mod_w": [1024, 9216], "mod_b": [9216], "norm_g": [6, 1024], "mod_b_fm": [128, 72], "norm_g_fm": [128, 48],
    "ffn_w_gate": [2, 1024, 2816], "ffn_w_up": [2, 1024, 2816], "ffn_w_down": [2, 2816, 1024],
}
MIXER_SPECS = {
    0: {"na_w_qkv": [1, 1024, 3072], "na_w_o": [1, 1024, 1024], "na_bias": [16, 128, 9, 128]},
    1: {"sw_w_qkv_p": [1024, 1280], "sw_w_o": [1, 1024, 1024], "sw_sink": [1, 16], "sw_cos": [S, 32], "sw_sin": [S, 32]},
    2: {"cv_w_pw1": [1, 1024, 2048], "cv_w_pw2": [1, 1024, 1024], "cv_fm": [128, 8, 36], "cv_b_pw2": [1, 1024]},
    3: {"ga_w_qkv_p": [1024, 1536], "ga_w_o": [1, 1024, 1024], "ga_gain": [1280], "ga_cos": [S, 64], "ga_sin": [S, 64]},
}
CORE_SPECS = {"x": [S, D], "ctx": [C, D], "cfm": [128, 8, 2]}


def shared_specs(layers):
    sp = {k: [len(layers)] + v for k, v in COMMON_SPECS.items()}
    for li in layers:
        sp.update(MIXER_SPECS[li % 4])
    return sp


def shared_for(sh, layers):
    out = {}
    for k, shp in shared_specs(layers).items():
        a = sh[k]
        if k in COMMON_SPECS:
            a = np.ascontiguousarray(a[list(layers)])
        assert list(a.shape) == shp, (k, a.shape, shp)
        out[k] = a
    return out


def host_shared(inp):
    f = lambda a: np.ascontiguousarray(np.asarray(a, dtype=np.float32))
    sh = {}
    sh["mod_w"] = f(inp["mod_w"])
    sh["mod_b"] = f(inp["mod_b"])
    sh["norm_g"] = f(inp["norm_g"])
    sh["mod_b_fm"] = f(np.asarray(inp["mod_b"]).reshape(4, 72, 128).transpose(0, 2, 1))
    sh["norm_g_fm"] = f(np.asarray(inp["norm_g"]).reshape(4, 48, 128).transpose(0, 2, 1))
    for k in ("ffn_w_gate", "ffn_w_up", "ffn_w_down"):
        sh[k] = f(inp[k])
    sh["na_w_qkv"] = f(inp["na_w_qkv"])
    sh["na_w_o"] = f(inp["na_w_o"])
    sh["na_bias"] = na_bias_host(inp["na_rpb"])
    perm64 = np.concatenate([np.arange(0, 64, 2), np.arange(1, 64, 2)])
    w = np.asarray(inp["sw_w_qkv"])[0]
    cols = np.concatenate([h * 64 + perm64 for h in range(18)] + [np.arange(1152, 1280)])
    sh["sw_w_qkv_p"] = f(w[:, cols])
    sh["sw_w_o"] = f(inp["sw_w_o"])
    sh["sw_sink"] = f(inp["sw_sink"])
    sh["sw_cos"], sh["sw_sin"] = rope_tables(64)
    sh["cv_w_pw1"] = f(inp["cv_w_pw1"])
    sh["cv_w_pw2"] = f(inp["cv_w_pw2"])
    sh["cv_b_pw2"] = f(inp["cv_b_pw2"])
    b1 = np.asarray(inp["cv_b_pw1"])[0]
    vecs = [b1[:1024], b1[1024:], np.asarray(inp["cv_b_dw"])[0], np.asarray(inp["cv_ln_g"])[0], np.asarray(inp["cv_ln_b"])[0]]
    vecs += [np.asarray(inp["cv_w_dw"])[0][j] for j in range(31)]
    sh["cv_fm"] = f(np.stack([v.reshape(8, 128).T for v in vecs], axis=-1))
    perm = np.concatenate([np.arange(0, 128, 2), np.arange(1, 128, 2)])
    w = np.asarray(inp["ga_w_qkv"])[0]
    cols = np.concatenate([h * 128 + perm for h in range(10)] + [np.arange(1280, 1536)])
    sh["ga_w_qkv_p"] = f(w[:, cols])
    sh["ga_w_o"] = f(inp["ga_w_o"])
    qn = np.asarray(inp["ga_q_norm"])[0][perm]
    kn = np.asarray(inp["ga_k_norm"])[0][perm]
    sh["ga_gain"] = f(np.concatenate([qn] * 8 + [kn] * 2))
    cs, sn = rope_tables(128)
    sh["ga_cos"], sh["ga_sin"] = cs, sn
    return sh


def rope_tables(head_dim):
    t = np.arange(S)
    row = (t // GRID_W).astype(np.float32)
    col = (t % GRID_W).astype(np.float32)
    n = head_dim // 4
    freq = np.power(np.float32(10000.0), -(np.arange(n, dtype=np.float32) / np.float32(n))).astype(np.float32)
    ang = np.concatenate([row[:, None] * freq[None, :], col[:, None] * freq[None, :]], axis=-1).astype(np.float32)
    return np.cos(ang).astype(np.float32), np.sin(ang).astype(np.float32)


def host_core(inp, b):
    f = lambda a: np.ascontiguousarray(np.asarray(a, dtype=np.float32))
    c = np.asarray(inp["c"])[b].reshape(8, 128).T
    cc = np.asarray(inp["c_ctx"]).reshape(8, 128).T
    return {"x": f(inp["x"][b]), "ctx": f(inp["ctx"][b]), "cfm": f(np.stack([c, cc], axis=-1))}


def layer_plan(li):
    last = li == 3
    return [("mod", li), ("ffn", li, 0, True), ("mixer", li), ("ffn", li, 1, not last)]


def build_program(plan, layers):
    nc = bass.Bass("TRN2", target_bir_lowering=False)
    T = {}
    for k, shp in list(shared_specs(layers).items()) + list(CORE_SPECS.items()):
        T[k] = nc.dram_tensor(k, shp, F32, kind="ExternalInput").ap()
    out = nc.dram_tensor("out", [S, D], F32, kind="ExternalOutput").ap()
    outc = nc.dram_tensor("outc", [C, D], F32, kind="ExternalOutput").ap()
    XL = nc.dram_tensor("XL", [S, D], F32, kind="Internal").ap()
    XC = nc.dram_tensor("XC", [C, D], F32, kind="Internal").ap()

    def xs(tt):
        if tt < 16:
            return XL[tt * 128:(tt + 1) * 128, :], "XL_%d" % tt
        return XC[(tt - 16) * 128:(tt - 15) * 128, :], "XC_%d" % (tt - 16)

    with ExitStack() as st:
        kb = KB(nc, st)
        P = kb.P
        stage_consts(kb, st)
        for tt in range(18):
            dst, n_dst = xs(tt)
            src = T["x"][tt * 128:(tt + 1) * 128, :] if tt < 16 else T["ctx"][(tt - 16) * 128:(tt - 15) * 128, :]
            P.op("sync", lambda e, dst=dst, src=src: e.dma_start(out=dst, in_=src), writes=[n_dst], dma=True)
        M = None
        lst = None
        for step in plan:
            lloc = list(layers).index(step[1])
            if step[0] == "mod":
                if lst is not None:
                    P.barrier()
                    P.flush()
                    lst.close()
                lst = ExitStack()
                with ExitStack() as st2:
                    M = stage_mod(kb, st2, T, lloc, lst)
                    P.barrier()
                    P.flush()
            elif step[0] == "ffn":
                which = step[2]
                tiles = list(range(18)) if step[3] else list(range(16))
                with ExitStack() as st2:
                    stage_ffn(kb, st2, T, lloc, which, tiles, M, 0 if which == 0 else 2, xs, xs)
            elif step[0] == "mixer":
                kind = step[1] % 4
                with ExitStack() as st2:
                    [stage_mixer_a, stage_mixer_b, stage_mixer_c, stage_mixer_d][kind](kb, st2, T, M, xs, xs)
            else:
                raise ValueError(step)
        for tt in range(18):
            src, n_src = xs(tt)
            dst = out[tt * 128:(tt + 1) * 128, :] if tt < 16 else outc[(tt - 16) * 128:(tt - 15) * 128, :]
            ev = P.op("sync", lambda e, dst=dst, src=src: e.dma_start(out=dst, in_=src), reads=[n_src], dma=True)
            P.out_evs.append(ev)
        waits = {}
        for ev in P.out_evs:
            P._need("sync", ev, waits)
        P.ops["sync"].append((None, list(waits.items()), None))
        P.barrier()
        P.flush()
        if lst is not None:
            lst.close()
    return nc


FUSED = False


def kernel(**inputs):
    sh = host_shared(inputs)
    cores = [host_core(inputs, b) for b in range(8)]
    groups = [[0, 1, 2, 3]] if FUSED else [[0], [1], [2], [3]]
    for layers in groups:
        plan = []
        for li in layers:
            plan += layer_plan(li)
        nc = build_program(plan, layers)
        shl = shared_for(sh, layers)
        res = run_bass_kernel_spmd(nc, [{**shl, **cores[b]} for b in range(8)], core_ids=list(range(8)))
        for b in range(8):
            cores[b]["x"] = np.ascontiguousarray(res.results[b]["out"], dtype=np.float32)
            cores[b]["ctx"] = np.ascontiguousarray(res.results[b]["outc"], dtype=np.float32)
    return np.stack([cores[b]["x"] for b in range(8)], axis=0).astype(np.float32)
```

```python
import numpy as np
from contextlib import ExitStack
import concourse.bass as bass
import concourse.mybir as mybir
from concourse.bass_utils import run_bass_kernel_spmd

F32 = mybir.dt.float32
BF16 = mybir.dt.bfloat16
AF = mybir.ActivationFunctionType
ALU = mybir.AluOpType
AX = mybir.AxisListType

D = 1024
S = 2048
C = 256
NTOK = S + C
DFF = 2816
EPS = 1e-6
KC = D // 128
GRID_W = 64


class Prog:
    ENGS = ["tensor", "vector", "scalar", "gpsimd", "sync"]

    def __init__(self, nc, stack, n_dma_sems=48):
        self.nc = nc
        self.ops = {e: [] for e in self.ENGS}
        self.count = {e: 0 for e in self.ENGS}
        self.waited = {e: {} for e in self.ENGS}
        self.last_w = {}
        self.readers = {}
        self.n_dma_sems = n_dma_sems
        self.dma_i = 0
        self.dma_j = 0
        self.dma_sem_use = [0] * n_dma_sems
        self.sems = {e: stack.enter_context(nc.semaphore("s_" + e)) for e in self.ENGS}
        self.dsems = [stack.enter_context(nc.semaphore("d_%d" % i)) for i in range(n_dma_sems)]
        self.out_evs = []

    def _need(self, eng, ev, waits):
        if ev is None:
            return
        if ev[0] == "e" and ev[1] == eng and eng == "tensor":
            return
        key = (ev[0], ev[1])
        if self.waited[eng].get(key, 0) >= ev[2]:
            return
        self.waited[eng][key] = ev[2]
        waits[key] = max(waits.get(key, 0), ev[2])

    def op(self, eng, fn, reads=(), writes=(), dma=False):
        writes = list(writes) + [r for r in reads if r.startswith("PSUM_")]
        reads = [r for r in reads if not r.startswith("PSUM_")]
        waits = {}
        for r in reads:
            self._need(eng, self.last_w.get(r), waits)
        for w in writes:
            self._need(eng, self.last_w.get(w), waits)
            for ev in self.readers.get(w, ()):
                self._need(eng, ev, waits)
        if dma:
            half = self.n_dma_sems // 2
            if eng == "gpsimd":
                si = half + self.dma_j % half
                self.dma_j += 1
            else:
                si = self.dma_i % half
                self.dma_i += 1
            if self.dma_sem_use[si] > 0:
                self._need(eng, ("d", si, 16 * self.dma_sem_use[si]), waits)
            self.dma_sem_use[si] += 1
            ev = ("d", si, 16 * self.dma_sem_use[si])
        else:
            self.count[eng] += 1
            ev = ("e", eng, self.count[eng])
        self.ops[eng].append((fn, list(waits.items()), ev))
        for r in reads:
            self.readers.setdefault(r, []).append(ev)
        for w in writes:
            self.last_w[w] = ev
            self.readers[w] = []
        return ev

    def barrier(self):
        evs = [("e", e, self.count[e]) for e in self.ENGS if self.count[e] > 0]
        evs += [("d", si, 16 * u) for si, u in enumerate(self.dma_sem_use) if u > 0]
        for e in self.ENGS:
            waits = {}
            for ev in evs:
                if ev[0] == "e" and ev[1] == e:
                    continue
                self._need(e, ev, waits)
            if waits:
                self.ops[e].append((None, list(waits.items()), None))
        self.last_w = {}
        self.readers = {}

    def flush(self):
        nc = self.nc
        with nc.Block() as block:
            def run(engname):
                def body(eng):
                    for fn, waits, ev in self.ops[engname]:
                        for (kind, k), val in waits:
                            sem = self.sems[k] if kind == "e" else self.dsems[k]
                            eng.wait_ge(sem, val)
                        if fn is None:
                            continue
                        ins = fn(eng)
                        if ev[0] == "e":
                            ins.then_inc(self.sems[ev[1]], 1)
                        else:
                            ins.then_inc(self.dsems[ev[1]], 16)
                return body
            block.tensor(run("tensor"))
            block.vector(run("vector"))
            block.scalar(run("scalar"))
            block.gpsimd(run("gpsimd"))
            block.sync(run("sync"))
        self.ops = {e: [] for e in self.ENGS}


class Ring:
    def __init__(self, items):
        self.items = items
        self.i = 0

    def next(self):
        it = self.items[self.i % len(self.items)]
        self.i += 1
        return it


class KB:
    def __init__(self, nc, st):
        self.nc = nc
        self.st = st
        self.P = Prog(nc, st)
        self.uid = 0

    def sb(self, st, name, shape, dt):
        self.uid += 1
        nm = "%s_%d" % (name, self.uid)
        return st.enter_context(self.nc.sbuf_tensor(nm, list(shape), dt)), nm

    def ps(self, st, name, shape, dt):
        self.uid += 1
        nm = "PSUM_%s_%d" % (name, self.uid)
        return st.enter_context(self.nc.psum_tensor(nm, list(shape), dt)), nm

    def ring(self, st, name, shape, dt, n, psum=False):
        return Ring([(self.ps if psum else self.sb)(st, name, shape, dt) for _ in range(n)])


def tok_macros(t0, ntok, width=512):
    out = []
    o = 0
    while o < ntok:
        n = min(width, ntok - o)
        out.append((t0 + o, n))
        o += n
    return out


def stage_consts(kb, st):
    nc, P = kb.nc, kb.P
    identf, n_if = kb.sb(st, "identf", [128, 128], F32)
    ident, n_i = kb.sb(st, "ident", [128, 128], BF16)
    nh, n_nh = kb.sb(st, "neghalf", [128, 8], F32)
    P.op("gpsimd", lambda e: e.memset(identf[:], 1.0), writes=[n_if])
    P.op("gpsimd", lambda e: e.affine_select(out=identf[:], in_=identf[:], pattern=[[-1, 128]],
                                              compare_op=ALU.is_equal, fill=0.0, base=0, channel_multiplier=1),
         reads=[n_if], writes=[n_if])
    P.op("vector", lambda e: e.tensor_copy(out=ident[:], in_=identf[:]), reads=[n_if], writes=[n_i])
    P.op("gpsimd", lambda e: e.memset(nh[:], -0.5), writes=[n_nh])
    kb.ident, kb.n_ident = ident, n_i
    kb.identf, kb.n_identf = identf, n_if
    kb.nh, kb.n_nh = nh, n_nh
    NEGB = -30000.0
    for nm, pat, cm in (("masklo", [[-1, 128]], 1), ("maskhi", [[1, 128]], -1)):
        mf, n_mf = kb.sb(st, nm + "f", [128, 128], F32)
        mb, n_mb = kb.sb(st, nm, [128, 128], BF16)
        P.op("gpsimd", lambda e, mf=mf: e.memset(mf[:], 0.0), writes=[n_mf])
        P.op("gpsimd", lambda e, mf=mf, pat=pat, cm=cm: e.affine_select(out=mf[:], in_=mf[:], pattern=pat, compare_op=ALU.is_ge,
                                                                        fill=NEGB, base=0, channel_multiplier=cm),
             reads=[n_mf], writes=[n_mf])
        P.op("vector", lambda e, mf=mf, mb=mb: e.tensor_copy(out=mb[:], in_=mf[:]), reads=[n_mf], writes=[n_mb])
        setattr(kb, nm, mb)
        setattr(kb, "n_" + nm, n_mb)
    nh2, n_nh2 = kb.sb(st, "neghalf2", [128, 16], F32)
    P.op("gpsimd", lambda e: e.memset(nh2[:], -0.5), writes=[n_nh2])
    kb.nh2, kb.n_nh2 = nh2, n_nh2


def rstd_from_ssq(kb, ssq, n_ssq, ncols, inv_n):
    P = kb.P
    P.op("vector", lambda e: e.tensor_scalar(out=ssq[:, 0:ncols], in0=ssq[:, 0:ncols], scalar1=inv_n, scalar2=EPS,
                                             op0=ALU.mult, op1=ALU.add), reads=[n_ssq], writes=[n_ssq])
    P.op("gpsimd", lambda e: e.tensor_tensor(out=ssq[:, 0:ncols], in0=ssq[:, 0:ncols], in1=kb.nh[:, 0:ncols],
                                             op=ALU.pow), reads=[n_ssq, kb.n_nh], writes=[n_ssq])


def stage_mod(kb, st, T, li, lst):
    nc, P = kb.nc, kb.P
    Afm, n_A = kb.sb(lst, "Afm", [128, 3, 2, 8], F32)
    Bfm, n_B = kb.sb(lst, "Bfm", [128, 3, 2, 8], F32)
    Gbc, n_G = kb.sb(lst, "Gbc", [128, 3, 2, 1024], F32)
    cf, n_cf = kb.sb(st, "cf", [128, 8, 2], F32)
    sc, n_sc = kb.sb(st, "sc", [128, 8, 2], F32)
    sbc, n_sbc = kb.sb(st, "sbc", [128, 2, 8, 128], F32)
    mbfm, n_mbfm = kb.sb(st, "mbfm", [128, 72], F32)
    gfm, n_gfm = kb.sb(st, "gfm", [128, 48], F32)
    modfm, n_modfm = kb.sb(st, "modfm", [128, 9, 8, 2], F32)
    wring = kb.ring(st, "modw", [128, 8, 512], F32, 2)
    bring = kb.ring(st, "modbb", [128, 512], F32, 2)
    gring = kb.ring(st, "gpost", [128, 512], F32, 2)
    tring = kb.ring(st, "modtmp", [128, 512], F32, 2)
    pfm, n_pfm = kb.ps(st, "pfm", [128, 9, 8, 2], F32)
    pbr = kb.ring(st, "pbc", [128, 512], F32, 2, psum=True)

    P.op("sync", lambda e: e.dma_start(out=cf[:], in_=T["cfm"][:, :, :]), writes=[n_cf], dma=True)
    P.op("sync", lambda e: e.dma_start(out=mbfm[:], in_=T["mod_b_fm"][li]), writes=[n_mbfm], dma=True)
    P.op("sync", lambda e: e.dma_start(out=gfm[:], in_=T["norm_g_fm"][li]), writes=[n_gfm], dma=True)
    P.op("scalar", lambda e: e.activation(out=sc[:], in_=cf[:], func=AF.Silu), reads=[n_cf], writes=[n_sc])
    for s in range(2):
        for kc in range(8):
            P.op("vector", lambda e, s=s, kc=kc: e.tensor_copy(out=sbc[:, s, kc, :],
                                                               in_=sc[:, kc, s:s + 1].to_broadcast([128, 128])),
                 reads=[n_sc], writes=[n_sbc])
    mw = T["mod_w"][li].rearrange("(kc p) n -> p kc n", p=128)
    wsub = [0.5, 1.0, 0.5]
    for slot in range(9):
        sub = slot // 3
        for half in range(2):
            col0 = slot * 1024 + half * 512
            (wt, n_wt) = wring.next()
            P.op("sync", lambda e, wt=wt, col0=col0: e.dma_start(out=wt[:], in_=mw[:, :, col0:col0 + 512]),
                 writes=[n_wt], dma=True)
            if slot % 3 != 2:
                def mm(e, wt=wt, slot=slot, half=half):
                    ins = None
                    for oc in range(4):
                        for kc in range(8):
                            ins = e.matmul(pfm[:, slot, half * 4 + oc, :], lhsT=wt[:, kc, oc * 128:(oc + 1) * 128],
                                           rhs=sc[:, kc, :], start=(kc == 0), stop=(kc == 7))
                    return ins
                P.op("tensor", mm, reads=[n_wt, n_sc], writes=[n_pfm])
            else:
                (bt, n_bt) = bring.next()
                (gt, n_gt) = gring.next()
                P.op("sync", lambda e, bt=bt, col0=col0: e.dma_start(
                    out=bt[:], in_=T["mod_b"][li, col0:col0 + 512].partition_broadcast(128)), writes=[n_bt], dma=True)
                P.op("sync", lambda e, gt=gt, sub=sub, half=half: e.dma_start(
                    out=gt[:], in_=T["norm_g"][li, 2 * sub + 1, half * 512:(half + 1) * 512].partition_broadcast(128)),
                    writes=[n_gt], dma=True)
                for s in range(2):
                    (pb, n_pb) = pbr.next()
                    (tt, n_tt) = tring.next()

                    def mm(e, wt=wt, s=s, pb=pb):
                        ins = None
                        for kc in range(8):
                            ins = e.matmul(pb[:], lhsT=sbc[:, s, kc, :], rhs=wt[:, kc, :], start=(kc == 0), stop=(kc == 7))
                        return ins
                    P.op("tensor", mm, reads=[n_wt, n_sbc], writes=[n_pb])
                    P.op("vector", lambda e, pb=pb, bt=bt, tt=tt: e.tensor_tensor(out=tt[:], in0=pb[:], in1=bt[:], op=ALU.add),
                         reads=[n_pb, n_bt], writes=[n_tt])
                    P.op("vector", lambda e, tt=tt, gt=gt, sub=sub, s=s, half=half: e.scalar_tensor_tensor(
                        out=Gbc[:, sub, s, half * 512:(half + 1) * 512], in0=tt[:], scalar=wsub[sub], in1=gt[:],
                        op0=ALU.mult, op1=ALU.mult), reads=[n_tt, n_gt], writes=[n_G])
    for slot in range(9):
        if slot % 3 == 2:
            continue
        for s in range(2):
            P.op("vector", lambda e, s=s, slot=slot: e.tensor_tensor(out=modfm[:, slot, :, s], in0=pfm[:, slot, :, s],
                                                                     in1=mbfm[:, slot * 8:slot * 8 + 8], op=ALU.add),
                 reads=[n_pfm, n_mbfm], writes=[n_modfm])
    for sub in range(3):
        for s in range(2):
            P.op("vector", lambda e, sub=sub, s=s: e.scalar_tensor_tensor(
                out=Afm[:, sub, s, :], in0=modfm[:, 3 * sub + 1, :, s], scalar=1.0,
                in1=gfm[:, (2 * sub) * 8:(2 * sub) * 8 + 8], op0=ALU.add, op1=ALU.mult),
                reads=[n_modfm, n_gfm], writes=[n_A])
            P.op("vector", lambda e, sub=sub, s=s: e.tensor_copy(out=Bfm[:, sub, s, :], in_=modfm[:, 3 * sub, :, s]),
                 reads=[n_modfm], writes=[n_B])
    return dict(Afm=Afm, n_A=n_A, Bfm=Bfm, n_B=n_B, Gbc=Gbc, n_G=n_G)


def stage_norm_T(kb, st, xsrc, tiles, M, sub, HT, n_HT, psum_ring=None, rings=(6, 2, 8)):
    nc, P = kb.nc, kb.P
    xr = kb.ring(st, "nx", [128, 1024], F32, rings[0])
    jr = kb.ring(st, "njunk", [128, 1024], BF16, rings[1])
    xnr = kb.ring(st, "nxn", [128, 1024], BF16, rings[2])
    sr = kb.ring(st, "nssq", [128, 8], F32, 2)
    ptr = psum_ring or kb.ring(st, "nptr", [128, 512], BF16, 2, psum=True)
    ev = 0
    for m0 in range(0, len(tiles), 4):
        grp = tiles[m0:m0 + 4]
        n = len(grp)
        s = 0 if grp[0] < 16 else 1
        assert all((t < 16) == (grp[0] < 16) for t in grp)
        (ssq, n_ssq) = sr.next()
        xts = []
        for j, tt in enumerate(grp):
            (xt, n_xt) = xr.next()
            (jk, n_jk) = jr.next()
            src, n_src = xsrc(tt)
            P.op("sync", lambda e, xt=xt, src=src: e.dma_start(out=xt[:], in_=src), reads=[n_src], writes=[n_xt], dma=True)
            P.op("scalar", lambda e, xt=xt, jk=jk, ssq=ssq, j=j: e.activation(out=jk[:], in_=xt[:], func=AF.Square,
                                                                               accum_out=ssq[:, j:j + 1]),
                 reads=[n_xt], writes=[n_jk, n_ssq])
            xts.append((xt, n_xt))
        rstd_from_ssq(kb, ssq, n_ssq, n, 1.0 / D)
        xns = []
        for j, tt in enumerate(grp):
            (xn, n_xn) = xnr.next()
            xt, n_xt = xts[j]
            P.op("gpsimd" if j % 2 else "vector", lambda e, xn=xn, xt=xt, ssq=ssq, j=j: e.tensor_scalar(
                out=xn[:], in0=xt[:], scalar1=ssq[:, j:j + 1], scalar2=None, op0=ALU.mult),
                reads=[n_xt, n_ssq], writes=[n_xn])
            xns.append((xn, n_xn))
        for kc in range(8):
            (pt, n_pt) = ptr.next()

            def tr(e, pt=pt, kc=kc, xns=xns):
                ins = None
                for j, (xn, _) in enumerate(xns):
                    ins = e.transpose(out=pt[:, j * 128:(j + 1) * 128], in_=xn[:, kc * 128:(kc + 1) * 128],
                                      identity=kb.ident[:])
                return ins
            P.op("tensor", tr, reads=[nm for _, nm in xns] + [kb.n_ident], writes=[n_pt])
            c0 = m0 * 128
            if ev % 2 == 0:
                P.op("scalar", lambda e, pt=pt, kc=kc, c0=c0, n=n, s=s: e.activation(
                    out=HT[:, kc, c0:c0 + n * 128], in_=pt[:, 0:n * 128], func=AF.Identity,
                    scale=M["Afm"][:, sub, s, kc:kc + 1], bias=M["Bfm"][:, sub, s, kc:kc + 1]),
                    reads=[n_pt, M["n_A"], M["n_B"]], writes=[n_HT])
            else:
                P.op("vector", lambda e, pt=pt, kc=kc, c0=c0, n=n, s=s: e.tensor_scalar(
                    out=HT[:, kc, c0:c0 + n * 128], in0=pt[:, 0:n * 128], scalar1=M["Afm"][:, sub, s, kc:kc + 1],
                    scalar2=M["Bfm"][:, sub, s, kc:kc + 1], op0=ALU.mult, op1=ALU.add),
                    reads=[n_pt, M["n_A"], M["n_B"]], writes=[n_HT])
            ev += 1


class PostStage:
    def __init__(self, kb, st, bufs=2):
        self.kb = kb
        self.xr = kb.ring(st, "px", [128, 1024], F32, bufs)
        self.jr = kb.ring(st, "pjunk", [128, 1024], BF16, bufs)
        self.tr = kb.ring(st, "ptmp", [128, 1024], F32, bufs)
        self.orr = kb.ring(st, "pout", [128, 1024], F32, bufs)
        self.sr = kb.ring(st, "pssq", [128, 8], F32, 4)

    def run(self, y_ap_halves, y_names, tt, M, sub, xsrc, xdst, is_out=False):
        kb = self.kb
        P = kb.P
        s = 0 if tt < 16 else 1
        (xt, n_xt) = self.xr.next()
        (jk, n_jk) = self.jr.next()
        (tm, n_tm) = self.tr.next()
        (xo, n_xo) = self.orr.next()
        (ssq, n_ssq) = self.sr.next()
        src, n_src = xsrc(tt)
        dst, n_dst = xdst(tt)
        P.op("sync", lambda e: e.dma_start(out=xt[:], in_=src), reads=[n_src], writes=[n_xt], dma=True)
        for h, yh in enumerate(y_ap_halves):
            P.op("scalar", lambda e, h=h, yh=yh: e.activation(out=jk[:, h * 512:(h + 1) * 512], in_=yh, func=AF.Square,
                                                             accum_out=ssq[:, h:h + 1]),
                 reads=[y_names[h]], writes=[n_jk, n_ssq])
        P.op("vector", lambda e: e.tensor_tensor(out=ssq[:, 2:3], in0=ssq[:, 0:1], in1=ssq[:, 1:2], op=ALU.add),
             reads=[n_ssq], writes=[n_ssq])
        P.op("vector", lambda e: e.tensor_scalar(out=ssq[:, 3:4], in0=ssq[:, 2:3], scalar1=1.0 / D, scalar2=EPS,
                                                 op0=ALU.mult, op1=ALU.add), reads=[n_ssq], writes=[n_ssq])
        P.op("gpsimd", lambda e: e.tensor_tensor(out=ssq[:, 4:5], in0=ssq[:, 3:4], in1=kb.nh[:, 0:1], op=ALU.pow),
             reads=[n_ssq, kb.n_nh], writes=[n_ssq])
        for h, yh in enumerate(y_ap_halves):
            P.op("vector", lambda e, h=h, yh=yh: e.scalar_tensor_tensor(
                out=tm[:, h * 512:(h + 1) * 512], in0=yh, scalar=ssq[:, 4:5],
                in1=M["Gbc"][:, sub, s, h * 512:(h + 1) * 512], op0=ALU.mult, op1=ALU.mult),
                reads=[y_names[h], n_ssq, M["n_G"]], writes=[n_tm])
        P.op("gpsimd", lambda e: e.tensor_tensor(out=xo[:], in0=tm[:], in1=xt[:], op=ALU.add),
             reads=[n_tm, n_xt], writes=[n_xo])
        ev = P.op("sync", lambda e: e.dma_start(out=dst, in_=xo[:]), reads=[n_xo], writes=[n_dst], dma=True)
        if is_out:
            P.out_evs.append(ev)


def stage_ffn(kb, st, T, li, which, tiles, M, sub, xsrc, xdst, is_out=False):
    nc, P = kb.nc, kb.P
    nt = len(tiles)
    ntok = nt * 128
    HT, n_HT = kb.sb(st, "HT", [128, 8, ntok], BF16)
    Y, n_Y = kb.sb(st, "Y", [128, nt, 1024], F32)
    with ExitStack() as st2:
        stage_norm_T(kb, st2, xsrc, tiles, M, sub, HT, n_HT)
        P.barrier()
        P.flush()
    with ExitStack() as st3:
        FG = 256
        ngrp = DFF // FG
        wgr = kb.ring(st3, "wg", [128, 8, FG], BF16, 2)
        wur = kb.ring(st3, "wu", [128, 8, FG], BF16, 2)
        wdr = kb.ring(st3, "wd", [128, FG // 128, 1024], BF16, 2)
        actr = kb.ring(st3, "actT", [128, FG // 128, ntok], BF16, 2)
        sgr = kb.ring(st3, "sg", [128, 512], F32, 3)
        pgr = kb.ring(st3, "pg", [128, 512], F32, 2, psum=True)
        pur = kb.ring(st3, "pu", [128, 512], F32, 2, psum=True)
        pdr = kb.ring(st3, "pd", [128, 512], F32, 3, psum=True)
        wgd = T["ffn_w_gate"][li, which].rearrange("(kc p) n -> p kc n", p=128)
        wud = T["ffn_w_up"][li, which].rearrange("(kc p) n -> p kc n", p=128)
        wdd = T["ffn_w_down"][li, which].rearrange("(fc p) n -> p fc n", p=128)
        macros = tok_macros(0, ntok)
        for g in range(ngrp):
            f0 = g * FG
            (wg, n_wg) = wgr.next()
            (wu, n_wu) = wur.next()
            (wd, n_wd) = wdr.next()
            (act, n_act) = actr.next()
            P.op("gpsimd", lambda e, wg=wg, f0=f0: e.dma_start(out=wg[:], in_=wgd[:, :, f0:f0 + FG]), writes=[n_wg], dma=True)
            P.op("gpsimd", lambda e, wu=wu, f0=f0: e.dma_start(out=wu[:], in_=wud[:, :, f0:f0 + FG]), writes=[n_wu], dma=True)
            P.op("gpsimd", lambda e, wd=wd, f0=f0: e.dma_start(out=wd[:], in_=wdd[:, f0 // 128:(f0 + FG) // 128, :]),
                 writes=[n_wd], dma=True)
            for fc in range(FG // 128):
                for (t0, n) in macros:
                    (pg, n_pg) = pgr.next()
                    (pu, n_pu) = pur.next()
                    (sg, n_sg) = sgr.next()

                    def mm(e, w=wg, p=pg, fc=fc, t0=t0, n=n):
                        ins = None
                        for kc in range(8):
                            ins = e.matmul(p[:, 0:n], lhsT=w[:, kc, fc * 128:(fc + 1) * 128], rhs=HT[:, kc, t0:t0 + n],
                                           start=(kc == 0), stop=(kc == 7))
                        return ins
                    P.op("tensor", mm, reads=[n_wg, n_HT], writes=[n_pg])

                    def mm2(e, w=wu, p=pu, fc=fc, t0=t0, n=n):
                        ins = None
                        for kc in range(8):
                            ins = e.matmul(p[:, 0:n], lhsT=w[:, kc, fc * 128:(fc + 1) * 128], rhs=HT[:, kc, t0:t0 + n],
                                           start=(kc == 0), stop=(kc == 7))
                        return ins
                    P.op("tensor", mm2, reads=[n_wu, n_HT], writes=[n_pu])
                    P.op("scalar", lambda e, sg=sg, pg=pg, n=n: e.activation(out=sg[:, 0:n], in_=pg[:, 0:n], func=AF.Silu),
                         reads=[n_pg], writes=[n_sg])
                    P.op("vector", lambda e, act=act, sg=sg, pu=pu, fc=fc, t0=t0, n=n: e.tensor_tensor(
                        out=act[:, fc, t0:t0 + n], in0=sg[:, 0:n], in1=pu[:, 0:n], op=ALU.mult),
                        reads=[n_sg, n_pu], writes=[n_act + "_%d" % (t0 // 512)])
            for j in range(nt):
                for h in range(2):
                    (pd, n_pd) = pdr.next()

                    def mmd(e, pd=pd, act=act, wd=wd, j=j, h=h):
                        ins = None
                        nfc = FG // 128
                        for fc in range(nfc):
                            ins = e.matmul(pd[:], lhsT=act[:, fc, j * 128:(j + 1) * 128], rhs=wd[:, fc, h * 512:(h + 1) * 512],
                                           start=(fc == 0), stop=(fc == nfc - 1))
                        return ins
                    P.op("tensor", mmd, reads=[n_act + "_%d" % (j // 4), n_wd], writes=[n_pd])
                    yname = n_Y + "_%d_%d" % (j, h)
                    if g == 0:
                        P.op("scalar", lambda e, pd=pd, j=j, h=h: e.activation(out=Y[:, j, h * 512:(h + 1) * 512], in_=pd[:],
                                                                                func=AF.Identity),
                             reads=[n_pd], writes=[yname])
                    else:
                        P.op("vector", lambda e, pd=pd, j=j, h=h: e.tensor_tensor(
                            out=Y[:, j, h * 512:(h + 1) * 512], in0=Y[:, j, h * 512:(h + 1) * 512], in1=pd[:], op=ALU.add),
                            reads=[n_pd, yname], writes=[yname])
        P.barrier()
        P.flush()
    post = PostStage(kb, st)
    for j, tt in enumerate(tiles):
        post.run([Y[:, j, 0:512], Y[:, j, 512:1024]], [n_Y + "_%d_0" % j, n_Y + "_%d_1" % j], tt, M, sub, xsrc, xdst,
                 is_out=is_out)
    P.barrier()
    P.flush()


class AttnCore:
    def __init__(self, kb, st, n_ps=3, n_pt=3):
        self.kb = kb
        self.psr = kb.ring(st, "pS", [128, 512], F32, n_ps, psum=True)
        self.ptr = kb.ring(st, "PT", [128, 512], BF16, n_pt)

    def bank(self, blocks, exp_scale):
        kb = self.kb
        P = kb.P
        (pS, n_pS) = self.psr.next()
        (PT, n_PT) = self.ptr.next()
        nb = len(blocks)
        assert 1 <= nb <= 4

        def mm(e):
            ins = None
            for i, b in enumerate(blocks):
                o = pS[:, i * 128:(i + 1) * 128]
                if b.get("bias") is not None:
                    e.matmul(o, lhsT=kb.ident[:], rhs=b["bias"], start=True, stop=False)
                    ins = e.matmul(o, lhsT=b["kT"], rhs=b["qT"], start=False, stop=True)
                else:
                    ins = e.matmul(o, lhsT=b["kT"], rhs=b["qT"], start=True, stop=True)
            return ins
        rd = [kb.n_ident]
        for b in blocks:
            rd += list(b["rd"])
        P.op("tensor", mm, reads=rd, writes=[n_pS])
        P.op("scalar", lambda e: e.activation(out=PT[:, 0:nb * 128], in_=pS[:, 0:nb * 128], func=AF.Exp, scale=exp_scale),
             reads=[n_pS], writes=[n_PT])

        def pv(e):
            ins = None
            for i, b in enumerate(blocks):
                ins = e.matmul(b["po"], lhsT=PT[:, i * 128:(i + 1) * 128], rhs=b["v"], start=b["start"], stop=b["stop"])
            return ins
        rd = [n_PT]
        wr = []
        for b in blocks:
            rd += list(b["rdv"])
            wr.append(b["n_po"])
        P.op("tensor", pv, reads=rd, writes=wr)


class OutProj:
    def __init__(self, kb, st, T, wo_dram, M, xsrc, xdst, post_bufs=1):
        self.kb = kb
        P = kb.P
        self.M = M
        self.xsrc, self.xdst = xsrc, xdst
        self.wo, self.n_wo = kb.sb(st, "wo", [128, 8, 1024], BF16)
        P.op("gpsimd", lambda e: e.dma_start(out=self.wo[:], in_=wo_dram.rearrange("(kc p) n -> p kc n", p=128)),
             writes=[self.n_wo], dma=True)
        self.otr = kb.ring(st, "OT", [128, 8, 128], BF16, 2)
        self.ptr = kb.ring(st, "pOT", [128, 1024], BF16, 1, psum=True)
        self.pyr = kb.ring(st, "pY", [128, 512], F32, 2, psum=True)
        self.post = PostStage(kb, st, bufs=post_bufs)

    def run(self, ocat_ap, n_ocat, tt, is_out=False):
        kb = self.kb
        P = kb.P
        (OT, n_OT) = self.otr.next()
        (pt, n_pt) = self.ptr.next()

        def tr(e):
            ins = None
            for c in range(8):
                ins = e.transpose(out=pt[:, c * 128:(c + 1) * 128], in_=ocat_ap[:, c * 128:(c + 1) * 128], identity=kb.ident[:])
            return ins
        P.op("tensor", tr, reads=[n_ocat, kb.n_ident], writes=[n_pt])
        P.op("vector", lambda e: e.tensor_copy(out=OT[:].rearrange("p c t -> p (c t)"), in_=pt[:]), reads=[n_pt], writes=[n_OT])
        halves = []
        names = []
        for h in range(2):
            (py, n_py) = self.pyr.next()

            def mm(e, py=py, h=h):
                ins = None
                for c in range(8):
                    ins = e.matmul(py[:], lhsT=OT[:, c, :], rhs=self.wo[:, c, h * 512:(h + 1) * 512], start=(c == 0), stop=(c == 7))
                return ins
            P.op("tensor", mm, reads=[n_OT, self.n_wo], writes=[n_py])
            halves.append(py[:])
            names.append(n_py)
        self.post.run(halves, names, tt, self.M, 1, self.xsrc, self.xdst, is_out=is_out)


def load_w_bf16(kb, st, name, dram_ap_pkn, ncols, piece=512):
    P = kb.P
    w, n_w = kb.sb(st, name, [128, 8, ncols], BF16)
    for c0 in range(0, ncols, piece):
        n = min(piece, ncols - c0)
        P.op("gpsimd", lambda e, c0=c0, n=n: e.dma_start(out=w[:, :, c0:c0 + n], in_=dram_ap_pkn[:, :, c0:c0 + n]),
             writes=[n_w + "_%d" % (c0 // piece)], dma=True)
    return w, n_w


def stage_mixer_d(kb, st, T, M, xsrc, xdst):
    nc, P = kb.nc, kb.P
    H, HKV, DH = 8, 2, 128
    QT, n_QT = kb.sb(st, "QT", [128, H, S], BF16)
    KT, n_KT = kb.sb(st, "KT", [128, HKV, NTOK], BF16)
    VA, n_VA = kb.sb(st, "VA", [128, 18, HKV, DH + 2], BF16)
    P.op("gpsimd", lambda e: e.memset(VA[:, :, :, DH:DH + 1], 1.0), writes=[n_VA])
    with ExitStack() as st2:
        HT, n_HT = kb.sb(st2, "HT", [128, 8, NTOK], BF16)
        with ExitStack() as st3:
            stage_norm_T(kb, st3, xsrc, list(range(18)), M, 1, HT, n_HT)
            P.barrier()
            P.flush()
        wq, n_wq = load_w_bf16(kb, st2, "wqkv", T["ga_w_qkv_p"].rearrange("(kc p) n -> p kc n", p=128), 1536)
        gain, n_gain = kb.sb(st2, "gain", [128, 10, 128], F32)
        cos, n_cos = kb.sb(st2, "cos", [128, 16, 64], F32)
        sin, n_sin = kb.sb(st2, "sin", [128, 16, 64], F32)
        P.op("sync", lambda e: e.dma_start(out=gain[:].rearrange("p a b -> p (a b)"),
                                           in_=T["ga_gain"][:].partition_broadcast(128)), writes=[n_gain], dma=True)
        P.op("sync", lambda e: e.dma_start(out=cos[:], in_=T["ga_cos"].rearrange("(t p) d -> p t d", p=128)), writes=[n_cos], dma=True)
        P.op("sync", lambda e: e.dma_start(out=sin[:], in_=T["ga_sin"].rearrange("(t p) d -> p t d", p=128)), writes=[n_sin], dma=True)
        ppr = kb.ring(st2, "pproj", [128, 512], F32, 3, psum=True)
        ptq = kb.ring(st2, "ptq", [128, 1024], BF16, 2, psum=True)
        qfr = kb.ring(st2, "qf", [128, 10, 128], F32, 2)
        sqr = kb.ring(st2, "qsq", [128, 10, 128], F32, 1)
        ssr = kb.ring(st2, "qss", [128, 16], F32, 2)
        rar = kb.ring(st2, "ra", [128, 10, 64], F32, 2)
        rbr = kb.ring(st2, "rb", [128, 10, 64], F32, 2)
        qrr = kb.ring(st2, "qr", [128, 10, 128], BF16, 2)
        import os
        for tt in range(18 if int(os.environ.get("MIX_CUT", "99")) >= 0 else 0):
            lat = tt < 16
            (qf, n_qf) = qfr.next()
            (sq, n_sq) = sqr.next()
            (ss, n_ss) = ssr.next()
            (qr, n_qr) = qrr.next()
            pieces = ([(0, 0), (512, 4)] if lat else []) + [(1024, 8)]
            for (c0, h0) in pieces:
                (pp, n_pp) = ppr.next()

                def mm(e, pp=pp, c0=c0, tt=tt):
                    ins = None
                    for kc in range(8):
                        ins = e.matmul(pp[:], lhsT=HT[:, kc, tt * 128:(tt + 1) * 128], rhs=wq[:, kc, c0:c0 + 512],
                                       start=(kc == 0), stop=(kc == 7))
                    return ins
                P.op("tensor", mm, reads=[n_HT, n_wq + "_%d" % (c0 // 512)], writes=[n_pp])
                if c0 < 1024:
                    P.op("scalar", lambda e, pp=pp, qf=qf, h0=h0: e.activation(
                        out=qf[:, h0:h0 + 4, :].rearrange("p a b -> p (a b)"), in_=pp[:], func=AF.Identity),
                        reads=[n_pp], writes=[n_qf])
                else:
                    P.op("scalar", lambda e, pp=pp, qf=qf: e.activation(
                        out=qf[:, 8:10, :].rearrange("p a b -> p (a b)"), in_=pp[:, 0:256], func=AF.Identity),
                        reads=[n_pp], writes=[n_qf])
                    P.op("vector", lambda e, pp=pp, tt=tt: e.tensor_copy(
                        out=VA[:, tt, :, 0:DH], in_=pp[:, 256:512].rearrange("p (a b) -> p a b", b=DH)),
                        reads=[n_pp], writes=[n_VA])
            h_lo = 0 if lat else 8
            nh_ = 10 - h_lo
            import os
            CUT = int(os.environ.get("MIX_CUT", "99"))
            if CUT < 1:
                continue
            P.op("vector", lambda e, qf=qf, sq=sq, h_lo=h_lo: e.tensor_tensor(out=sq[:, h_lo:10, :], in0=qf[:, h_lo:10, :],
                                                                            in1=qf[:, h_lo:10, :], op=ALU.mult),
                 reads=[n_qf], writes=[n_sq])
            P.op("vector", lambda e, sq=sq, ss=ss, h_lo=h_lo: e.tensor_reduce(out=ss[:, h_lo:10], in_=sq[:, h_lo:10, :],
                                                                            axis=AX.X, op=ALU.add),
                 reads=[n_sq], writes=[n_ss])
            P.op("vector", lambda e, ss=ss: e.tensor_scalar(out=ss[:, 0:10], in0=ss[:, 0:10], scalar1=1.0 / DH, scalar2=EPS,
                                                            op0=ALU.mult, op1=ALU.add), reads=[n_ss], writes=[n_ss])
            P.op("gpsimd", lambda e, ss=ss: e.tensor_tensor(out=ss[:, 0:10], in0=ss[:, 0:10], in1=kb.nh2[:, 0:10], op=ALU.pow),
                 reads=[n_ss, kb.n_nh2], writes=[n_ss])
            P.op("vector", lambda e, qf=qf, ss=ss, h_lo=h_lo, nh_=nh_: e.tensor_tensor(
                out=qf[:, h_lo:10, :], in0=qf[:, h_lo:10, :], in1=ss[:, h_lo:10].unsqueeze(2).to_broadcast([128, nh_, 128]),
                op=ALU.mult), reads=[n_qf, n_ss], writes=[n_qf])
            if CUT < 2:
                continue
            if lat:
                P.op("gpsimd", lambda e, qf=qf: e.tensor_tensor(out=qf[:], in0=qf[:], in1=gain[:], op=ALU.mult),
                     reads=[n_qf, n_gain], writes=[n_qf])
                (ra, n_ra) = rar.next()
                (rb, n_rb) = rbr.next()
                cb = cos[:, tt, :].unsqueeze(1).to_broadcast([128, 10, 64])
                sb_ = sin[:, tt, :].unsqueeze(1).to_broadcast([128, 10, 64])
                x1 = qf[:, :, 0:64]
                x2 = qf[:, :, 64:128]
                P.op("vector", lambda e, ra=ra, x1=x1, cb=cb: e.tensor_tensor(out=ra[:], in0=x1, in1=cb, op=ALU.mult),
                     reads=[n_qf, n_cos], writes=[n_ra])
                P.op("gpsimd", lambda e, rb=rb, x2=x2, sb_=sb_: e.tensor_tensor(out=rb[:], in0=x2, in1=sb_, op=ALU.mult),
                     reads=[n_qf, n_sin], writes=[n_rb])
                P.op("vector", lambda e, qr=qr, ra=ra, rb=rb: e.tensor_tensor(out=qr[:, :, 0:64], in0=ra[:], in1=rb[:], op=ALU.subtract),
                     reads=[n_ra, n_rb], writes=[n_qr])
                (ra2, n_ra2) = rar.next()
                (rb2, n_rb2) = rbr.next()
                P.op("vector", lambda e, ra2=ra2, x1=x1, sb_=sb_: e.tensor_tensor(out=ra2[:], in0=x1, in1=sb_, op=ALU.mult),
                     reads=[n_qf, n_sin], writes=[n_ra2])
                P.op("gpsimd", lambda e, rb2=rb2, x2=x2, cb=cb: e.tensor_tensor(out=rb2[:], in0=x2, in1=cb, op=ALU.mult),
                     reads=[n_qf, n_cos], writes=[n_rb2])
                P.op("vector", lambda e, qr=qr, ra2=ra2, rb2=rb2: e.tensor_tensor(out=qr[:, :, 64:128], in0=ra2[:], in1=rb2[:], op=ALU.add),
                     reads=[n_ra2, n_rb2], writes=[n_qr])
            else:
                P.op("gpsimd", lambda e, qf=qf, qr=qr: e.tensor_tensor(out=qr[:, 8:10, :], in0=qf[:, 8:10, :], in1=gain[:, 8:10, :],
                                                                     op=ALU.mult), reads=[n_qf, n_gain], writes=[n_qr])
            if CUT < 3:
                continue
            groups = ([(0, 8, "q")] if lat else []) + [(8, 2, "k")]
            for (h0, n, kind) in groups:
                (pt, n_pt) = ptq.next()

                def tr(e, pt=pt, qr=qr, h0=h0, n=n):
                    ins = None
                    for j in range(n):
                        ins = e.transpose(out=pt[:, j * 128:(j + 1) * 128], in_=qr[:, h0 + j, :], identity=kb.ident[:])
                    return ins
                P.op("tensor", tr, reads=[n_qr, kb.n_ident], writes=[n_pt])
                if kind == "q":
                    P.op("scalar", lambda e, pt=pt, tt=tt: e.activation(
                        out=QT[:, :, tt * 128:(tt + 1) * 128], in_=pt[:].rearrange("p (a b) -> p a b", b=128), func=AF.Identity),
                        reads=[n_pt], writes=[n_QT])
                else:
                    P.op("vector", lambda e, pt=pt, tt=tt: e.tensor_copy(
                        out=KT[:, :, tt * 128:(tt + 1) * 128], in_=pt[:, 0:256].rearrange("p (a b) -> p a b", b=128)),
                        reads=[n_pt], writes=[n_KT])
        P.barrier()
        P.flush()
    import os
    if os.environ.get("MIX_STOP") == "1":
        return
    with ExitStack() as st2:
        core = AttnCore(kb, st2)
        op_ = OutProj(kb, st2, T, T["ga_w_o"][0], M, xsrc, xdst)
        ocr = kb.ring(st2, "Ocat", [128, 4, 1024], BF16, 2)
        por = kb.ring(st2, "pO", [128, 512], F32, 2, psum=True)
        rcr = kb.ring(st2, "rc", [128, 4], F32, 4)
        scale = DH ** -0.5
        for mq in range(4):
            (oc, n_oc) = ocr.next()
            for j in range(4):
                q0 = (mq * 4 + j) * 128
                for h in range(H):
                    g = h // (H // HKV)
                    (po, n_po) = por.next()
                    for k0 in range(0, 18, 4):
                        blocks = []
                        for kt in range(k0, min(k0 + 4, 18)):
                            blocks.append(dict(kT=KT[:, g, kt * 128:(kt + 1) * 128], qT=QT[:, h, q0:q0 + 128], rd=[n_KT, n_QT],
                                               v=VA[:, kt, g, 0:DH + 1], rdv=[n_VA], po=po[:, 0:DH + 1], n_po=n_po,
                                               start=(kt == 0), stop=(kt == 17)))
                        core.bank(blocks, scale)
                    (rc, n_rc) = rcr.next()
                    P.op("vector", lambda e, rc=rc, po=po: e.reciprocal(out=rc[:, 0:1], in_=po[:, DH:DH + 1]),
                         reads=[n_po], writes=[n_rc])
                    P.op("vector", lambda e, oc=oc, po=po, rc=rc, h=h, j=j: e.tensor_scalar(
                        out=oc[:, j, h * DH:(h + 1) * DH], in0=po[:, 0:DH], scalar1=rc[:, 0:1], scalar2=None, op0=ALU.mult),
                        reads=[n_po, n_rc], writes=[n_oc + "_%d" % j])
            for j in range(4):
                op_.run(oc[:, j, :], n_oc + "_%d" % j, mq * 4 + j)
        P.barrier()
        P.flush()


def normalize_head(kb, po, n_po, dv, oc_ap, n_oc, rcr, extra=None):
    P = kb.P
    (rc, n_rc) = rcr.next()
    if extra is not None:
        ex_ap, n_ex = extra
        P.op("vector", lambda e: e.tensor_tensor(out=rc[:, 1:2], in0=po[:, dv:dv + 1], in1=ex_ap, op=ALU.add),
             reads=[n_po, n_ex], writes=[n_rc])
        P.op("vector", lambda e: e.reciprocal(out=rc[:, 0:1], in_=rc[:, 1:2]), reads=[n_rc], writes=[n_rc])
    else:
        P.op("vector", lambda e: e.reciprocal(out=rc[:, 0:1], in_=po[:, dv:dv + 1]), reads=[n_po], writes=[n_rc])
    P.op("vector", lambda e: e.tensor_scalar(out=oc_ap, in0=po[:, 0:dv], scalar1=rc[:, 0:1], scalar2=None, op0=ALU.mult),
         reads=[n_po, n_rc], writes=[n_oc])


def stage_mixer_b(kb, st, T, M, xsrc, xdst):
    nc, P = kb.nc, kb.P
    H, HKV, DH = 16, 2, 64
    QT, n_QT = kb.sb(st, "QT2", [128, 8, NTOK], BF16)
    KT, n_KT = kb.sb(st, "KT2", [128, HKV, NTOK], BF16)
    VA, n_VA = kb.sb(st, "VA", [128, 18, HKV, DH + 2], BF16)
    P.op("gpsimd", lambda e: e.memset(VA[:, :, :, DH:DH + 1], 1.0), writes=[n_VA])
    with ExitStack() as st2:
        HT, n_HT = kb.sb(st2, "HT", [128, 8, NTOK], BF16)
        with ExitStack() as st3:
            stage_norm_T(kb, st3, xsrc, list(range(18)), M, 1, HT, n_HT)
            P.barrier()
            P.flush()
        wq, n_wq = load_w_bf16(kb, st2, "wqkv", T["sw_w_qkv_p"].rearrange("(kc p) n -> p kc n", p=128), 1280, piece=256)
        cos, n_cos = kb.sb(st2, "cos", [128, 16, 32], F32)
        sin, n_sin = kb.sb(st2, "sin", [128, 16, 32], F32)
        P.op("sync", lambda e: e.dma_start(out=cos[:], in_=T["sw_cos"].rearrange("(t p) d -> p t d", p=128)), writes=[n_cos], dma=True)
        P.op("sync", lambda e: e.dma_start(out=sin[:], in_=T["sw_sin"].rearrange("(t p) d -> p t d", p=128)), writes=[n_sin], dma=True)
        ppr = kb.ring(st2, "pproj", [128, 512], F32, 3, psum=True)
        ptq = kb.ring(st2, "ptq", [128, 1024], BF16, 2, psum=True)
        qfr = kb.ring(st2, "qf", [128, 18, 64], F32, 2)
        rar = kb.ring(st2, "ra", [128, 18, 32], F32, 2)
        rbr = kb.ring(st2, "rb", [128, 18, 32], F32, 2)
        qrr = kb.ring(st2, "qr", [128, 18, 64], BF16, 2)
        kdr = kb.ring(st2, "kd", [128, 2, 2, 64], BF16, 2)
        for tt in range(18):
            lat = tt < 16
            (qf, n_qf) = qfr.next()
            (qr, n_qr) = qrr.next()
            (kd, n_kd) = kdr.next()
            for (c0, ncol, h0) in [(0, 512, 0), (512, 512, 8), (1024, 256, 16)]:
                (pp, n_pp) = ppr.next()

                def mm(e, pp=pp, c0=c0, ncol=ncol, tt=tt):
                    ins = None
                    for kc in range(8):
                        ins = e.matmul(pp[:, 0:ncol], lhsT=HT[:, kc, tt * 128:(tt + 1) * 128], rhs=wq[:, kc, c0:c0 + ncol],
                                       start=(kc == 0), stop=(kc == 7))
                    return ins
                P.op("tensor", mm, reads=[n_HT] + [n_wq + "_%d" % i for i in range(c0 // 256, (c0 + ncol) // 256)], writes=[n_pp])
                if c0 < 1024:
                    P.op("scalar", lambda e, pp=pp, qf=qf, h0=h0: e.activation(
                        out=qf[:, h0:h0 + 8, :].rearrange("p a b -> p (a b)"), in_=pp[:], func=AF.Identity),
                        reads=[n_pp], writes=[n_qf])
                else:
                    P.op("scalar", lambda e, pp=pp, qf=qf: e.activation(
                        out=qf[:, 16:18, :].rearrange("p a b -> p (a b)"), in_=pp[:, 0:128], func=AF.Identity),
                        reads=[n_pp], writes=[n_qf])
                    P.op("scalar", lambda e, pp=pp, tt=tt: e.activation(
                        out=VA[:, tt, :, 0:DH], in_=pp[:, 128:256].rearrange("p (a b) -> p a b", b=DH), func=AF.Identity),
                        reads=[n_pp], writes=[n_VA])
            if lat:
                (ra, n_ra) = rar.next()
                (rb, n_rb) = rbr.next()
                cb = cos[:, tt, :].unsqueeze(1).to_broadcast([128, 18, 32])
                sb_ = sin[:, tt, :].unsqueeze(1).to_broadcast([128, 18, 32])
                x1 = qf[:, :, 0:32]
                x2 = qf[:, :, 32:64]
                P.op("vector", lambda e, ra=ra, x1=x1, cb=cb: e.tensor_tensor(out=ra[:], in0=x1, in1=cb, op=ALU.mult),
                     reads=[n_qf, n_cos], writes=[n_ra])
                P.op("gpsimd", lambda e, rb=rb, x2=x2, sb_=sb_: e.tensor_tensor(out=rb[:], in0=x2, in1=sb_, op=ALU.mult),
                     reads=[n_qf, n_sin], writes=[n_rb])
                P.op("vector", lambda e, qr=qr, ra=ra, rb=rb: e.tensor_tensor(out=qr[:, :, 0:32], in0=ra[:], in1=rb[:], op=ALU.subtract),
                     reads=[n_ra, n_rb], writes=[n_qr])
                (ra2, n_ra2) = rar.next()
                (rb2, n_rb2) = rbr.next()
                P.op("vector", lambda e, ra2=ra2, x1=x1, sb_=sb_: e.tensor_tensor(out=ra2[:], in0=x1, in1=sb_, op=ALU.mult),
                     reads=[n_qf, n_sin], writes=[n_ra2])
                P.op("gpsimd", lambda e, rb2=rb2, x2=x2, cb=cb: e.tensor_tensor(out=rb2[:], in0=x2, in1=cb, op=ALU.mult),
                     reads=[n_qf, n_cos], writes=[n_rb2])
                P.op("vector", lambda e, qr=qr, ra2=ra2, rb2=rb2: e.tensor_tensor(out=qr[:, :, 32:64], in0=ra2[:], in1=rb2[:], op=ALU.add),
                     reads=[n_ra2, n_rb2], writes=[n_qr])
            else:
                P.op("vector", lambda e, qr=qr, qf=qf: e.tensor_copy(out=qr[:], in_=qf[:]), reads=[n_qf], writes=[n_qr])
            for dup in range(2):
                P.op("gpsimd", lambda e, kd=kd, qr=qr, dup=dup: e.tensor_copy(out=kd[:, :, dup, :], in_=qr[:, 16:18, :]),
                     reads=[n_qr], writes=[n_kd])
            (pt, n_pt) = ptq.next()

            def tr(e, pt=pt, qr=qr):
                ins = None
                for p_ in range(8):
                    ins = e.transpose(out=pt[:, p_ * 128:(p_ + 1) * 128],
                                      in_=qr[:, 2 * p_:2 * p_ + 2, :].rearrange("p a b -> p (a b)"), identity=kb.ident[:])
                return ins
            P.op("tensor", tr, reads=[n_qr, kb.n_ident], writes=[n_pt])
            P.op("scalar", lambda e, pt=pt, tt=tt: e.activation(
                out=QT[:, :, tt * 128:(tt + 1) * 128], in_=pt[:].rearrange("p (a b) -> p a b", b=128), func=AF.Identity),
                reads=[n_pt], writes=[n_QT])
            (pt2, n_pt2) = ptq.next()

            def tr2(e, pt2=pt2, kd=kd):
                ins = None
                for g in range(2):
                    ins = e.transpose(out=pt2[:, g * 128:(g + 1) * 128], in_=kd[:, g, :, :].rearrange("p a b -> p (a b)"),
                                      identity=kb.ident[:])
                return ins
            P.op("tensor", tr2, reads=[n_kd, kb.n_ident], writes=[n_pt2])
            P.op("vector", lambda e, pt2=pt2, tt=tt: e.tensor_copy(
                out=KT[:, :, tt * 128:(tt + 1) * 128], in_=pt2[:, 0:256].rearrange("p (a b) -> p a b", b=128)),
                reads=[n_pt2], writes=[n_KT])
        P.barrier()
        P.flush()
    with ExitStack() as st2:
        core = AttnCore(kb, st2)
        op_ = OutProj(kb, st2, T, T["sw_w_o"][0], M, xsrc, xdst)
        ocr = kb.ring(st2, "Ocat", [128, 1024], BF16, 2)
        por = kb.ring(st2, "pO", [128, 512], F32, 2, psum=True)
        rcr = kb.ring(st2, "rc", [128, 4], F32, 4)
        esk, n_esk = kb.sb(st2, "esink", [128, 16], F32)
        P.op("sync", lambda e: e.dma_start(out=esk[:], in_=T["sw_sink"][0, :].partition_broadcast(128)), writes=[n_esk], dma=True)
        P.op("scalar", lambda e: e.activation(out=esk[:], in_=esk[:], func=AF.Exp), reads=[n_esk], writes=[n_esk])
        scale = DH ** -0.5
        for tt in range(18):
            (oc, n_oc) = ocr.next()
            if tt < 16:
                kl = []
                if tt - 1 >= 0:
                    kl.append((tt - 1, kb.masklo[:], kb.n_masklo))
                kl.append((tt, None, None))
                if tt + 1 <= 15:
                    kl.append((tt + 1, kb.maskhi[:], kb.n_maskhi))
                kl += [(16, None, None), (17, None, None)]
            else:
                kl = [(16, None, None), (17, None, None)]
            for h in range(H):
                g = h // (H // HKV)
                b0 = (h % 2) * 64
                pr = h // 2
                (po, n_po) = por.next()
                for k0 in range(0, len(kl), 4):
                    blocks = []
                    for idx in range(k0, min(k0 + 4, len(kl))):
                        kt, bias, n_bias = kl[idx]
                        blocks.append(dict(kT=KT[b0:b0 + 64, g, kt * 128:(kt + 1) * 128], qT=QT[b0:b0 + 64, pr, tt * 128:(tt + 1) * 128],
                                           bias=bias, rd=[n_KT, n_QT] + ([n_bias] if n_bias else []),
                                           v=VA[:, kt, g, 0:DH + 1], rdv=[n_VA], po=po[:, 0:DH + 1], n_po=n_po,
                                           start=(idx == 0), stop=(idx == len(kl) - 1)))
                    core.bank(blocks, scale)
                normalize_head(kb, po, n_po, DH, oc[:, h * DH:(h + 1) * DH], n_oc, rcr, extra=(esk[:, h:h + 1], n_esk))
            op_.run(oc[:], n_oc, tt)
        P.barrier()
        P.flush()


NEG_BIAS = -30000.0


def na_structure():
    rows = S // GRID_W
    p = np.arange(128)
    combos = []
    keyl = []
    sig = {}
    for m in range(16):
        r = 2 * m + p // 64
        j = p % 64
        rs = np.clip(r - 4, 0, rows - 8)
        ws = np.clip(j - 8, 0, GRID_W - 16)
        lst = []
        for kt in range(16):
            kr = 2 * kt + p // 64
            kc = p % 64
            valid = ((kr[:, None] >= rs[None, :]) & (kr[:, None] < rs[None, :] + 8)
                     & (kc[:, None] >= ws[None, :]) & (kc[:, None] < ws[None, :] + 16))
            if not valid.any():
                continue
            ridx = np.clip(kr[:, None] - r[None, :] + 7, 0, 14)
            cidx = np.clip(kc[:, None] - j[None, :] + 15, 0, 30)
            ridx = np.where(valid, ridx, 0)
            cidx = np.where(valid, cidx, 0)
            key = (ridx.tobytes(), cidx.tobytes(), valid.tobytes())
            if key not in sig:
                sig[key] = len(combos)
                combos.append((ridx, cidx, valid))
            lst.append((kt, sig[key]))
        keyl.append(lst)
    return keyl, combos


def na_bias_host(rpb):
    keyl, combos = na_structure()
    rpb = np.asarray(rpb, dtype=np.float32)[0]
    out = np.empty((16, 128, len(combos), 128), dtype=np.float32)
    for ci, (ridx, cidx, valid) in enumerate(combos):
        g = rpb[:, ridx, cidx]
        out[:, :, ci, :] = np.where(valid[None], g, np.float32(NEG_BIAS))
    return out


def stage_mixer_a(kb, st, T, M, xsrc, xdst):
    nc, P = kb.nc, kb.P
    H, DH = 16, 64
    keyl, combos = na_structure()
    NCMB = len(combos)
    BIG, n_BIG = kb.sb(st, "HT_OC", [128, 8 * NTOK], BF16)
    HT, n_HT = BIG[:].rearrange("p (c t) -> p c t", c=8), n_BIG + "_ht"
    OC, n_OC = BIG[:].rearrange("p (t d) -> p t d", t=18), n_BIG + "_oc"
    stA = ExitStack()
    QT, n_QT = kb.sb(stA, "QT2", [128, 8, NTOK], BF16)
    KT, n_KT = kb.sb(stA, "KT2", [128, 8, NTOK], BF16)
    VA, n_VA = kb.sb(stA, "VA", [128, 18, H, DH + 2], BF16)
    P.op("gpsimd", lambda e: e.memset(VA[:, :, :, DH:DH + 1], 1.0), writes=[n_VA])
    scale = DH ** -0.5
    with ExitStack() as st2:
        with ExitStack() as st3:
            stage_norm_T(kb, st3, xsrc, list(range(18)), M, 1, HT, n_HT, rings=(4, 1, 4))
            P.barrier()
            P.flush()
        wr = kb.ring(st2, "wqkv", [128, 8, 512], BF16, 2)
        ppr = kb.ring(st2, "pproj", [128, 512], F32, 4, psum=True)
        wd = T["na_w_qkv"][0].rearrange("(kc p) n -> p kc n", p=128)
        macros = tok_macros(0, NTOK)
        ev = 0
        for piece in range(6):
            (w, n_w) = wr.next()
            P.op("gpsimd", lambda e, w=w, piece=piece: e.dma_start(out=w[:], in_=wd[:, :, piece * 512:(piece + 1) * 512]),
                 writes=[n_w], dma=True)
            if piece < 4:
                dst, n_dst = (QT, n_QT) if piece < 2 else (KT, n_KT)
                for pl in range(4):
                    pr = (piece % 2) * 4 + pl
                    for (t0, n) in macros:
                        (pp, n_pp) = ppr.next()

                        def mm(e, pp=pp, w=w, pl=pl, t0=t0, n=n):
                            ins = None
                            for kc in range(8):
                                ins = e.matmul(pp[:, 0:n], lhsT=w[:, kc, pl * 128:(pl + 1) * 128], rhs=HT[:, kc, t0:t0 + n],
                                               start=(kc == 0), stop=(kc == 7))
                            return ins
                        P.op("tensor", mm, reads=[n_w, n_HT], writes=[n_pp])
                        if piece < 2:
                            P.op("scalar", lambda e, pp=pp, pr=pr, t0=t0, n=n: e.activation(
                                out=QT[:, pr, t0:t0 + n], in_=pp[:, 0:n], func=AF.Identity, scale=scale),
                                reads=[n_pp], writes=[n_QT])
                        else:
                            P.op("vector", lambda e, pp=pp, pr=pr, t0=t0, n=n: e.tensor_copy(out=KT[:, pr, t0:t0 + n], in_=pp[:, 0:n]),
                                 reads=[n_pp], writes=[n_KT])
            else:
                h0 = (piece - 4) * 8
                for tt in range(18):
                    (pp, n_pp) = ppr.next()

                    def mm(e, pp=pp, w=w, tt=tt):
                        ins = None
                        for kc in range(8):
                            ins = e.matmul(pp[:], lhsT=HT[:, kc, tt * 128:(tt + 1) * 128], rhs=w[:, kc, :], start=(kc == 0), stop=(kc == 7))
                        return ins
                    P.op("tensor", mm, reads=[n_w, n_HT], writes=[n_pp])
                    if ev % 2 == 0:
                        P.op("scalar", lambda e, pp=pp, tt=tt, h0=h0: e.activation(
                            out=VA[:, tt, h0:h0 + 8, 0:DH], in_=pp[:].rearrange("p (a b) -> p a b", b=DH), func=AF.Identity),
                            reads=[n_pp], writes=[n_VA])
                    else:
                        P.op("vector", lambda e, pp=pp, tt=tt, h0=h0: e.tensor_copy(
                            out=VA[:, tt, h0:h0 + 8, 0:DH], in_=pp[:].rearrange("p (a b) -> p a b", b=DH)),
                            reads=[n_pp], writes=[n_VA])
                    ev += 1
        P.barrier()
        P.flush()
    with ExitStack() as st2:
        core = AttnCore(kb, st2, n_ps=4, n_pt=4)
        por = kb.ring(st2, "pO", [128, 512], F32, 2, psum=True)
        rcr = kb.ring(st2, "rc", [128, 4], F32, 4)
        br = kb.ring(st2, "nabias", [128, NCMB, 128], BF16, 2)
        for h in range(H):
            (bt, n_bt) = br.next()
            P.op("gpsimd", lambda e, bt=bt, h=h: e.dma_start(out=bt[:], in_=T["na_bias"][h]), writes=[n_bt], dma=True)
            b0 = (h % 2) * 64
            pr = h // 2
            for tt in range(18):
                if tt < 16:
                    kl = [(kt, bt[:, ci, :], n_bt) for (kt, ci) in keyl[tt]] + [(16, None, None), (17, None, None)]
                else:
                    kl = [(16, None, None), (17, None, None)]
                (po, n_po) = por.next()
                for k0 in range(0, len(kl), 4):
                    blocks = []
                    for idx in range(k0, min(k0 + 4, len(kl))):
                        kt, bias, n_bias = kl[idx]
                        blocks.append(dict(kT=KT[b0:b0 + 64, pr, kt * 128:(kt + 1) * 128], qT=QT[b0:b0 + 64, pr, tt * 128:(tt + 1) * 128],
                                           bias=bias, rd=[n_KT, n_QT] + ([n_bias] if n_bias else []),
                                           v=VA[:, kt, h, 0:DH + 1], rdv=[n_VA], po=po[:, 0:DH + 1], n_po=n_po,
                                           start=(idx == 0), stop=(idx == len(kl) - 1)))
                    core.bank(blocks, 1.0)
                normalize_head(kb, po, n_po, DH, OC[:, tt, h * DH:(h + 1) * DH], n_OC + "_%d" % tt, rcr)
        P.barrier()
        P.flush()
    stA.close()
    with ExitStack() as st2:
        op_ = OutProj(kb, st2, T, T["na_w_o"][0], M, xsrc, xdst, post_bufs=2)
        for tt in range(18):
            op_.run(OC[:, tt, :], n_OC + "_%d" % tt, tt)
        P.barrier()
        P.flush()


def stage_mixer_c(kb, st, T, M, xsrc, xdst):
    nc, P = kb.nc, kb.P
    CW = 31
    HW_ = CW // 2
    HT, n_HT = kb.sb(st, "HT", [128, 8, NTOK], BF16)
    with ExitStack() as st3:
        stage_norm_T(kb, st3, xsrc, list(range(18)), M, 1, HT, n_HT, rings=(4, 1, 4))
        P.barrier()
        P.flush()
    w1, n_w1 = load_w_bf16(kb, st, "w1", T["cv_w_pw1"][0].rearrange("(kc p) n -> p kc n", p=128), 2048)
    w2, n_w2 = load_w_bf16(kb, st, "w2", T["cv_w_pw2"][0].rearrange("(kc p) n -> p kc n", p=128), 1024)
    cvf, n_cvf = kb.sb(st, "cvf", [128, 8, 36], F32)
    b2, n_b2 = kb.sb(st, "b2", [128, 1024], F32)
    onesm, n_ones = kb.sb(st, "onesm", [128, 128], F32)
    U, n_U = kb.sb(st, "U", [128, 8, 512 + 2 * HW_], F32)
    V, n_V = kb.sb(st, "V", [128, 8, 512], F32)
    Z, n_Z = kb.sb(st, "Z", [128, 8, 512], BF16)
    sgr = kb.ring(st, "sig", [128, 512 + 2 * HW_], F32, 2)
    msq, n_msq = kb.sb(st, "msq", [128, 512], F32)
    rstd, n_rstd = kb.sb(st, "rstd", [128, 512], F32)
    ysr = kb.ring(st, "Ysb", [128, 1024], F32, 2)
    pA, n_pA = kb.ps(st, "pA", [128, 1024], F32)
    pB, n_pB = kb.ps(st, "pB", [128, 1024], F32)
    pM, n_pM = kb.ps(st, "pM", [128, 512], F32)
    pQ, n_pQ = kb.ps(st, "pQ", [128, 512], F32)
    pyr = kb.ring(st, "pY", [128, 512], F32, 2, psum=True)
    post = PostStage(kb, st, bufs=1)
    P.op("sync", lambda e: e.dma_start(out=cvf[:], in_=T["cv_fm"][:, :, :]), writes=[n_cvf], dma=True)
    P.op("sync", lambda e: e.dma_start(out=b2[:], in_=T["cv_b_pw2"][0, :].partition_broadcast(128)), writes=[n_b2], dma=True)
    P.op("gpsimd", lambda e: e.memset(onesm[:], 1.0 / D), writes=[n_ones])
    segs = [(t0, 512, 0, S) for t0 in range(0, S, 512)] + [(S, C, S, S + C)]
    for (t0, n, seq0, seq1) in segs:
        lo = max(t0 - HW_, seq0)
        hi = min(t0 + n + HW_, seq1)
        w = hi - lo
        off = lo - (t0 - HW_)
        if off > 0 or off + w < n + 2 * HW_:
            P.op("gpsimd", lambda e, n=n: e.memset(U[:, :, 0:n + 2 * HW_], 0.0), writes=[n_U])
        pieces = [(0, min(w, 512))] + ([(512, w - 512)] if w > 512 else [])
        for c in range(8):
            (sg, n_sg) = sgr.next()
            for (pp, n_pp, cbase) in ((pA, n_pA, 0), (pB, n_pB, 1024)):
                def mm(e, pp=pp, cbase=cbase, c=c, lo=lo, pieces=pieces):
                    ins = None
                    for (a, ln) in pieces:
                        for kc in range(8):
                            ins = e.matmul(pp[:, a:a + ln], lhsT=w1[:, kc, cbase + c * 128:cbase + (c + 1) * 128],
                                           rhs=HT[:, kc, lo + a:lo + a + ln], start=(kc == 0), stop=(kc == 7))
                    return ins
                P.op("tensor", mm, reads=[n_HT, n_w1 + "_%d" % ((cbase + c * 128) // 512)], writes=[n_pp])
            P.op("scalar", lambda e, sg=sg, c=c, w=w: e.activation(out=sg[:, 0:w], in_=pB[:, 0:w], func=AF.Sigmoid,
                                                                 bias=cvf[:, c, 1:2], scale=1.0),
                 reads=[n_pB, n_cvf], writes=[n_sg])
            P.op("vector", lambda e, sg=sg, c=c, w=w, off=off: e.scalar_tensor_tensor(
                out=U[:, c, off:off + w], in0=pA[:, 0:w], scalar=cvf[:, c, 0:1], in1=sg[:, 0:w], op0=ALU.add, op1=ALU.mult),
                reads=[n_pA, n_cvf, n_sg], writes=[n_U])
            P.op("vector", lambda e, c=c, n=n: e.tensor_scalar(out=V[:, c, 0:n], in0=U[:, c, 0:n], scalar1=cvf[:, c, 5:6],
                                                               scalar2=cvf[:, c, 2:3], op0=ALU.mult, op1=ALU.add),
                 reads=[n_U, n_cvf], writes=[n_V])
            for j in range(1, CW):
                P.op("vector", lambda e, c=c, n=n, j=j: e.scalar_tensor_tensor(
                    out=V[:, c, 0:n], in0=U[:, c, j:j + n], scalar=cvf[:, c, 5 + j:6 + j], in1=V[:, c, 0:n],
                    op0=ALU.mult, op1=ALU.add), reads=[n_U, n_cvf, n_V], writes=[n_V])
        def mmM(e, n=n):
            ins = None
            for c in range(8):
                ins = e.matmul(pM[:, 0:n], lhsT=onesm[:], rhs=V[:, c, 0:n], start=(c == 0), stop=(c == 7))
            return ins
        P.op("tensor", mmM, reads=[n_ones, n_V], writes=[n_pM])
        for c in range(8):
            P.op("scalar", lambda e, c=c, n=n: e.activation(out=U[:, c, 0:n], in_=V[:, c, 0:n], func=AF.Square),
                 reads=[n_V], writes=[n_U])

        def mmQ(e, n=n):
            ins = None
            for c in range(8):
                ins = e.matmul(pQ[:, 0:n], lhsT=onesm[:], rhs=U[:, c, 0:n], start=(c == 0), stop=(c == 7))
            return ins
        P.op("tensor", mmQ, reads=[n_ones, n_U], writes=[n_pQ])
        P.op("scalar", lambda e, n=n: e.activation(out=msq[:, 0:n], in_=pM[:, 0:n], func=AF.Square), reads=[n_pM], writes=[n_msq])
        P.op("vector", lambda e, n=n: e.tensor_tensor(out=rstd[:, 0:n], in0=pQ[:, 0:n], in1=msq[:, 0:n], op=ALU.subtract),
             reads=[n_pQ, n_msq], writes=[n_rstd])
        P.op("vector", lambda e, n=n: e.tensor_scalar(out=rstd[:, 0:n], in0=rstd[:, 0:n], scalar1=EPS, scalar2=None, op0=ALU.add),
             reads=[n_rstd], writes=[n_rstd])
        P.op("scalar", lambda e, n=n: e.activation(out=rstd[:, 0:n], in_=rstd[:, 0:n], func=AF.Sqrt), reads=[n_rstd], writes=[n_rstd])
        P.op("vector", lambda e, n=n: e.reciprocal(out=rstd[:, 0:n], in_=rstd[:, 0:n]), reads=[n_rstd], writes=[n_rstd])
        for c in range(8):
            P.op("vector", lambda e, c=c, n=n: e.tensor_tensor(out=V[:, c, 0:n], in0=V[:, c, 0:n], in1=pM[:, 0:n], op=ALU.subtract),
                 reads=[n_V, n_pM], writes=[n_V])
            P.op("gpsimd", lambda e, c=c, n=n: e.tensor_tensor(out=V[:, c, 0:n], in0=V[:, c, 0:n], in1=rstd[:, 0:n], op=ALU.mult),
                 reads=[n_V, n_rstd], writes=[n_V])
            P.op("scalar", lambda e, c=c, n=n: e.activation(out=Z[:, c, 0:n], in_=V[:, c, 0:n], func=AF.Silu,
                                                           scale=cvf[:, c, 3:4], bias=cvf[:, c, 4:5]),
                 reads=[n_V, n_cvf], writes=[n_Z])
        for jt in range(n // 128):
            (ys, n_ys) = ysr.next()
            for h in range(2):
                (py, n_py) = pyr.next()

                def mmy(e, py=py, jt=jt, h=h):
                    ins = None
                    for c in range(8):
                        ins = e.matmul(py[:], lhsT=Z[:, c, jt * 128:(jt + 1) * 128], rhs=w2[:, c, h * 512:(h + 1) * 512],
                                       start=(c == 0), stop=(c == 7))
                    return ins
                P.op("tensor", mmy, reads=[n_Z, n_w2 + "_%d" % h], writes=[n_py])
                P.op("vector", lambda e, ys=ys, py=py, h=h: e.tensor_tensor(out=ys[:, h * 512:(h + 1) * 512], in0=py[:],
                                                                           in1=b2[:, h * 512:(h + 1) * 512], op=ALU.add),
                     reads=[n_py, n_b2], writes=[n_ys + "_%d" % h])
            post.run([ys[:, 0:512], ys[:, 512:1024]], [n_ys + "_0", n_ys + "_1"], t0 // 128 + jt, M, 1, xsrc, xdst)
    P.barrier()
    P.flush()


COMMON_SPECS = {
    "mod_w": [1024, 9216], "mod_b": [9216], "norm_g": [6, 1024], "mod_b_fm": [128, 72], "norm_g_fm": [128, 48],
    "ffn_w_gate": [2, 1024, 2816], "ffn_w_up": [2, 1024, 2816], "ffn_w_down": [2, 2816, 1024],
}
MIXER_SPECS = {
    0: {"na_w_qkv": [1, 1024, 3072], "na_w_o": [1, 1024, 1024], "na_bias": [16, 128, 9, 128]},
    1: {"sw_w_qkv_p": [1024, 1280], "sw_w_o": [1, 1024, 1024], "sw_sink": [1, 16], "sw_cos": [S, 32], "sw_sin": [S, 32]},
    2: {"cv_w_pw1": [1, 1024, 2048], "cv_w_pw2": [1, 1024, 1024], "cv_fm": [128, 8, 36], "cv_b_pw2": [1, 1024]},
    3: {"ga_w_qkv_p": [1024, 1536], "ga_w_o": [1, 1024, 1024], "ga_gain": [1280], "ga_cos": [S, 64], "ga_sin": [S, 64]},
}
CORE_SPECS = {"x": [S, D], "ctx": [C, D], "cfm": [128, 8, 2]}


def shared_specs(layers):
    sp = {k: [len(layers)] + v for k, v in COMMON_SPECS.items()}
    for li in layers:
        sp.update(MIXER_SPECS[li % 4])
    return sp


def shared_for(sh, layers):
    out = {}
    for k, shp in shared_specs(layers).items():
        a = sh[k]
        if k in COMMON_SPECS:
            a = np.ascontiguousarray(a[list(layers)])
        assert list(a.shape) == shp, (k, a.shape, shp)
        out[k] = a
    return out


def host_shared(inp):
    f = lambda a: np.ascontiguousarray(np.asarray(a, dtype=np.float32))
    sh = {}
    sh["mod_w"] = f(inp["mod_w"])
    sh["mod_b"] = f(inp["mod_b"])
    sh["norm_g"] = f(inp["norm_g"])
    sh["mod_b_fm"] = f(np.asarray(inp["mod_b"]).reshape(4, 72, 128).transpose(0, 2, 1))
    sh["norm_g_fm"] = f(np.asarray(inp["norm_g"]).reshape(4, 48, 128).transpose(0, 2, 1))
    for k in ("ffn_w_gate", "ffn_w_up", "ffn_w_down"):
        sh[k] = f(inp[k])
    sh["na_w_qkv"] = f(inp["na_w_qkv"])
    sh["na_w_o"] = f(inp["na_w_o"])
    sh["na_bias"] = na_bias_host(inp["na_rpb"])
    perm64 = np.concatenate([np.arange(0, 64, 2), np.arange(1, 64, 2)])
    w = np.asarray(inp["sw_w_qkv"])[0]
    cols = np.concatenate([h * 64 + perm64 for h in range(18)] + [np.arange(1152, 1280)])
    sh["sw_w_qkv_p"] = f(w[:, cols])
    sh["sw_w_o"] = f(inp["sw_w_o"])
    sh["sw_sink"] = f(inp["sw_sink"])
    sh["sw_cos"], sh["sw_sin"] = rope_tables(64)
    sh["cv_w_pw1"] = f(inp["cv_w_pw1"])
    sh["cv_w_pw2"] = f(inp["cv_w_pw2"])
    sh["cv_b_pw2"] = f(inp["cv_b_pw2"])
    b1 = np.asarray(inp["cv_b_pw1"])[0]
    vecs = [b1[:1024], b1[1024:], np.asarray(inp["cv_b_dw"])[0], np.asarray(inp["cv_ln_g"])[0], np.asarray(inp["cv_ln_b"])[0]]
    vecs += [np.asarray(inp["cv_w_dw"])[0][j] for j in range(31)]
    sh["cv_fm"] = f(np.stack([v.reshape(8, 128).T for v in vecs], axis=-1))
    perm = np.concatenate([np.arange(0, 128, 2), np.arange(1, 128, 2)])
    w = np.asarray(inp["ga_w_qkv"])[0]
    cols = np.concatenate([h * 128 + perm for h in range(10)] + [np.arange(1280, 1536)])
    sh["ga_w_qkv_p"] = f(w[:, cols])
    sh["ga_w_o"] = f(inp["ga_w_o"])
    qn = np.asarray(inp["ga_q_norm"])[0][perm]
    kn = np.asarray(inp["ga_k_norm"])[0][perm]
    sh["ga_gain"] = f(np.concatenate([qn] * 8 + [kn] * 2))
    cs, sn = rope_tables(128)
    sh["ga_cos"], sh["ga_sin"] = cs, sn
    return sh


def rope_tables(head_dim):
    t = np.arange(S)
    row = (t // GRID_W).astype(np.float32)
    col = (t % GRID_W).astype(np.float32)
    n = head_dim // 4
    freq = np.power(np.float32(10000.0), -(np.arange(n, dtype=np.float32) / np.float32(n))).astype(np.float32)
    ang = np.concatenate([row[:, None] * freq[None, :], col[:, None] * freq[None, :]], axis=-1).astype(np.float32)
    return np.cos(ang).astype(np.float32), np.sin(ang).astype(np.float32)


def host_core(inp, b):
    f = lambda a: np.ascontiguousarray(np.asarray(a, dtype=np.float32))
    c = np.asarray(inp["c"])[b].reshape(8, 128).T
    cc = np.asarray(inp["c_ctx"]).reshape(8, 128).T
    return {"x": f(inp["x"][b]), "ctx": f(inp["ctx"][b]), "cfm": f(np.stack([c, cc], axis=-1))}


def layer_plan(li):
    last = li == 3
    return [("mod", li), ("ffn", li, 0, True), ("mixer", li), ("ffn", li, 1, not last)]


def build_program(plan, layers):
    nc = bass.Bass("TRN2", target_bir_lowering=False)
    T = {}
    for k, shp in list(shared_specs(layers).items()) + list(CORE_SPECS.items()):
        T[k] = nc.dram_tensor(k, shp, F32, kind="ExternalInput").ap()
    out = nc.dram_tensor("out", [S, D], F32, kind="ExternalOutput").ap()
    outc = nc.dram_tensor("outc", [C, D], F32, kind="ExternalOutput").ap()
    XL = nc.dram_tensor("XL", [S, D], F32, kind="Internal").ap()
    XC = nc.dram_tensor("XC", [C, D], F32, kind="Internal").ap()

    def xs(tt):
        if tt < 16:
            return XL[tt * 128:(tt + 1) * 128, :], "XL_%d" % tt
        return XC[(tt - 16) * 128:(tt - 15) * 128, :], "XC_%d" % (tt - 16)

    with ExitStack() as st:
        kb = KB(nc, st)
        P = kb.P
        stage_consts(kb, st)
        for tt in range(18):
            dst, n_dst = xs(tt)
            src = T["x"][tt * 128:(tt + 1) * 128, :] if tt < 16 else T["ctx"][(tt - 16) * 128:(tt - 15) * 128, :]
            P.op("sync", lambda e, dst=dst, src=src: e.dma_start(out=dst, in_=src), writes=[n_dst], dma=True)
        M = None
        lst = None
        for step in plan:
            lloc = list(layers).index(step[1])
            if step[0] == "mod":
                if lst is not None:
                    P.barrier()
                    P.flush()
                    lst.close()
                lst = ExitStack()
                with ExitStack() as st2:
                    M = stage_mod(kb, st2, T, lloc, lst)
                    P.barrier()
                    P.flush()
            elif step[0] == "ffn":
                which = step[2]
                tiles = list(range(18)) if step[3] else list(range(16))
                with ExitStack() as st2:
                    stage_ffn(kb, st2, T, lloc, which, tiles, M, 0 if which == 0 else 2, xs, xs)
            elif step[0] == "mixer":
                kind = step[1] % 4
                with ExitStack() as st2:
                    [stage_mixer_a, stage_mixer_b, stage_mixer_c, stage_mixer_d][kind](kb, st2, T, M, xs, xs)
            else:
                raise ValueError(step)
        for tt in range(18):
            src, n_src = xs(tt)
            dst = out[tt * 128:(tt + 1) * 128, :] if tt < 16 else outc[(tt - 16) * 128:(tt - 15) * 128, :]
            ev = P.op("sync", lambda e, dst=dst, src=src: e.dma_start(out=dst, in_=src), reads=[n_src], dma=True)
            P.out_evs.append(ev)
        waits = {}
        for ev in P.out_evs:
            P._need("sync", ev, waits)
        P.ops["sync"].append((None, list(waits.items()), None))
        P.barrier()
        P.flush()
        if lst is not None:
            lst.close()
    return nc


FUSED = True


def kernel(**inputs):
    sh = host_shared(inputs)
    cores = [host_core(inputs, b) for b in range(8)]
    groups = [[0, 1, 2, 3]] if FUSED else [[0], [1], [2], [3]]
    for layers in groups:
        plan = []
        for li in layers:
            plan += layer_plan(li)
        nc = build_program(plan, layers)
        shl = shared_for(sh, layers)
        res = run_bass_kernel_spmd(nc, [{**shl, **cores[b]} for b in range(8)], core_ids=list(range(8)))
        for b in range(8):
            cores[b]["x"] = np.ascontiguousarray(res.results[b]["out"], dtype=np.float32)
            cores[b]["ctx"] = np.ascontiguousarray(res.results[b]["outc"], dtype=np.float32)
    return np.stack([cores[b]["x"] for b in range(8)], axis=0).astype(np.float32)
```

```python
import numpy as np
from contextlib import ExitStack
import concourse.bass as bass
import concourse.mybir as mybir
from concourse.bass_utils import run_bass_kernel_spmd

F32 = mybir.dt.float32
BF16 = mybir.dt.bfloat16
AF = mybir.ActivationFunctionType
ALU = mybir.AluOpType
AX = mybir.AxisListType

D = 1024
S = 2048
C = 256
NTOK = S + C
DFF = 2816
EPS = 1e-6
KC = D // 128
GRID_W = 64


class Prog:
    ENGS = ["tensor", "vector", "scalar", "gpsimd", "sync"]

    def __init__(self, nc, stack, n_dma_sems=48):
        self.nc = nc
        self.ops = {e: [] for e in self.ENGS}
        self.count = {e: 0 for e in self.ENGS}
        self.waited = {e: {} for e in self.ENGS}
        self.last_w = {}
        self.readers = {}
        self.n_dma_sems = n_dma_sems
        self.dma_i = 0
        self.dma_j = 0
        self.dma_sem_use = [0] * n_dma_sems
        self.sems = {e: stack.enter_context(nc.semaphore("s_" + e)) for e in self.ENGS}
        self.dsems = [stack.enter_context(nc.semaphore("d_%d" % i)) for i in range(n_dma_sems)]
        self.out_evs = []

    def _need(self, eng, ev, waits):
        if ev is None:
            return
        if ev[0] == "e" and ev[1] == eng and eng == "tensor":
            return
        key = (ev[0], ev[1])
        if self.waited[eng].get(key, 0) >= ev[2]:
            return
        self.waited[eng][key] = ev[2]
        waits[key] = max(waits.get(key, 0), ev[2])

    def op(self, eng, fn, reads=(), writes=(), dma=False):
        writes = list(writes) + [r for r in reads if r.startswith("PSUM_")]
        reads = [r for r in reads if not r.startswith("PSUM_")]
        waits = {}
        for r in reads:
            self._need(eng, self.last_w.get(r), waits)
        for w in writes:
            self._need(eng, self.last_w.get(w), waits)
            for ev in self.readers.get(w, ()):
                self._need(eng, ev, waits)
        if dma:
            half = self.n_dma_sems // 2
            if eng == "gpsimd":
                si = half + self.dma_j % half
                self.dma_j += 1
            else:
                si = self.dma_i % half
                self.dma_i += 1
            if self.dma_sem_use[si] > 0:
                self._need(eng, ("d", si, 16 * self.dma_sem_use[si]), waits)
            self.dma_sem_use[si] += 1
            ev = ("d", si, 16 * self.dma_sem_use[si])
        else:
            self.count[eng] += 1
            ev = ("e", eng, self.count[eng])
        self.ops[eng].append((fn, list(waits.items()), ev))
        for r in reads:
            self.readers.setdefault(r, []).append(ev)
        for w in writes:
            self.last_w[w] = ev
            self.readers[w] = []
        return ev

    def barrier(self):
        evs = [("e", e, self.count[e]) for e in self.ENGS if self.count[e] > 0]
        evs += [("d", si, 16 * u) for si, u in enumerate(self.dma_sem_use) if u > 0]
        for e in self.ENGS:
            waits = {}
            for ev in evs:
                if ev[0] == "e" and ev[1] == e:
                    continue
                self._need(e, ev, waits)
            if waits:
                self.ops[e].append((None, list(waits.items()), None))
        self.last_w = {}
        self.readers = {}

    def flush(self):
        nc = self.nc
        with nc.Block() as block:
            def run(engname):
                def body(eng):
                    for fn, waits, ev in self.ops[engname]:
                        for (kind, k), val in waits:
                            sem = self.sems[k] if kind == "e" else self.dsems[k]
                            eng.wait_ge(sem, val)
                        if fn is None:
                            continue
                        ins = fn(eng)
                        if ev[0] == "e":
                            ins.then_inc(self.sems[ev[1]], 1)
                        else:
                            ins.then_inc(self.dsems[ev[1]], 16)
                return body
            block.tensor(run("tensor"))
            block.vector(run("vector"))
            block.scalar(run("scalar"))
            block.gpsimd(run("gpsimd"))
            block.sync(run("sync"))
        self.ops = {e: [] for e in self.ENGS}


class Ring:
    def __init__(self, items):
        self.items = items
        self.i = 0

    def next(self):
        it = self.items[self.i % len(self.items)]
        self.i += 1
        return it


class KB:
    def __init__(self, nc, st):
        self.nc = nc
        self.st = st
        self.P = Prog(nc, st)
        self.uid = 0

    def sb(self, st, name, shape, dt):
        self.uid += 1
        nm = "%s_%d" % (name, self.uid)
        return st.enter_context(self.nc.sbuf_tensor(nm, list(shape), dt)), nm

    def ps(self, st, name, shape, dt):
        self.uid += 1
        nm = "PSUM_%s_%d" % (name, self.uid)
        return st.enter_context(self.nc.psum_tensor(nm, list(shape), dt)), nm

    def ring(self, st, name, shape, dt, n, psum=False):
        return Ring([(self.ps if psum else self.sb)(st, name, shape, dt) for _ in range(n)])


def run_lanes(gens, width):
    active = []
    it = iter(gens)
    more = True
    while True:
        while more and len(active) < width:
            try:
                active.append(next(it))
            except StopIteration:
                more = False
        if not active:
            break
        for g in list(active):
            try:
                next(g)
            except StopIteration:
                active.remove(g)


def tok_macros(t0, ntok, width=512):
    out = []
    o = 0
    while o < ntok:
        n = min(width, ntok - o)
        out.append((t0 + o, n))
        o += n
    return out


def stage_consts(kb, st):
    nc, P = kb.nc, kb.P
    identf, n_if = kb.sb(st, "identf", [128, 128], F32)
    ident, n_i = kb.sb(st, "ident", [128, 128], BF16)
    nh, n_nh = kb.sb(st, "neghalf", [128, 8], F32)
    P.op("gpsimd", lambda e: e.memset(identf[:], 1.0), writes=[n_if])
    P.op("gpsimd", lambda e: e.affine_select(out=identf[:], in_=identf[:], pattern=[[-1, 128]],
                                              compare_op=ALU.is_equal, fill=0.0, base=0, channel_multiplier=1),
         reads=[n_if], writes=[n_if])
    P.op("vector", lambda e: e.tensor_copy(out=ident[:], in_=identf[:]), reads=[n_if], writes=[n_i])
    P.op("gpsimd", lambda e: e.memset(nh[:], -0.5), writes=[n_nh])
    kb.ident, kb.n_ident = ident, n_i
    kb.identf, kb.n_identf = identf, n_if
    kb.nh, kb.n_nh = nh, n_nh
    NEGB = -30000.0
    for nm, pat, cm in (("masklo", [[-1, 128]], 1), ("maskhi", [[1, 128]], -1)):
        mf, n_mf = kb.sb(st, nm + "f", [128, 128], F32)
        mb, n_mb = kb.sb(st, nm, [128, 128], BF16)
        P.op("gpsimd", lambda e, mf=mf: e.memset(mf[:], 1.0), writes=[n_mf])
        P.op("gpsimd", lambda e, mf=mf, pat=pat, cm=cm: e.affine_select(out=mf[:], in_=mf[:], pattern=pat, compare_op=ALU.is_ge,
                                                                        fill=0.0, base=0, channel_multiplier=cm),
             reads=[n_mf], writes=[n_mf])
        P.op("vector", lambda e, mf=mf, mb=mb: e.tensor_copy(out=mb[:], in_=mf[:]), reads=[n_mf], writes=[n_mb])
        setattr(kb, nm, mb)
        setattr(kb, "n_" + nm, n_mb)
    nh2, n_nh2 = kb.sb(st, "neghalf2", [128, 16], F32)
    P.op("gpsimd", lambda e: e.memset(nh2[:], -0.5), writes=[n_nh2])
    kb.nh2, kb.n_nh2 = nh2, n_nh2


def rstd_from_ssq(kb, ssq, n_ssq, ncols, inv_n):
    P = kb.P
    P.op("vector", lambda e: e.tensor_scalar(out=ssq[:, 0:ncols], in0=ssq[:, 0:ncols], scalar1=inv_n, scalar2=EPS,
                                             op0=ALU.mult, op1=ALU.add), reads=[n_ssq], writes=[n_ssq])
    P.op("gpsimd", lambda e: e.tensor_tensor(out=ssq[:, 0:ncols], in0=ssq[:, 0:ncols], in1=kb.nh[:, 0:ncols],
                                             op=ALU.pow), reads=[n_ssq, kb.n_nh], writes=[n_ssq])


def stage_mod(kb, st, T, li, lst):
    nc, P = kb.nc, kb.P
    Afm, n_A = kb.sb(lst, "Afm", [128, 3, 2, 8], F32)
    Bfm, n_B = kb.sb(lst, "Bfm", [128, 3, 2, 8], F32)
    Gbc, n_G = kb.sb(lst, "Gbc", [128, 3, 2, 1024], F32)
    cf, n_cf = kb.sb(st, "cf", [128, 8, 2], F32)
    sc, n_sc = kb.sb(st, "sc", [128, 8, 2], F32)
    sbc, n_sbc = kb.sb(st, "sbc", [128, 2, 8, 128], F32)
    mbfm, n_mbfm = kb.sb(st, "mbfm", [128, 72], F32)
    gfm, n_gfm = kb.sb(st, "gfm", [128, 48], F32)
    modfm, n_modfm = kb.sb(st, "modfm", [128, 9, 8, 2], F32)
    wring = kb.ring(st, "modw", [128, 8, 512], F32, 2)
    bring = kb.ring(st, "modbb", [128, 512], F32, 2)
    gring = kb.ring(st, "gpost", [128, 512], F32, 2)
    tring = kb.ring(st, "modtmp", [128, 512], F32, 2)
    pfm, n_pfm = kb.ps(st, "pfm", [128, 9, 8, 2], F32)
    pbr = kb.ring(st, "pbc", [128, 512], F32, 2, psum=True)

    P.op("sync", lambda e: e.dma_start(out=cf[:], in_=T["cfm"][:, :, :]), writes=[n_cf], dma=True)
    P.op("sync", lambda e: e.dma_start(out=mbfm[:], in_=T["mod_b_fm"][li]), writes=[n_mbfm], dma=True)
    P.op("sync", lambda e: e.dma_start(out=gfm[:], in_=T["norm_g_fm"][li]), writes=[n_gfm], dma=True)
    P.op("scalar", lambda e: e.activation(out=sc[:], in_=cf[:], func=AF.Silu), reads=[n_cf], writes=[n_sc])
    for s in range(2):
        for kc in range(8):
            P.op("vector", lambda e, s=s, kc=kc: e.tensor_copy(out=sbc[:, s, kc, :],
                                                               in_=sc[:, kc, s:s + 1].to_broadcast([128, 128])),
                 reads=[n_sc], writes=[n_sbc])
    mw = T["mod_w"][li].rearrange("(kc p) n -> p kc n", p=128)
    wsub = [0.5, 1.0, 0.5]
    for slot in range(9):
        sub = slot // 3
        for half in range(2):
            col0 = slot * 1024 + half * 512
            (wt, n_wt) = wring.next()
            P.op("sync", lambda e, wt=wt, col0=col0: e.dma_start(out=wt[:], in_=mw[:, :, col0:col0 + 512]),
                 writes=[n_wt], dma=True)
            if slot % 3 != 2:
                def mm(e, wt=wt, slot=slot, half=half):
                    ins = None
                    for oc in range(4):
                        for kc in range(8):
                            ins = e.matmul(pfm[:, slot, half * 4 + oc, :], lhsT=wt[:, kc, oc * 128:(oc + 1) * 128],
                                           rhs=sc[:, kc, :], start=(kc == 0), stop=(kc == 7))
                    return ins
                P.op("tensor", mm, reads=[n_wt, n_sc], writes=[n_pfm])
            else:
                (bt, n_bt) = bring.next()
                (gt, n_gt) = gring.next()
                P.op("sync", lambda e, bt=bt, col0=col0: e.dma_start(
                    out=bt[:], in_=T["mod_b"][li, col0:col0 + 512].partition_broadcast(128)), writes=[n_bt], dma=True)
                P.op("sync", lambda e, gt=gt, sub=sub, half=half: e.dma_start(
                    out=gt[:], in_=T["norm_g"][li, 2 * sub + 1, half * 512:(half + 1) * 512].partition_broadcast(128)),
                    writes=[n_gt], dma=True)
                for s in range(2):
                    (pb, n_pb) = pbr.next()
                    (tt, n_tt) = tring.next()

                    def mm(e, wt=wt, s=s, pb=pb):
                        ins = None
                        for kc in range(8):
                            ins = e.matmul(pb[:], lhsT=sbc[:, s, kc, :], rhs=wt[:, kc, :], start=(kc == 0), stop=(kc == 7))
                        return ins
                    P.op("tensor", mm, reads=[n_wt, n_sbc], writes=[n_pb])
                    P.op("vector", lambda e, pb=pb, bt=bt, tt=tt: e.tensor_tensor(out=tt[:], in0=pb[:], in1=bt[:], op=ALU.add),
                         reads=[n_pb, n_bt], writes=[n_tt])
                    P.op("vector", lambda e, tt=tt, gt=gt, sub=sub, s=s, half=half: e.scalar_tensor_tensor(
                        out=Gbc[:, sub, s, half * 512:(half + 1) * 512], in0=tt[:], scalar=wsub[sub], in1=gt[:],
                        op0=ALU.mult, op1=ALU.mult), reads=[n_tt, n_gt], writes=[n_G])
    for slot in range(9):
        if slot % 3 == 2:
            continue
        for s in range(2):
            P.op("vector", lambda e, s=s, slot=slot: e.tensor_tensor(out=modfm[:, slot, :, s], in0=pfm[:, slot, :, s],
                                                                     in1=mbfm[:, slot * 8:slot * 8 + 8], op=ALU.add),
                 reads=[n_pfm, n_mbfm], writes=[n_modfm])
    for sub in range(3):
        for s in range(2):
            P.op("vector", lambda e, sub=sub, s=s: e.scalar_tensor_tensor(
                out=Afm[:, sub, s, :], in0=modfm[:, 3 * sub + 1, :, s], scalar=1.0,
                in1=gfm[:, (2 * sub) * 8:(2 * sub) * 8 + 8], op0=ALU.add, op1=ALU.mult),
                reads=[n_modfm, n_gfm], writes=[n_A])
            P.op("vector", lambda e, sub=sub, s=s: e.tensor_copy(out=Bfm[:, sub, s, :], in_=modfm[:, 3 * sub, :, s]),
                 reads=[n_modfm], writes=[n_B])
    return dict(Afm=Afm, n_A=n_A, Bfm=Bfm, n_B=n_B, Gbc=Gbc, n_G=n_G)


def stage_norm_T(kb, st, xsrc, tiles, M, sub, HT, n_HT, psum_ring=None, rings=(6, 2, 8)):
    nc, P = kb.nc, kb.P
    xr = kb.ring(st, "nx", [128, 1024], F32, rings[0])
    jr = kb.ring(st, "njunk", [128, 1024], BF16, rings[1])
    xnr = kb.ring(st, "nxn", [128, 1024], BF16, rings[2])
    sr = kb.ring(st, "nssq", [128, 8], F32, 2)
    ptr = psum_ring or kb.ring(st, "nptr", [128, 512], BF16, 2, psum=True)
    ev = 0
    for m0 in range(0, len(tiles), 4):
        grp = tiles[m0:m0 + 4]
        n = len(grp)
        s = 0 if grp[0] < 16 else 1
        assert all((t < 16) == (grp[0] < 16) for t in grp)
        (ssq, n_ssq) = sr.next()
        xts = []
        for j, tt in enumerate(grp):
            (xt, n_xt) = xr.next()
            (jk, n_jk) = jr.next()
            src, n_src = xsrc(tt)
            P.op("sync", lambda e, xt=xt, src=src: e.dma_start(out=xt[:], in_=src), reads=[n_src], writes=[n_xt], dma=True)
            P.op("scalar", lambda e, xt=xt, jk=jk, ssq=ssq, j=j: e.activation(out=jk[:], in_=xt[:], func=AF.Square,
                                                                               accum_out=ssq[:, j:j + 1]),
                 reads=[n_xt], writes=[n_jk, n_ssq])
            xts.append((xt, n_xt))
        rstd_from_ssq(kb, ssq, n_ssq, n, 1.0 / D)
        xns = []
        for j, tt in enumerate(grp):
            (xn, n_xn) = xnr.next()
            xt, n_xt = xts[j]
            if j % 2 == 0:
                P.op("vector", lambda e, xn=xn, xt=xt, ssq=ssq, j=j: e.tensor_scalar(
                    out=xn[:], in0=xt[:], scalar1=ssq[:, j:j + 1], scalar2=None, op0=ALU.mult),
                    reads=[n_xt, n_ssq], writes=[n_xn])
            else:
                P.op("scalar", lambda e, xn=xn, xt=xt, ssq=ssq, j=j: e.activation(
                    out=xn[:], in_=xt[:], func=AF.Identity, scale=ssq[:, j:j + 1]),
                    reads=[n_xt, n_ssq], writes=[n_xn])
            xns.append((xn, n_xn))
        for kc in range(8):
            (pt, n_pt) = ptr.next()

            def tr(e, pt=pt, kc=kc, xns=xns):
                ins = None
                for j, (xn, _) in enumerate(xns):
                    ins = e.transpose(out=pt[:, j * 128:(j + 1) * 128], in_=xn[:, kc * 128:(kc + 1) * 128],
                                      identity=kb.ident[:])
                return ins
            P.op("tensor", tr, reads=[nm for _, nm in xns] + [kb.n_ident], writes=[n_pt])
            c0 = m0 * 128
            if ev % 2 == 0:
                P.op("scalar", lambda e, pt=pt, kc=kc, c0=c0, n=n, s=s: e.activation(
                    out=HT[:, kc, c0:c0 + n * 128], in_=pt[:, 0:n * 128], func=AF.Identity,
                    scale=M["Afm"][:, sub, s, kc:kc + 1], bias=M["Bfm"][:, sub, s, kc:kc + 1]),
                    reads=[n_pt, M["n_A"], M["n_B"]], writes=[n_HT])
            else:
                P.op("vector", lambda e, pt=pt, kc=kc, c0=c0, n=n, s=s: e.tensor_scalar(
                    out=HT[:, kc, c0:c0 + n * 128], in0=pt[:, 0:n * 128], scalar1=M["Afm"][:, sub, s, kc:kc + 1],
                    scalar2=M["Bfm"][:, sub, s, kc:kc + 1], op0=ALU.mult, op1=ALU.add),
                    reads=[n_pt, M["n_A"], M["n_B"]], writes=[n_HT])
            ev += 1


class PostStage:
    def __init__(self, kb, st, bufs=2):
        self.kb = kb
        self.xr = kb.ring(st, "px", [128, 1024], F32, bufs)
        self.jr = kb.ring(st, "pjunk", [128, 1024], BF16, bufs)
        self.tr = kb.ring(st, "ptmp", [128, 1024], F32, bufs)
        self.sr = kb.ring(st, "pssq", [128, 8], F32, 6)

    def run(self, *a, **k):
        for _ in self.run_gen(*a, **k):
            pass

    def run_gen(self, y_ap_halves, y_names, tt, M, sub, xsrc, xdst, is_out=False):
        kb = self.kb
        P = kb.P
        s = 0 if tt < 16 else 1
        (xt, n_xt) = self.xr.next()
        (jk, n_jk) = self.jr.next()
        (tm, n_tm) = self.tr.next()
        (ssq, n_ssq) = self.sr.next()
        src, n_src = xsrc(tt)
        dst, n_dst = xdst(tt)
        P.op("sync", lambda e: e.dma_start(out=xt[:], in_=src), reads=[n_src], writes=[n_xt], dma=True)
        for h, yh in enumerate(y_ap_halves):
            P.op("scalar", lambda e, h=h, yh=yh: e.activation(out=jk[:, h * 512:(h + 1) * 512], in_=yh, func=AF.Square,
                                                             accum_out=ssq[:, h:h + 1]),
                 reads=[y_names[h]], writes=[n_jk, n_ssq])
        yield
        P.op("vector", lambda e: e.tensor_tensor(out=ssq[:, 2:3], in0=ssq[:, 0:1], in1=ssq[:, 1:2], op=ALU.add),
             reads=[n_ssq], writes=[n_ssq])
        yield
        P.op("vector", lambda e: e.tensor_scalar(out=ssq[:, 3:4], in0=ssq[:, 2:3], scalar1=1.0 / D, scalar2=EPS,
                                                 op0=ALU.mult, op1=ALU.add), reads=[n_ssq], writes=[n_ssq])
        yield
        P.op("gpsimd", lambda e: e.tensor_tensor(out=ssq[:, 4:5], in0=ssq[:, 3:4], in1=kb.nh[:, 0:1], op=ALU.pow),
             reads=[n_ssq, kb.n_nh], writes=[n_ssq])
        yield
        for h, yh in enumerate(y_ap_halves):
            P.op("vector", lambda e, h=h, yh=yh: e.scalar_tensor_tensor(
                out=tm[:, h * 512:(h + 1) * 512], in0=yh, scalar=ssq[:, 4:5],
                in1=M["Gbc"][:, sub, s, h * 512:(h + 1) * 512], op0=ALU.mult, op1=ALU.mult),
                reads=[y_names[h], n_ssq, M["n_G"]], writes=[n_tm])
        yield
        P.op("vector", lambda e: e.tensor_tensor(out=tm[:], in0=tm[:], in1=xt[:], op=ALU.add),
             reads=[n_tm, n_xt], writes=[n_tm])
        ev = P.op("sync", lambda e: e.dma_start(out=dst, in_=tm[:]), reads=[n_tm], writes=[n_dst], dma=True)
        if is_out:
            P.out_evs.append(ev)


def stage_ffn(kb, st, T, li, which, tiles, M, sub, xsrc, xdst, is_out=False):
    nc, P = kb.nc, kb.P
    nt = len(tiles)
    ntok = nt * 128
    HT, n_HT = kb.sb(st, "HT", [128, 8, ntok], BF16)
    Y, n_Y = kb.sb(st, "Y", [128, nt, 1024], F32)
    with ExitStack() as st2:
        stage_norm_T(kb, st2, xsrc, tiles, M, sub, HT, n_HT)
        P.barrier()
        P.flush()
    with ExitStack() as st3:
        FG = 256
        ngrp = DFF // FG
        wgr = kb.ring(st3, "wg", [128, 8, FG], BF16, 2)
        wur = kb.ring(st3, "wu", [128, 8, FG], BF16, 2)
        wdr = kb.ring(st3, "wd", [128, FG // 128, 1024], BF16, 2)
        actr = kb.ring(st3, "actT", [128, FG // 128, ntok], BF16, 2)
        sgr = kb.ring(st3, "sg", [128, 512], F32, 3)
        pgr = kb.ring(st3, "pg", [128, 512], F32, 2, psum=True)
        pur = kb.ring(st3, "pu", [128, 512], F32, 2, psum=True)
        pdr = kb.ring(st3, "pd", [128, 512], F32, 3, psum=True)
        wgd = T["ffn_w_gate"][li, which].rearrange("(kc p) n -> p kc n", p=128)
        wud = T["ffn_w_up"][li, which].rearrange("(kc p) n -> p kc n", p=128)
        wdd = T["ffn_w_down"][li, which].rearrange("(fc p) n -> p fc n", p=128)
        macros = tok_macros(0, ntok)
        for g in range(ngrp):
            f0 = g * FG
            (wg, n_wg) = wgr.next()
            (wu, n_wu) = wur.next()
            (wd, n_wd) = wdr.next()
            (act, n_act) = actr.next()
            P.op("gpsimd", lambda e, wg=wg, f0=f0: e.dma_start(out=wg[:], in_=wgd[:, :, f0:f0 + FG]), writes=[n_wg], dma=True)
            P.op("gpsimd", lambda e, wu=wu, f0=f0: e.dma_start(out=wu[:], in_=wud[:, :, f0:f0 + FG]), writes=[n_wu], dma=True)
            P.op("gpsimd", lambda e, wd=wd, f0=f0: e.dma_start(out=wd[:], in_=wdd[:, f0 // 128:(f0 + FG) // 128, :]),
                 writes=[n_wd], dma=True)
            for fc in range(FG // 128):
                for (t0, n) in macros:
                    (pg, n_pg) = pgr.next()
                    (pu, n_pu) = pur.next()
                    (sg, n_sg) = sgr.next()

                    def mm(e, w=wg, p=pg, fc=fc, t0=t0, n=n):
                        ins = None
                        for kc in range(8):
                            ins = e.matmul(p[:, 0:n], lhsT=w[:, kc, fc * 128:(fc + 1) * 128], rhs=HT[:, kc, t0:t0 + n],
                                           start=(kc == 0), stop=(kc == 7))
                        return ins
                    P.op("tensor", mm, reads=[n_wg, n_HT], writes=[n_pg])

                    def mm2(e, w=wu, p=pu, fc=fc, t0=t0, n=n):
                        ins = None
                        for kc in range(8):
                            ins = e.matmul(p[:, 0:n], lhsT=w[:, kc, fc * 128:(fc + 1) * 128], rhs=HT[:, kc, t0:t0 + n],
                                           start=(kc == 0), stop=(kc == 7))
                        return ins
                    P.op("tensor", mm2, reads=[n_wu, n_HT], writes=[n_pu])
                    P.op("scalar", lambda e, sg=sg, pg=pg, n=n: e.activation(out=sg[:, 0:n], in_=pg[:, 0:n], func=AF.Silu),
                         reads=[n_pg], writes=[n_sg])
                    P.op("vector", lambda e, act=act, sg=sg, pu=pu, fc=fc, t0=t0, n=n: e.tensor_tensor(
                        out=act[:, fc, t0:t0 + n], in0=sg[:, 0:n], in1=pu[:, 0:n], op=ALU.mult),
                        reads=[n_sg, n_pu], writes=[n_act + "_%d" % (t0 // 512)])
            for j in range(nt):
                for h in range(2):
                    (pd, n_pd) = pdr.next()

                    def mmd(e, pd=pd, act=act, wd=wd, j=j, h=h):
                        ins = None
                        nfc = FG // 128
                        for fc in range(nfc):
                            ins = e.matmul(pd[:], lhsT=act[:, fc, j * 128:(j + 1) * 128], rhs=wd[:, fc, h * 512:(h + 1) * 512],
                                           start=(fc == 0), stop=(fc == nfc - 1))
                        return ins
                    P.op("tensor", mmd, reads=[n_act + "_%d" % (j // 4), n_wd], writes=[n_pd])
                    yname = n_Y + "_%d_%d" % (j, h)
                    if g == 0:
                        P.op("scalar", lambda e, pd=pd, j=j, h=h: e.activation(out=Y[:, j, h * 512:(h + 1) * 512], in_=pd[:],
                                                                                func=AF.Identity),
                             reads=[n_pd], writes=[yname])
                    else:
                        P.op("vector", lambda e, pd=pd, j=j, h=h: e.tensor_tensor(
                            out=Y[:, j, h * 512:(h + 1) * 512], in0=Y[:, j, h * 512:(h + 1) * 512], in1=pd[:], op=ALU.add),
                            reads=[n_pd, yname], writes=[yname])
        P.barrier()
        P.flush()
    post = PostStage(kb, st, bufs=4)
    run_lanes((post.run_gen([Y[:, j, 0:512], Y[:, j, 512:1024]], [n_Y + "_%d_0" % j, n_Y + "_%d_1" % j], tt, M, sub, xsrc, xdst,
                            is_out=is_out) for j, tt in enumerate(tiles)), 4)
    P.barrier()
    P.flush()


class AttnCore:
    def __init__(self, kb, st, n_ps=3, n_pt=3):
        self.kb = kb
        self.psr = kb.ring(st, "pS", [128, 512], F32, n_ps, psum=True)
        self.ptr = kb.ring(st, "PT", [128, 512], BF16, n_pt)
        self.pending = None

    def _submit(self, qk_fn, qk_reads, ncols, exp_scale, mults, pv_fn, pv_reads, pv_writes, after, wide_mult=None, nq=4):
        kb = self.kb
        P = kb.P
        (pS, n_pS) = self.psr.next()
        (PT, n_PT) = self.ptr.next()
        P.op("tensor", lambda e: qk_fn(e, pS), reads=qk_reads, writes=[n_pS])
        P.op("scalar", lambda e: e.activation(out=PT[:, 0:ncols], in_=pS[:, 0:ncols], func=AF.Exp, scale=exp_scale),
             reads=[n_pS], writes=[n_PT])
        for (i, m_ap, n_m) in mults:
            P.op("vector", lambda e, i=i, m_ap=m_ap: e.tensor_tensor(out=PT[:, i * 128:(i + 1) * 128], in0=PT[:, i * 128:(i + 1) * 128],
                                                                   in1=m_ap, op=ALU.mult), reads=[n_PT, n_m], writes=[n_PT])
        if wide_mult is not None:
            m_ap, n_m = wide_mult
            P.op("vector", lambda e: e.tensor_tensor(out=PT[:, 0:nq * 128].rearrange("p (a b) -> p a b", b=128),
                                                     in0=PT[:, 0:nq * 128].rearrange("p (a b) -> p a b", b=128),
                                                     in1=m_ap.unsqueeze(1).to_broadcast([128, nq, 128]), op=ALU.mult),
                 reads=[n_PT, n_m], writes=[n_PT])
        prev = self.pending
        self.pending = (lambda e: pv_fn(e, PT), [n_PT] + list(pv_reads), list(pv_writes), after)
        if prev is not None:
            self._emit(prev)

    def _emit(self, pend):
        fn, rd, wr, after = pend
        self.kb.P.op("tensor", fn, reads=rd, writes=wr)
        if after is not None:
            after()

    def flush_pending(self):
        if self.pending is not None:
            p = self.pending
            self.pending = None
            self._emit(p)

    def bank(self, blocks, exp_scale, after=None):
        kb = self.kb
        nb = len(blocks)
        assert 1 <= nb <= 4

        def qk(e, pS):
            ins = None
            for i, b in enumerate(blocks):
                ins = e.matmul(pS[:, i * 128:(i + 1) * 128], lhsT=b["kT"], rhs=b["qT"], start=True, stop=True)
            return ins

        def pv(e, PT):
            ins = None
            for i, b in enumerate(blocks):
                ins = e.matmul(b["po"], lhsT=PT[:, i * 128:(i + 1) * 128], rhs=b["v"], start=b["start"], stop=b["stop"])
            return ins
        rd = []
        rdv = []
        wr = []
        mults = []
        for i, b in enumerate(blocks):
            rd += list(b["rd"])
            rdv += list(b["rdv"])
            wr.append(b["n_po"])
            if b.get("mult") is not None:
                mults.append((i, b["mult"], b["n_mult"]))
        self._submit(qk, rd, nb * 128, exp_scale, mults, pv, rdv, wr, after)

    def bank_wide(self, kT, qT_wide, rd, v, rdv, pos, start, stop, exp_scale, nq=4, after=None, mult=None):
        def qk(e, pS):
            return e.matmul(pS[:, 0:nq * 128], lhsT=kT, rhs=qT_wide, start=True, stop=True)

        def pv(e, PT):
            ins = None
            for i in range(nq):
                ins = e.matmul(pos[i][0], lhsT=PT[:, i * 128:(i + 1) * 128], rhs=v, start=start, stop=stop)
            return ins
        self._submit(qk, list(rd), nq * 128, exp_scale, [], pv, rdv, [p[1] for p in pos], after, wide_mult=mult, nq=nq)


class OutProj:
    def __init__(self, kb, st, T, wo_dram, M, xsrc, xdst, post_bufs=1, n_py=2, n_pot=1, n_ot=2):
        self.kb = kb
        P = kb.P
        self.M = M
        self.xsrc, self.xdst = xsrc, xdst
        self.wo, self.n_wo = kb.sb(st, "wo", [128, 8, 1024], BF16)
        P.op("gpsimd", lambda e: e.dma_start(out=self.wo[:], in_=wo_dram.rearrange("(kc p) n -> p kc n", p=128)),
             writes=[self.n_wo], dma=True)
        self.otr = kb.ring(st, "OT", [128, 8, 128], BF16, n_ot)
        self.ptr = kb.ring(st, "pOT", [128, 1024], BF16, n_pot, psum=True)
        self.pyr = kb.ring(st, "pY", [128, 512], F32, n_py, psum=True)
        self.ysr = kb.ring(st, "Ysb", [128, 1024], F32, 2) if n_py < 2 else None
        self.post = PostStage(kb, st, bufs=post_bufs)

    def run(self, *a, **k):
        for _ in self.run_gen(*a, **k):
            pass

    def run_gen(self, ocat_ap, n_ocat, tt, is_out=False):
        kb = self.kb
        P = kb.P
        (OT, n_OT) = self.otr.next()
        (pt, n_pt) = self.ptr.next()

        def tr(e):
            ins = None
            for c in range(8):
                ins = e.transpose(out=pt[:, c * 128:(c + 1) * 128], in_=ocat_ap[:, c * 128:(c + 1) * 128], identity=kb.ident[:])
            return ins
        P.op("tensor", tr, reads=[n_ocat, kb.n_ident], writes=[n_pt])
        yield
        P.op("vector", lambda e: e.tensor_copy(out=OT[:].rearrange("p c t -> p (c t)"), in_=pt[:]), reads=[n_pt], writes=[n_OT])
        yield
        halves = []
        names = []
        for h in range(2):
            (py, n_py) = self.pyr.next()

            def mm(e, py=py, h=h):
                ins = None
                for c in range(8):
                    ins = e.matmul(py[:], lhsT=OT[:, c, :], rhs=self.wo[:, c, h * 512:(h + 1) * 512], start=(c == 0), stop=(c == 7))
                return ins
            P.op("tensor", mm, reads=[n_OT, self.n_wo], writes=[n_py])
            if self.ysr is not None:
                if h == 0:
                    (ys, n_ys) = self.ysr.next()
                P.op("scalar", lambda e, ys=ys, py=py, h=h: e.activation(out=ys[:, h * 512:(h + 1) * 512], in_=py[:], func=AF.Identity),
                     reads=[n_py], writes=[n_ys + "_%d" % h])
                halves.append(ys[:, h * 512:(h + 1) * 512])
                names.append(n_ys + "_%d" % h)
            else:
                halves.append(py[:])
                names.append(n_py)
        yield
        yield from self.post.run_gen(halves, names, tt, self.M, 1, self.xsrc, self.xdst, is_out=is_out)


def load_w_bf16(kb, st, name, dram_ap_pkn, ncols, piece=512):
    P = kb.P
    w, n_w = kb.sb(st, name, [128, 8, ncols], BF16)
    for c0 in range(0, ncols, piece):
        n = min(piece, ncols - c0)
        P.op("gpsimd", lambda e, c0=c0, n=n: e.dma_start(out=w[:, :, c0:c0 + n], in_=dram_ap_pkn[:, :, c0:c0 + n]),
             writes=[n_w + "_%d" % (c0 // piece)], dma=True)
    return w, n_w


def stage_mixer_d(kb, st, T, M, xsrc, xdst):
    nc, P = kb.nc, kb.P
    H, HKV, DH = 8, 2, 128
    QT, n_QT = kb.sb(st, "QT", [128, H, S], BF16)
    KT, n_KT = kb.sb(st, "KT", [128, HKV, NTOK], BF16)
    VA, n_VA = kb.sb(st, "VA", [128, 18, HKV, DH + 2], BF16)
    P.op("gpsimd", lambda e: e.memset(VA[:, :, :, DH:DH + 1], 1.0), writes=[n_VA])
    with ExitStack() as st2:
        HT, n_HT = kb.sb(st2, "HT", [128, 8, NTOK], BF16)
        with ExitStack() as st3:
            stage_norm_T(kb, st3, xsrc, list(range(18)), M, 1, HT, n_HT)
            P.barrier()
            P.flush()
        wq, n_wq = load_w_bf16(kb, st2, "wqkv", T["ga_w_qkv_p"].rearrange("(kc p) n -> p kc n", p=128), 1536)
        gain, n_gain = kb.sb(st2, "gain", [128, 10, 128], F32)
        cos, n_cos = kb.sb(st2, "cos", [128, 16, 64], F32)
        sin, n_sin = kb.sb(st2, "sin", [128, 16, 64], F32)
        P.op("sync", lambda e: e.dma_start(out=gain[:].rearrange("p a b -> p (a b)"),
                                           in_=T["ga_gain"][:].partition_broadcast(128)), writes=[n_gain], dma=True)
        P.op("sync", lambda e: e.dma_start(out=cos[:], in_=T["ga_cos"].rearrange("(t p) d -> p t d", p=128)), writes=[n_cos], dma=True)
        P.op("sync", lambda e: e.dma_start(out=sin[:], in_=T["ga_sin"].rearrange("(t p) d -> p t d", p=128)), writes=[n_sin], dma=True)
        ppr = kb.ring(st2, "pproj", [128, 512], F32, 3, psum=True)
        ptq = kb.ring(st2, "ptq", [128, 1024], BF16, 2, psum=True)
        qfr = kb.ring(st2, "qf", [128, 10, 128], F32, 3)
        sqr = kb.ring(st2, "qsq", [128, 10, 128], F32, 2)
        ssr = kb.ring(st2, "qss", [128, 16], F32, 3)
        rar = kb.ring(st2, "ra", [128, 10, 64], F32, 4)
        rbr = kb.ring(st2, "rb", [128, 10, 64], F32, 4)
        qrr = kb.ring(st2, "qr", [128, 10, 128], BF16, 3)
        def tile_gen(tt):
            lat = tt < 16
            (qf, n_qf) = qfr.next()
            (sq, n_sq) = sqr.next()
            (ss, n_ss) = ssr.next()
            (qr, n_qr) = qrr.next()
            pieces = ([(0, 0), (512, 4)] if lat else []) + [(1024, 8)]
            for (c0, h0) in pieces:
                (pp, n_pp) = ppr.next()

                def mm(e, pp=pp, c0=c0, tt=tt):
                    ins = None
                    for kc in range(8):
                        ins = e.matmul(pp[:], lhsT=HT[:, kc, tt * 128:(tt + 1) * 128], rhs=wq[:, kc, c0:c0 + 512],
                                       start=(kc == 0), stop=(kc == 7))
                    return ins
                P.op("tensor", mm, reads=[n_HT, n_wq + "_%d" % (c0 // 512)], writes=[n_pp])
                if c0 < 1024:
                    P.op("scalar", lambda e, pp=pp, qf=qf, h0=h0: e.activation(
                        out=qf[:, h0:h0 + 4, :].rearrange("p a b -> p (a b)"), in_=pp[:], func=AF.Identity),
                        reads=[n_pp], writes=[n_qf])
                else:
                    P.op("scalar", lambda e, pp=pp, qf=qf: e.activation(
                        out=qf[:, 8:10, :].rearrange("p a b -> p (a b)"), in_=pp[:, 0:256], func=AF.Identity),
                        reads=[n_pp], writes=[n_qf])
                    P.op("vector", lambda e, pp=pp, tt=tt: e.tensor_copy(
                        out=VA[:, tt, :, 0:DH], in_=pp[:, 256:512].rearrange("p (a b) -> p a b", b=DH)),
                        reads=[n_pp], writes=[n_VA])
            h_lo = 0 if lat else 8
            yield
            nh_ = 10 - h_lo
            P.op("vector", lambda e, qf=qf, sq=sq, h_lo=h_lo: e.tensor_tensor(out=sq[:, h_lo:10, :], in0=qf[:, h_lo:10, :],
                                                                            in1=qf[:, h_lo:10, :], op=ALU.mult),
                 reads=[n_qf], writes=[n_sq])
            P.op("vector", lambda e, sq=sq, ss=ss, h_lo=h_lo: e.tensor_reduce(out=ss[:, h_lo:10], in_=sq[:, h_lo:10, :],
                                                                            axis=AX.X, op=ALU.add),
                 reads=[n_sq], writes=[n_ss])
            yield
            P.op("vector", lambda e, ss=ss: e.tensor_scalar(out=ss[:, 0:10], in0=ss[:, 0:10], scalar1=1.0 / DH, scalar2=EPS,
                                                            op0=ALU.mult, op1=ALU.add), reads=[n_ss], writes=[n_ss])
            P.op("gpsimd", lambda e, ss=ss: e.tensor_tensor(out=ss[:, 0:10], in0=ss[:, 0:10], in1=kb.nh2[:, 0:10], op=ALU.pow),
                 reads=[n_ss, kb.n_nh2], writes=[n_ss])
            yield
            P.op("vector", lambda e, qf=qf, ss=ss, h_lo=h_lo, nh_=nh_: e.tensor_tensor(
                out=qf[:, h_lo:10, :], in0=qf[:, h_lo:10, :], in1=ss[:, h_lo:10].unsqueeze(2).to_broadcast([128, nh_, 128]),
                op=ALU.mult), reads=[n_qf, n_ss], writes=[n_qf])
            yield
            if lat:
                P.op("vector", lambda e, qf=qf: e.tensor_tensor(out=qf[:], in0=qf[:], in1=gain[:], op=ALU.mult),
                     reads=[n_qf, n_gain], writes=[n_qf])
                (ra, n_ra) = rar.next()
                (rb, n_rb) = rbr.next()
                cb = cos[:, tt, :].unsqueeze(1).to_broadcast([128, 10, 64])
                sb_ = sin[:, tt, :].unsqueeze(1).to_broadcast([128, 10, 64])
                x1 = qf[:, :, 0:64]
                x2 = qf[:, :, 64:128]
                P.op("vector", lambda e, ra=ra, x1=x1, cb=cb: e.tensor_tensor(out=ra[:], in0=x1, in1=cb, op=ALU.mult),
                     reads=[n_qf, n_cos], writes=[n_ra])
                P.op("vector", lambda e, rb=rb, x2=x2, sb_=sb_: e.tensor_tensor(out=rb[:], in0=x2, in1=sb_, op=ALU.mult),
                     reads=[n_qf, n_sin], writes=[n_rb])
                P.op("vector", lambda e, qr=qr, ra=ra, rb=rb: e.tensor_tensor(out=qr[:, :, 0:64], in0=ra[:], in1=rb[:], op=ALU.subtract),
                     reads=[n_ra, n_rb], writes=[n_qr])
                yield
                (ra2, n_ra2) = rar.next()
                (rb2, n_rb2) = rbr.next()
                P.op("vector", lambda e, ra2=ra2, x1=x1, sb_=sb_: e.tensor_tensor(out=ra2[:], in0=x1, in1=sb_, op=ALU.mult),
                     reads=[n_qf, n_sin], writes=[n_ra2])
                P.op("vector", lambda e, rb2=rb2, x2=x2, cb=cb: e.tensor_tensor(out=rb2[:], in0=x2, in1=cb, op=ALU.mult),
                     reads=[n_qf, n_cos], writes=[n_rb2])
                P.op("vector", lambda e, qr=qr, ra2=ra2, rb2=rb2: e.tensor_tensor(out=qr[:, :, 64:128], in0=ra2[:], in1=rb2[:], op=ALU.add),
                     reads=[n_ra2, n_rb2], writes=[n_qr])
            else:
                P.op("vector", lambda e, qf=qf, qr=qr: e.tensor_tensor(out=qr[:, 8:10, :], in0=qf[:, 8:10, :], in1=gain[:, 8:10, :],
                                                                     op=ALU.mult), reads=[n_qf, n_gain], writes=[n_qr])
            yield
            groups = ([(0, 8, "q")] if lat else []) + [(8, 2, "k")]
            for (h0, n, kind) in groups:
                (pt, n_pt) = ptq.next()

                def tr(e, pt=pt, qr=qr, h0=h0, n=n):
                    ins = None
                    for j in range(n):
                        ins = e.transpose(out=pt[:, j * 128:(j + 1) * 128], in_=qr[:, h0 + j, :], identity=kb.ident[:])
                    return ins
                P.op("tensor", tr, reads=[n_qr, kb.n_ident], writes=[n_pt])
                if kind == "q":
                    P.op("scalar", lambda e, pt=pt, tt=tt: e.activation(
                        out=QT[:, :, tt * 128:(tt + 1) * 128], in_=pt[:].rearrange("p (a b) -> p a b", b=128), func=AF.Identity),
                        reads=[n_pt], writes=[n_QT])
                else:
                    P.op("vector", lambda e, pt=pt, tt=tt: e.tensor_copy(
                        out=KT[:, :, tt * 128:(tt + 1) * 128], in_=pt[:, 0:256].rearrange("p (a b) -> p a b", b=128)),
                        reads=[n_pt], writes=[n_KT])

        run_lanes((tile_gen(tt) for tt in range(18)), 2)
        P.barrier()
        P.flush()
    import os
    if os.environ.get("MIX_STOP") == "1":
        return
    with ExitStack() as st2:
        core = AttnCore(kb, st2, n_ps=2, n_pt=3)
        op_ = OutProj(kb, st2, T, T["ga_w_o"][0], M, xsrc, xdst, n_py=1)
        ocr = kb.ring(st2, "Ocat", [128, 4, 1024], BF16, 2)
        pos_ = [kb.ps(st2, "pO", [128, 512], F32) for _ in range(4)]
        rcr = kb.ring(st2, "rc", [128, 4], F32, 8)
        scale = DH ** -0.5
        for mq in range(4):
            (oc, n_oc) = ocr.next()
            for h in range(H):
                g = h // (H // HKV)
                def after(h=h, oc=oc, n_oc=n_oc):
                    for j in range(4):
                        (po, n_po) = pos_[j]
                        normalize_head(kb, po, n_po, DH, oc[:, j, h * DH:(h + 1) * DH], n_oc + "_%d" % j, rcr)
                for kt in range(18):
                    core.bank_wide(KT[:, g, kt * 128:(kt + 1) * 128], QT[:, h, mq * 512:(mq + 1) * 512], [n_KT, n_QT],
                                   VA[:, kt, g, 0:DH + 1], [n_VA], [(po[:, 0:DH + 1], n_po) for (po, n_po) in pos_],
                                   kt == 0, kt == 17, scale, after=(after if kt == 17 else None))
            core.flush_pending()
            for j in range(4):
                op_.run(oc[:, j, :], n_oc + "_%d" % j, mq * 4 + j)
        P.barrier()
        P.flush()


def normalize_head(kb, po, n_po, dv, oc_ap, n_oc, rcr, extra=None):
    P = kb.P
    (rc, n_rc) = rcr.next()
    if extra is not None:
        ex_ap, n_ex = extra
        P.op("vector", lambda e: e.tensor_tensor(out=rc[:, 1:2], in0=po[:, dv:dv + 1], in1=ex_ap, op=ALU.add),
             reads=[n_po, n_ex], writes=[n_rc])
        P.op("vector", lambda e: e.reciprocal(out=rc[:, 0:1], in_=rc[:, 1:2]), reads=[n_rc], writes=[n_rc])
    else:
        P.op("vector", lambda e: e.reciprocal(out=rc[:, 0:1], in_=po[:, dv:dv + 1]), reads=[n_po], writes=[n_rc])
    P.op("vector", lambda e: e.tensor_scalar(out=oc_ap, in0=po[:, 0:dv], scalar1=rc[:, 0:1], scalar2=None, op0=ALU.mult),
         reads=[n_po, n_rc], writes=[n_oc])


def stage_mixer_b(kb, st, T, M, xsrc, xdst):
    nc, P = kb.nc, kb.P
    H, HKV, DH = 16, 2, 64
    QT, n_QT = kb.sb(st, "QT2", [128, 8, NTOK], BF16)
    KT, n_KT = kb.sb(st, "KT2", [128, HKV, NTOK], BF16)
    VA, n_VA = kb.sb(st, "VA", [128, 18, HKV, DH + 2], BF16)
    P.op("gpsimd", lambda e: e.memset(VA[:, :, :, DH:DH + 1], 1.0), writes=[n_VA])
    with ExitStack() as st2:
        HT, n_HT = kb.sb(st2, "HT", [128, 8, NTOK], BF16)
        with ExitStack() as st3:
            stage_norm_T(kb, st3, xsrc, list(range(18)), M, 1, HT, n_HT)
            P.barrier()
            P.flush()
        wq, n_wq = load_w_bf16(kb, st2, "wqkv", T["sw_w_qkv_p"].rearrange("(kc p) n -> p kc n", p=128), 1280, piece=256)
        cos, n_cos = kb.sb(st2, "cos", [128, 16, 32], F32)
        sin, n_sin = kb.sb(st2, "sin", [128, 16, 32], F32)
        P.op("sync", lambda e: e.dma_start(out=cos[:], in_=T["sw_cos"].rearrange("(t p) d -> p t d", p=128)), writes=[n_cos], dma=True)
        P.op("sync", lambda e: e.dma_start(out=sin[:], in_=T["sw_sin"].rearrange("(t p) d -> p t d", p=128)), writes=[n_sin], dma=True)
        ppr = kb.ring(st2, "pproj", [128, 512], F32, 3, psum=True)
        ptq = kb.ring(st2, "ptq", [128, 1024], BF16, 2, psum=True)
        qfr = kb.ring(st2, "qf", [128, 18, 64], F32, 3)
        rar = kb.ring(st2, "ra", [128, 18, 32], F32, 4)
        rbr = kb.ring(st2, "rb", [128, 18, 32], F32, 4)
        qrr = kb.ring(st2, "qr", [128, 18, 64], BF16, 3)
        kdr = kb.ring(st2, "kd", [128, 2, 2, 64], BF16, 3)
        def tile_gen(tt):
            lat = tt < 16
            (qf, n_qf) = qfr.next()
            (qr, n_qr) = qrr.next()
            (kd, n_kd) = kdr.next()
            for (c0, ncol, h0) in [(0, 512, 0), (512, 512, 8), (1024, 256, 16)]:
                (pp, n_pp) = ppr.next()

                def mm(e, pp=pp, c0=c0, ncol=ncol, tt=tt):
                    ins = None
                    for kc in range(8):
                        ins = e.matmul(pp[:, 0:ncol], lhsT=HT[:, kc, tt * 128:(tt + 1) * 128], rhs=wq[:, kc, c0:c0 + ncol],
                                       start=(kc == 0), stop=(kc == 7))
                    return ins
                P.op("tensor", mm, reads=[n_HT] + [n_wq + "_%d" % i for i in range(c0 // 256, (c0 + ncol) // 256)], writes=[n_pp])
                if c0 < 1024:
                    P.op("scalar", lambda e, pp=pp, qf=qf, h0=h0: e.activation(
                        out=qf[:, h0:h0 + 8, :].rearrange("p a b -> p (a b)"), in_=pp[:], func=AF.Identity),
                        reads=[n_pp], writes=[n_qf])
                else:
                    P.op("scalar", lambda e, pp=pp, qf=qf: e.activation(
                        out=qf[:, 16:18, :].rearrange("p a b -> p (a b)"), in_=pp[:, 0:128], func=AF.Identity),
                        reads=[n_pp], writes=[n_qf])
                    P.op("scalar", lambda e, pp=pp, tt=tt: e.activation(
                        out=VA[:, tt, :, 0:DH], in_=pp[:, 128:256].rearrange("p (a b) -> p a b", b=DH), func=AF.Identity),
                        reads=[n_pp], writes=[n_VA])
            yield
            if lat:
                (ra, n_ra) = rar.next()
                (rb, n_rb) = rbr.next()
                cb = cos[:, tt, :].unsqueeze(1).to_broadcast([128, 18, 32])
                sb_ = sin[:, tt, :].unsqueeze(1).to_broadcast([128, 18, 32])
                x1 = qf[:, :, 0:32]
                x2 = qf[:, :, 32:64]
                P.op("vector", lambda e, ra=ra, x1=x1, cb=cb: e.tensor_tensor(out=ra[:], in0=x1, in1=cb, op=ALU.mult),
                     reads=[n_qf, n_cos], writes=[n_ra])
                P.op("vector", lambda e, rb=rb, x2=x2, sb_=sb_: e.tensor_tensor(out=rb[:], in0=x2, in1=sb_, op=ALU.mult),
                     reads=[n_qf, n_sin], writes=[n_rb])
                P.op("vector", lambda e, qr=qr, ra=ra, rb=rb: e.tensor_tensor(out=qr[:, :, 0:32], in0=ra[:], in1=rb[:], op=ALU.subtract),
                     reads=[n_ra, n_rb], writes=[n_qr])
                yield
                (ra2, n_ra2) = rar.next()
                (rb2, n_rb2) = rbr.next()
                P.op("vector", lambda e, ra2=ra2, x1=x1, sb_=sb_: e.tensor_tensor(out=ra2[:], in0=x1, in1=sb_, op=ALU.mult),
                     reads=[n_qf, n_sin], writes=[n_ra2])
                P.op("vector", lambda e, rb2=rb2, x2=x2, cb=cb: e.tensor_tensor(out=rb2[:], in0=x2, in1=cb, op=ALU.mult),
                     reads=[n_qf, n_cos], writes=[n_rb2])
                P.op("vector", lambda e, qr=qr, ra2=ra2, rb2=rb2: e.tensor_tensor(out=qr[:, :, 32:64], in0=ra2[:], in1=rb2[:], op=ALU.add),
                     reads=[n_ra2, n_rb2], writes=[n_qr])
            else:
                P.op("vector", lambda e, qr=qr, qf=qf: e.tensor_copy(out=qr[:], in_=qf[:]), reads=[n_qf], writes=[n_qr])
            yield
            for dup in range(2):
                P.op("scalar", lambda e, kd=kd, qr=qr, dup=dup: e.activation(out=kd[:, :, dup, :], in_=qr[:, 16:18, :], func=AF.Identity),
                     reads=[n_qr], writes=[n_kd])
            (pt, n_pt) = ptq.next()

            def tr(e, pt=pt, qr=qr):
                ins = None
                for p_ in range(8):
                    ins = e.transpose(out=pt[:, p_ * 128:(p_ + 1) * 128],
                                      in_=qr[:, 2 * p_:2 * p_ + 2, :].rearrange("p a b -> p (a b)"), identity=kb.ident[:])
                return ins
            P.op("tensor", tr, reads=[n_qr, kb.n_ident], writes=[n_pt])
            P.op("scalar", lambda e, pt=pt, tt=tt: e.activation(
                out=QT[:, :, tt * 128:(tt + 1) * 128], in_=pt[:].rearrange("p (a b) -> p a b", b=128), func=AF.Identity),
                reads=[n_pt], writes=[n_QT])
            (pt2, n_pt2) = ptq.next()

            def tr2(e, pt2=pt2, kd=kd):
                ins = None
                for g in range(2):
                    ins = e.transpose(out=pt2[:, g * 128:(g + 1) * 128], in_=kd[:, g, :, :].rearrange("p a b -> p (a b)"),
                                      identity=kb.ident[:])
                return ins
            P.op("tensor", tr2, reads=[n_kd, kb.n_ident], writes=[n_pt2])
            P.op("vector", lambda e, pt2=pt2, tt=tt: e.tensor_copy(
                out=KT[:, :, tt * 128:(tt + 1) * 128], in_=pt2[:, 0:256].rearrange("p (a b) -> p a b", b=128)),
                reads=[n_pt2], writes=[n_KT])
        run_lanes((tile_gen(tt) for tt in range(18)), 2)
        P.barrier()
        P.flush()
    with ExitStack() as st2:
        core = AttnCore(kb, st2, n_ps=2, n_pt=3)
        op_ = OutProj(kb, st2, T, T["sw_w_o"][0], M, xsrc, xdst, n_py=1)
        ocr = kb.ring(st2, "Ocat", [128, 1024], BF16, 2)
        pos_ = [kb.ps(st2, "pO", [128, 512], F32) for _ in range(4)]
        rcr = kb.ring(st2, "rc", [128, 4], F32, 8)
        esk, n_esk = kb.sb(st2, "esink", [128, 16], F32)
        P.op("sync", lambda e: e.dma_start(out=esk[:], in_=T["sw_sink"][0, :].partition_broadcast(128)), writes=[n_esk], dma=True)
        P.op("scalar", lambda e: e.activation(out=esk[:], in_=esk[:], func=AF.Exp), reads=[n_esk], writes=[n_esk])
        scale = DH ** -0.5
        for tt in range(18):
            (oc, n_oc) = ocr.next()
            if tt < 16:
                kl = []
                if tt - 1 >= 0:
                    kl.append((tt - 1, kb.masklo[:], kb.n_masklo))
                kl.append((tt, None, None))
                if tt + 1 <= 15:
                    kl.append((tt + 1, kb.maskhi[:], kb.n_maskhi))
                kl += [(16, None, None), (17, None, None)]
            else:
                kl = [(16, None, None), (17, None, None)]
            for g in range(HKV):
                for par in range(2):
                    b0 = par * 64
                    heads = [g * 8 + 2 * i + par for i in range(4)]

                    def after(heads=heads, oc=oc, n_oc=n_oc):
                        for i, h in enumerate(heads):
                            (po, n_po) = pos_[i]
                            normalize_head(kb, po, n_po, DH, oc[:, h * DH:(h + 1) * DH], n_oc, rcr, extra=(esk[:, h:h + 1], n_esk))
                    for idx, (kt, mk, n_mk) in enumerate(kl):
                        core.bank_wide(KT[b0:b0 + 64, g, kt * 128:(kt + 1) * 128],
                                       QT[b0:b0 + 64, g * 4:(g + 1) * 4, tt * 128:(tt + 1) * 128], [n_KT, n_QT],
                                       VA[:, kt, g, 0:DH + 1], [n_VA], [(po[:, 0:DH + 1], n_po) for (po, n_po) in pos_],
                                       idx == 0, idx == len(kl) - 1, scale,
                                       after=(after if idx == len(kl) - 1 else None),
                                       mult=((mk, n_mk) if mk is not None else None))
            core.flush_pending()
            op_.run(oc[:], n_oc, tt)
        P.barrier()
        P.flush()


NEG_BIAS = -30000.0


def na_structure():
    rows = S // GRID_W
    p = np.arange(128)
    combos = []
    keyl = []
    sig = {}
    for m in range(16):
        r = 2 * m + p // 64
        j = p % 64
        rs = np.clip(r - 4, 0, rows - 8)
        ws = np.clip(j - 8, 0, GRID_W - 16)
        lst = []
        for kt in range(16):
            kr = 2 * kt + p // 64
            kc = p % 64
            valid = ((kr[:, None] >= rs[None, :]) & (kr[:, None] < rs[None, :] + 8)
                     & (kc[:, None] >= ws[None, :]) & (kc[:, None] < ws[None, :] + 16))
            if not valid.any():
                continue
            ridx = np.clip(kr[:, None] - r[None, :] + 7, 0, 14)
            cidx = np.clip(kc[:, None] - j[None, :] + 15, 0, 30)
            ridx = np.where(valid, ridx, 0)
            cidx = np.where(valid, cidx, 0)
            key = (ridx.tobytes(), cidx.tobytes(), valid.tobytes())
            if key not in sig:
                sig[key] = len(combos)
                combos.append((ridx, cidx, valid))
            lst.append((kt, sig[key]))
        keyl.append(lst)
    return keyl, combos


def na_bias_host(rpb):
    keyl, combos = na_structure()
    rpb = np.asarray(rpb, dtype=np.float32)[0]
    out = np.empty((16, 128, len(combos), 128), dtype=np.float32)
    for ci, (ridx, cidx, valid) in enumerate(combos):
        g = rpb[:, ridx, cidx]
        out[:, :, ci, :] = np.where(valid[None], g, np.float32(NEG_BIAS))
    return out


def stage_mixer_a(kb, st, T, M, xsrc, xdst):
    nc, P = kb.nc, kb.P
    H, DH = 16, 64
    keyl, combos = na_structure()
    NCMB = len(combos)
    BIG, n_BIG = kb.sb(st, "HT_OC", [128, 8 * NTOK], BF16)
    HT, n_HT = BIG[:].rearrange("p (c t) -> p c t", c=8), n_BIG + "_ht"
    OC, n_OC = BIG[:].rearrange("p (t d) -> p t d", t=18), n_BIG + "_oc"
    stA = ExitStack()
    QT, n_QT = kb.sb(stA, "QT2", [128, 8, NTOK], BF16)
    KT, n_KT = kb.sb(stA, "KT2", [128, 8, NTOK], BF16)
    VA, n_VA = kb.sb(stA, "VA", [128, 18, H, DH + 2], BF16)
    P.op("gpsimd", lambda e: e.memset(VA[:, :, :, DH:DH + 1], 1.0), writes=[n_VA])
    scale = DH ** -0.5
    with ExitStack() as st2:
        with ExitStack() as st3:
            stage_norm_T(kb, st3, xsrc, list(range(18)), M, 1, HT, n_HT, rings=(4, 1, 4))
            P.barrier()
            P.flush()
        wr = kb.ring(st2, "wqkv", [128, 8, 512], BF16, 2)
        ppr = kb.ring(st2, "pproj", [128, 512], F32, 4, psum=True)
        wd = T["na_w_qkv"][0].rearrange("(kc p) n -> p kc n", p=128)
        macros = tok_macros(0, NTOK)
        ev = 0
        for piece in range(6):
            (w, n_w) = wr.next()
            P.op("gpsimd", lambda e, w=w, piece=piece: e.dma_start(out=w[:], in_=wd[:, :, piece * 512:(piece + 1) * 512]),
                 writes=[n_w], dma=True)
            if piece < 4:
                dst, n_dst = (QT, n_QT) if piece < 2 else (KT, n_KT)
                for pl in range(4):
                    pr = (piece % 2) * 4 + pl
                    for (t0, n) in macros:
                        (pp, n_pp) = ppr.next()

                        def mm(e, pp=pp, w=w, pl=pl, t0=t0, n=n):
                            ins = None
                            for kc in range(8):
                                ins = e.matmul(pp[:, 0:n], lhsT=w[:, kc, pl * 128:(pl + 1) * 128], rhs=HT[:, kc, t0:t0 + n],
                                               start=(kc == 0), stop=(kc == 7))
                            return ins
                        P.op("tensor", mm, reads=[n_w, n_HT], writes=[n_pp])
                        if piece < 2:
                            P.op("scalar", lambda e, pp=pp, pr=pr, t0=t0, n=n: e.activation(
                                out=QT[:, pr, t0:t0 + n], in_=pp[:, 0:n], func=AF.Identity, scale=scale),
                                reads=[n_pp], writes=[n_QT])
                        else:
                            P.op("vector", lambda e, pp=pp, pr=pr, t0=t0, n=n: e.tensor_copy(out=KT[:, pr, t0:t0 + n], in_=pp[:, 0:n]),
                                 reads=[n_pp], writes=[n_KT])
            else:
                h0 = (piece - 4) * 8
                for tt in range(18):
                    (pp, n_pp) = ppr.next()

                    def mm(e, pp=pp, w=w, tt=tt):
                        ins = None
                        for kc in range(8):
                            ins = e.matmul(pp[:], lhsT=HT[:, kc, tt * 128:(tt + 1) * 128], rhs=w[:, kc, :], start=(kc == 0), stop=(kc == 7))
                        return ins
                    P.op("tensor", mm, reads=[n_w, n_HT], writes=[n_pp])
                    if ev % 2 == 0:
                        P.op("scalar", lambda e, pp=pp, tt=tt, h0=h0: e.activation(
                            out=VA[:, tt, h0:h0 + 8, 0:DH], in_=pp[:].rearrange("p (a b) -> p a b", b=DH), func=AF.Identity),
                            reads=[n_pp], writes=[n_VA])
                    else:
                        P.op("vector", lambda e, pp=pp, tt=tt, h0=h0: e.tensor_copy(
                            out=VA[:, tt, h0:h0 + 8, 0:DH], in_=pp[:].rearrange("p (a b) -> p a b", b=DH)),
                            reads=[n_pp], writes=[n_VA])
                    ev += 1
        P.barrier()
        P.flush()
    with ExitStack() as st2:
        core = AttnCore(kb, st2, n_ps=4, n_pt=4)
        por = kb.ring(st2, "pO", [128, 512], F32, 2, psum=True)
        rcr = kb.ring(st2, "rc", [128, 4], F32, 4)
        br = kb.ring(st2, "nabias", [128, NCMB, 128], F32, 2)
        for h in range(H):
            (bt, n_bt) = br.next()
            P.op("sync", lambda e, bt=bt, h=h: e.dma_start(out=bt[:], in_=T["na_bias"][h]), writes=[n_bt], dma=True)
            P.op("scalar", lambda e, bt=bt: e.activation(out=bt[:], in_=bt[:], func=AF.Exp), reads=[n_bt], writes=[n_bt])
            b0 = (h % 2) * 64
            pr = h // 2
            for tt in range(18):
                if tt < 16:
                    kl = [(kt, bt[:, ci, :], n_bt) for (kt, ci) in keyl[tt]] + [(16, None, None), (17, None, None)]
                else:
                    kl = [(16, None, None), (17, None, None)]
                (po, n_po) = por.next()
                for k0 in range(0, len(kl), 4):
                    blocks = []
                    for idx in range(k0, min(k0 + 4, len(kl))):
                        kt, bias, n_bias = kl[idx]
                        blocks.append(dict(kT=KT[b0:b0 + 64, pr, kt * 128:(kt + 1) * 128], qT=QT[b0:b0 + 64, pr, tt * 128:(tt + 1) * 128],
                                           mult=bias, n_mult=n_bias, rd=[n_KT, n_QT],
                                           v=VA[:, kt, h, 0:DH + 1], rdv=[n_VA], po=po[:, 0:DH + 1], n_po=n_po,
                                           start=(idx == 0), stop=(idx == len(kl) - 1)))
                    last = (k0 + 4 >= len(kl))
                    core.bank(blocks, 1.0, after=((lambda po=po, n_po=n_po, h=h, tt=tt: normalize_head(
                        kb, po, n_po, DH, OC[:, tt, h * DH:(h + 1) * DH], n_OC + "_%d" % tt, rcr)) if last else None))
        core.flush_pending()
        P.barrier()
        P.flush()
    stA.close()
    with ExitStack() as st2:
        op_ = OutProj(kb, st2, T, T["na_w_o"][0], M, xsrc, xdst, post_bufs=2, n_py=4, n_pot=2, n_ot=2)
        run_lanes((op_.run_gen(OC[:, tt, :], n_OC + "_%d" % tt, tt) for tt in range(18)), 2)
        P.barrier()
        P.flush()


def stage_mixer_c(kb, st, T, M, xsrc, xdst):
    nc, P = kb.nc, kb.P
    CW = 31
    HW_ = CW // 2
    HT, n_HT = kb.sb(st, "HT", [128, 8, NTOK], BF16)
    with ExitStack() as st3:
        stage_norm_T(kb, st3, xsrc, list(range(18)), M, 1, HT, n_HT, rings=(4, 1, 4))
        P.barrier()
        P.flush()
    w1, n_w1 = load_w_bf16(kb, st, "w1", T["cv_w_pw1"][0].rearrange("(kc p) n -> p kc n", p=128), 2048)
    w2, n_w2 = load_w_bf16(kb, st, "w2", T["cv_w_pw2"][0].rearrange("(kc p) n -> p kc n", p=128), 1024)
    cvf, n_cvf = kb.sb(st, "cvf", [128, 8, 36], F32)
    b2, n_b2 = kb.sb(st, "b2", [128, 1024], F32)
    onesm, n_ones = kb.sb(st, "onesm", [128, 128], F32)
    U, n_U = kb.sb(st, "U", [128, 8, 512 + 2 * HW_], BF16)
    dgr = kb.ring(st, "Dg", [128, CW, 128], BF16, 2)
    sqr_ = kb.ring(st, "SQc", [128, 512], F32, 2)
    pV, n_pV = kb.ps(st, "pV", [128, 512], F32)
    V, n_V = kb.sb(st, "V", [128, 8, 512], F32)
    Z, n_Z = kb.sb(st, "Z", [128, 8, 512], BF16)
    sgr = kb.ring(st, "sig", [128, 512 + 2 * HW_], F32, 2)
    msq, n_msq = kb.sb(st, "msq", [128, 512], F32)
    rstd, n_rstd = kb.sb(st, "rstd", [128, 512], F32)
    ysr = kb.ring(st, "Ysb", [128, 1024], F32, 2)
    pA, n_pA = kb.ps(st, "pA", [128, 1024], F32)
    pB, n_pB = kb.ps(st, "pB", [128, 1024], F32)
    pM, n_pM = kb.ps(st, "pM", [128, 512], F32)
    pQ, n_pQ = kb.ps(st, "pQ", [128, 512], F32)
    pyr = kb.ring(st, "pY", [128, 512], F32, 1, psum=True)
    post = PostStage(kb, st, bufs=1)
    P.op("sync", lambda e: e.dma_start(out=cvf[:], in_=T["cv_fm"][:, :, :]), writes=[n_cvf], dma=True)
    P.op("sync", lambda e: e.dma_start(out=b2[:], in_=T["cv_b_pw2"][0, :].partition_broadcast(128)), writes=[n_b2], dma=True)
    P.op("gpsimd", lambda e: e.memset(onesm[:], 1.0 / D), writes=[n_ones])
    segs = [(t0, 512, 0, S) for t0 in range(0, S, 512)] + [(S, C, S, S + C)]
    for (t0, n, seq0, seq1) in segs:
        lo = max(t0 - HW_, seq0)
        hi = min(t0 + n + HW_, seq1)
        w = hi - lo
        off = lo - (t0 - HW_)
        if off > 0 or off + w < n + 2 * HW_:
            P.op("gpsimd", lambda e, n=n: e.memset(U[:, :, 0:n + 2 * HW_], 0.0), writes=[n_U])
        pieces = [(0, min(w, 512))] + ([(512, w - 512)] if w > 512 else [])
        for c in range(8):
            (sg, n_sg) = sgr.next()
            for (pp, n_pp, cbase) in ((pA, n_pA, 0), (pB, n_pB, 1024)):
                def mm(e, pp=pp, cbase=cbase, c=c, lo=lo, pieces=pieces):
                    ins = None
                    for (a, ln) in pieces:
                        for kc in range(8):
                            ins = e.matmul(pp[:, a:a + ln], lhsT=w1[:, kc, cbase + c * 128:cbase + (c + 1) * 128],
                                           rhs=HT[:, kc, lo + a:lo + a + ln], start=(kc == 0), stop=(kc == 7))
                    return ins
                P.op("tensor", mm, reads=[n_HT, n_w1 + "_%d" % ((cbase + c * 128) // 512)], writes=[n_pp])
            P.op("scalar", lambda e, sg=sg, c=c, w=w: e.activation(out=sg[:, 0:w], in_=pB[:, 0:w], func=AF.Sigmoid,
                                                                 bias=cvf[:, c, 1:2], scale=1.0),
                 reads=[n_pB, n_cvf], writes=[n_sg])
            P.op("vector", lambda e, sg=sg, c=c, w=w, off=off: e.scalar_tensor_tensor(
                out=U[:, c, off:off + w], in0=pA[:, 0:w], scalar=cvf[:, c, 0:1], in1=sg[:, 0:w], op0=ALU.add, op1=ALU.mult),
                reads=[n_pA, n_cvf, n_sg], writes=[n_U])
            (dg, n_dg) = dgr.next()
            for j in range(CW):
                if j % 2 == 0:
                    P.op("scalar", lambda e, dg=dg, c=c, j=j: e.activation(out=dg[:, j, :], in_=kb.ident[:], func=AF.Identity,
                                                                         scale=cvf[:, c, 5 + j:6 + j]),
                         reads=[kb.n_ident, n_cvf], writes=[n_dg + "_%d" % j])
                else:
                    P.op("vector", lambda e, dg=dg, c=c, j=j: e.tensor_scalar(out=dg[:, j, :], in0=kb.ident[:], scalar1=cvf[:, c, 5 + j:6 + j],
                                                                           scalar2=None, op0=ALU.mult),
                         reads=[kb.n_ident, n_cvf], writes=[n_dg + "_%d" % j])

            def mmc(e, dg=dg, c=c, n=n):
                ins = None
                for j in range(CW):
                    ins = e.matmul(pV[:, 0:n], lhsT=dg[:, j, :], rhs=U[:, c, j:j + n], start=(j == 0), stop=(j == CW - 1))
                return ins
            P.op("tensor", mmc, reads=[n_dg + "_%d" % j for j in range(CW)] + [n_U], writes=[n_pV])
            P.op("scalar", lambda e, c=c, n=n: e.activation(out=V[:, c, 0:n], in_=pV[:, 0:n], func=AF.Identity, bias=cvf[:, c, 2:3],
                                                           scale=1.0), reads=[n_pV, n_cvf], writes=[n_V])
        def mmM(e, n=n):
            ins = None
            for c in range(8):
                ins = e.matmul(pM[:, 0:n], lhsT=onesm[:], rhs=V[:, c, 0:n], start=(c == 0), stop=(c == 7))
            return ins
        P.op("tensor", mmM, reads=[n_ones, n_V], writes=[n_pM])
        for c in range(8):
            (sqc, n_sqc) = sqr_.next()
            P.op("scalar", lambda e, c=c, n=n, sqc=sqc: e.activation(out=sqc[:, 0:n], in_=V[:, c, 0:n], func=AF.Square),
                 reads=[n_V], writes=[n_sqc])
            P.op("tensor", lambda e, c=c, n=n, sqc=sqc: e.matmul(pQ[:, 0:n], lhsT=onesm[:], rhs=sqc[:, 0:n], start=(c == 0), stop=(c == 7)),
                 reads=[n_ones, n_sqc], writes=[n_pQ])
        P.op("scalar", lambda e, n=n: e.activation(out=msq[:, 0:n], in_=pM[:, 0:n], func=AF.Square), reads=[n_pM], writes=[n_msq])
        P.op("vector", lambda e, n=n: e.tensor_tensor(out=rstd[:, 0:n], in0=pQ[:, 0:n], in1=msq[:, 0:n], op=ALU.subtract),
             reads=[n_pQ, n_msq], writes=[n_rstd])
        P.op("vector", lambda e, n=n: e.tensor_scalar(out=rstd[:, 0:n], in0=rstd[:, 0:n], scalar1=EPS, scalar2=None, op0=ALU.add),
             reads=[n_rstd], writes=[n_rstd])
        P.op("scalar", lambda e, n=n: e.activation(out=rstd[:, 0:n], in_=rstd[:, 0:n], func=AF.Sqrt), reads=[n_rstd], writes=[n_rstd])
        P.op("vector", lambda e, n=n: e.reciprocal(out=rstd[:, 0:n], in_=rstd[:, 0:n]), reads=[n_rstd], writes=[n_rstd])
        for c in range(8):
            P.op("vector", lambda e, c=c, n=n: e.tensor_tensor(out=V[:, c, 0:n], in0=V[:, c, 0:n], in1=pM[:, 0:n], op=ALU.subtract),
                 reads=[n_V, n_pM], writes=[n_V])
            P.op("vector", lambda e, c=c, n=n: e.tensor_tensor(out=V[:, c, 0:n], in0=V[:, c, 0:n], in1=rstd[:, 0:n], op=ALU.mult),
                 reads=[n_V, n_rstd], writes=[n_V])
            P.op("scalar", lambda e, c=c, n=n: e.activation(out=Z[:, c, 0:n], in_=V[:, c, 0:n], func=AF.Silu,
                                                           scale=cvf[:, c, 3:4], bias=cvf[:, c, 4:5]),
                 reads=[n_V, n_cvf], writes=[n_Z])
        for jt in range(n // 128):
            (ys, n_ys) = ysr.next()
            for h in range(2):
                (py, n_py) = pyr.next()

                def mmy(e, py=py, jt=jt, h=h):
                    ins = None
                    for c in range(8):
                        ins = e.matmul(py[:], lhsT=Z[:, c, jt * 128:(jt + 1) * 128], rhs=w2[:, c, h * 512:(h + 1) * 512],
                                       start=(c == 0), stop=(c == 7))
                    return ins
                P.op("tensor", mmy, reads=[n_Z, n_w2 + "_%d" % h], writes=[n_py])
                P.op("vector", lambda e, ys=ys, py=py, h=h: e.tensor_tensor(out=ys[:, h * 512:(h + 1) * 512], in0=py[:],
                                                                           in1=b2[:, h * 512:(h + 1) * 512], op=ALU.add),
                     reads=[n_py, n_b2], writes=[n_ys + "_%d" % h])
            post.run([ys[:, 0:512], ys[:, 512:1024]], [n_ys + "_0", n_ys + "_1"], t0 // 128 + jt, M, 1, xsrc, xdst)
    P.barrier()
    P.flush()


COMMON_SPECS = {
    "mod_w": [1024, 9216], "mod_b": [9216], "norm_g": [6, 1024], "mod_b_fm": [128, 72], "norm_g_fm": [128, 48],
    "ffn_w_gate": [2, 1024, 2816], "ffn_w_up": [2, 1024, 2816], "ffn_w_down": [2, 2816, 1024],
}
MIXER_SPECS = {
    0: {"na_w_qkv": [1, 1024, 3072], "na_w_o": [1, 1024, 1024], "na_bias": [16, 128, 9, 128]},
    1: {"sw_w_qkv_p": [1024, 1280], "sw_w_o": [1, 1024, 1024], "sw_sink": [1, 16], "sw_cos": [S, 32], "sw_sin": [S, 32]},
    2: {"cv_w_pw1": [1, 1024, 2048], "cv_w_pw2": [1, 1024, 1024], "cv_fm": [128, 8, 36], "cv_b_pw2": [1, 1024]},
    3: {"ga_w_qkv_p": [1024, 1536], "ga_w_o": [1, 1024, 1024], "ga_gain": [1280], "ga_cos": [S, 64], "ga_sin": [S, 64]},
}
CORE_SPECS = {"x": [S, D], "ctx": [C, D], "cfm": [128, 8, 2]}


def shared_specs(layers):
    sp = {k: [len(layers)] + v for k, v in COMMON_SPECS.items()}
    for li in layers:
        sp.update(MIXER_SPECS[li % 4])
    return sp


def shared_for(sh, layers):
    out = {}
    for k, shp in shared_specs(layers).items():
        a = sh[k]
        if k in COMMON_SPECS:
            a = np.ascontiguousarray(a[list(layers)])
        assert list(a.shape) == shp, (k, a.shape, shp)
        out[k] = a
    return out


def host_shared(inp):
    f = lambda a: np.ascontiguousarray(np.asarray(a, dtype=np.float32))
    sh = {}
    sh["mod_w"] = f(inp["mod_w"])
    sh["mod_b"] = f(inp["mod_b"])
    sh["norm_g"] = f(inp["norm_g"])
    sh["mod_b_fm"] = f(np.asarray(inp["mod_b"]).reshape(4, 72, 128).transpose(0, 2, 1))
    sh["norm_g_fm"] = f(np.asarray(inp["norm_g"]).reshape(4, 48, 128).transpose(0, 2, 1))
    for k in ("ffn_w_gate", "ffn_w_up", "ffn_w_down"):
        sh[k] = f(inp[k])
    sh["na_w_qkv"] = f(inp["na_w_qkv"])
    sh["na_w_o"] = f(inp["na_w_o"])
    sh["na_bias"] = na_bias_host(inp["na_rpb"])
    perm64 = np.concatenate([np.arange(0, 64, 2), np.arange(1, 64, 2)])
    w = np.asarray(inp["sw_w_qkv"])[0]
    cols = np.concatenate([h * 64 + perm64 for h in range(18)] + [np.arange(1152, 1280)])
    sh["sw_w_qkv_p"] = f(w[:, cols])
    sh["sw_w_o"] = f(inp["sw_w_o"])
    sh["sw_sink"] = f(inp["sw_sink"])
    sh["sw_cos"], sh["sw_sin"] = rope_tables(64)
    sh["cv_w_pw1"] = f(inp["cv_w_pw1"])
    sh["cv_w_pw2"] = f(inp["cv_w_pw2"])
    sh["cv_b_pw2"] = f(inp["cv_b_pw2"])
    b1 = np.asarray(inp["cv_b_pw1"])[0]
    vecs = [b1[:1024], b1[1024:], np.asarray(inp["cv_b_dw"])[0], np.asarray(inp["cv_ln_g"])[0], np.asarray(inp["cv_ln_b"])[0]]
    vecs += [np.asarray(inp["cv_w_dw"])[0][j] for j in range(31)]
    sh["cv_fm"] = f(np.stack([v.reshape(8, 128).T for v in vecs], axis=-1))
    perm = np.concatenate([np.arange(0, 128, 2), np.arange(1, 128, 2)])
    w = np.asarray(inp["ga_w_qkv"])[0]
    cols = np.concatenate([h * 128 + perm for h in range(10)] + [np.arange(1280, 1536)])
    sh["ga_w_qkv_p"] = f(w[:, cols])
    sh["ga_w_o"] = f(inp["ga_w_o"])
    qn = np.asarray(inp["ga_q_norm"])[0][perm]
    kn = np.asarray(inp["ga_k_norm"])[0][perm]
    sh["ga_gain"] = f(np.concatenate([qn] * 8 + [kn] * 2))
    cs, sn = rope_tables(128)
    sh["ga_cos"], sh["ga_sin"] = cs, sn
    return sh


def rope_tables(head_dim):
    t = np.arange(S)
    row = (t // GRID_W).astype(np.float32)
    col = (t % GRID_W).astype(np.float32)
    n = head_dim // 4
    freq = np.power(np.float32(10000.0), -(np.arange(n, dtype=np.float32) / np.float32(n))).astype(np.float32)
    ang = np.concatenate([row[:, None] * freq[None, :], col[:, None] * freq[None, :]], axis=-1).astype(np.float32)
    return np.cos(ang).astype(np.float32), np.sin(ang).astype(np.float32)


def host_core(inp, b):
    f = lambda a: np.ascontiguousarray(np.asarray(a, dtype=np.float32))
    c = np.asarray(inp["c"])[b].reshape(8, 128).T
    cc = np.asarray(inp["c_ctx"]).reshape(8, 128).T
    return {"x": f(inp["x"][b]), "ctx": f(inp["ctx"][b]), "cfm": f(np.stack([c, cc], axis=-1))}


def layer_plan(li):
    last = li == 3
    return [("mod", li), ("ffn", li, 0, True), ("mixer", li), ("ffn", li, 1, not last)]


def build_program(plan, layers):
    nc = bass.Bass("TRN2", target_bir_lowering=False)
    T = {}
    for k, shp in list(shared_specs(layers).items()) + list(CORE_SPECS.items()):
        T[k] = nc.dram_tensor(k, shp, F32, kind="ExternalInput").ap()
    out = nc.dram_tensor("out", [S, D], F32, kind="ExternalOutput").ap()
    outc = nc.dram_tensor("outc", [C, D], F32, kind="ExternalOutput").ap()
    XL = nc.dram_tensor("XL", [S, D], F32, kind="Internal").ap()
    XC = nc.dram_tensor("XC", [C, D], F32, kind="Internal").ap()

    def xs(tt):
        if tt < 16:
            return XL[tt * 128:(tt + 1) * 128, :], "XL_%d" % tt
        return XC[(tt - 16) * 128:(tt - 15) * 128, :], "XC_%d" % (tt - 16)

    with ExitStack() as st:
        kb = KB(nc, st)
        P = kb.P
        stage_consts(kb, st)
        for tt in range(18):
            dst, n_dst = xs(tt)
            src = T["x"][tt * 128:(tt + 1) * 128, :] if tt < 16 else T["ctx"][(tt - 16) * 128:(tt - 15) * 128, :]
            P.op("sync", lambda e, dst=dst, src=src: e.dma_start(out=dst, in_=src), writes=[n_dst], dma=True)
        M = None
        lst = None
        for step in plan:
            lloc = list(layers).index(step[1])
            if step[0] == "mod":
                if lst is not None:
                    P.barrier()
                    P.flush()
                    lst.close()
                lst = ExitStack()
                with ExitStack() as st2:
                    M = stage_mod(kb, st2, T, lloc, lst)
                    P.barrier()
                    P.flush()
            elif step[0] == "ffn":
                which = step[2]
                tiles = list(range(18)) if step[3] else list(range(16))
                with ExitStack() as st2:
                    stage_ffn(kb, st2, T, lloc, which, tiles, M, 0 if which == 0 else 2, xs, xs)
            elif step[0] == "mixer":
                kind = step[1] % 4
                with ExitStack() as st2:
                    [stage_mixer_a, stage_mixer_b, stage_mixer_c, stage_mixer_d][kind](kb, st2, T, M, xs, xs)
            else:
                raise ValueError(step)
        for tt in range(18):
            src, n_src = xs(tt)
            dst = out[tt * 128:(tt + 1) * 128, :] if tt < 16 else outc[(tt - 16) * 128:(tt - 15) * 128, :]
            ev = P.op("sync", lambda e, dst=dst, src=src: e.dma_start(out=dst, in_=src), reads=[n_src], dma=True)
            P.out_evs.append(ev)
        waits = {}
        for ev in P.out_evs:
            P._need("sync", ev, waits)
        P.ops["sync"].append((None, list(waits.items()), None))
        P.barrier()
        P.flush()
        if lst is not None:
            lst.close()
    return nc


FUSED = True


def kernel(**inputs):
    sh = host_shared(inputs)
    cores = [host_core(inputs, b) for b in range(8)]
    groups = [[0, 1, 2, 3]] if FUSED else [[0], [1], [2], [3]]
    for layers in groups:
        plan = []
        for li in layers:
            plan += layer_plan(li)
        nc = build_program(plan, layers)
        shl = shared_for(sh, layers)
        res = run_bass_kernel_spmd(nc, [{**shl, **cores[b]} for b in range(8)], core_ids=list(range(8)))
        for b in range(8):
            cores[b]["x"] = np.ascontiguousarray(res.results[b]["out"], dtype=np.float32)
            cores[b]["ctx"] = np.ascontiguousarray(res.results[b]["outc"], dtype=np.float32)
    return np.stack([cores[b]["x"] for b in range(8)], axis=0).astype(np.float32)
```

```python
import numpy as np
from contextlib import ExitStack
import concourse.bass as bass
import concourse.mybir as mybir
from concourse.bass_utils import run_bass_kernel_spmd

F32 = mybir.dt.float32
BF16 = mybir.dt.bfloat16
AF = mybir.ActivationFunctionType
ALU = mybir.AluOpType
AX = mybir.AxisListType

D = 1024
S = 2048
C = 256
NTOK = S + C
DFF = 2816
EPS = 1e-6
KC = D // 128
GRID_W = 64


class Prog:
    ENGS = ["tensor", "vector", "scalar", "gpsimd", "sync"]

    def __init__(self, nc, stack, n_dma_sems=48):
        self.nc = nc
        self.ops = {e: [] for e in self.ENGS}
        self.count = {e: 0 for e in self.ENGS}
        self.waited = {e: {} for e in self.ENGS}
        self.last_w = {}
        self.readers = {}
        self.n_dma_sems = n_dma_sems
        self.dma_i = 0
        self.dma_j = 0
        self.dma_sem_use = [0] * n_dma_sems
        self.sems = {e: stack.enter_context(nc.semaphore("s_" + e)) for e in self.ENGS}
        self.dsems = [stack.enter_context(nc.semaphore("d_%d" % i)) for i in range(n_dma_sems)]
        self.out_evs = []

    def _need(self, eng, ev, waits):
        if ev is None:
            return
        if ev[0] == "e" and ev[1] == eng and eng == "tensor":
            return
        key = (ev[0], ev[1])
        if self.waited[eng].get(key, 0) >= ev[2]:
            return
        self.waited[eng][key] = ev[2]
        waits[key] = max(waits.get(key, 0), ev[2])

    def op(self, eng, fn, reads=(), writes=(), dma=False):
        writes = list(writes) + [r for r in reads if r.startswith("PSUM_")]
        reads = [r for r in reads if not r.startswith("PSUM_")]
        waits = {}
        for r in reads:
            self._need(eng, self.last_w.get(r), waits)
        for w in writes:
            self._need(eng, self.last_w.get(w), waits)
            for ev in self.readers.get(w, ()):
                self._need(eng, ev, waits)
        if dma:
            half = self.n_dma_sems // 2
            if eng == "gpsimd":
                si = half + self.dma_j % half
                self.dma_j += 1
            else:
                si = self.dma_i % half
                self.dma_i += 1
            if self.dma_sem_use[si] > 0:
                self._need(eng, ("d", si, 16 * self.dma_sem_use[si]), waits)
            self.dma_sem_use[si] += 1
            ev = ("d", si, 16 * self.dma_sem_use[si])
        else:
            self.count[eng] += 1
            ev = ("e", eng, self.count[eng])
        self.ops[eng].append((fn, list(waits.items()), ev))
        for r in reads:
            self.readers.setdefault(r, []).append(ev)
        for w in writes:
            self.last_w[w] = ev
            self.readers[w] = []
        return ev

    def barrier(self):
        evs = [("e", e, self.count[e]) for e in self.ENGS if self.count[e] > 0]
        evs += [("d", si, 16 * u) for si, u in enumerate(self.dma_sem_use) if u > 0]
        for e in self.ENGS:
            waits = {}
            for ev in evs:
                if ev[0] == "e" and ev[1] == e:
                    continue
                self._need(e, ev, waits)
            if waits:
                self.ops[e].append((None, list(waits.items()), None))
        self.last_w = {}
        self.readers = {}

    def flush(self):
        nc = self.nc
        with nc.Block() as block:
            def run(engname):
                def body(eng):
                    for fn, waits, ev in self.ops[engname]:
                        for (kind, k), val in waits:
                            sem = self.sems[k] if kind == "e" else self.dsems[k]
                            eng.wait_ge(sem, val)
                        if fn is None:
                            continue
                        ins = fn(eng)
                        if ev[0] == "e":
                            ins.then_inc(self.sems[ev[1]], 1)
                        else:
                            ins.then_inc(self.dsems[ev[1]], 16)
                return body
            block.tensor(run("tensor"))
            block.vector(run("vector"))
            block.scalar(run("scalar"))
            block.gpsimd(run("gpsimd"))
            block.sync(run("sync"))
        self.ops = {e: [] for e in self.ENGS}


class Ring:
    def __init__(self, items):
        self.items = items
        self.i = 0

    def next(self):
        it = self.items[self.i % len(self.items)]
        self.i += 1
        return it


class KB:
    def __init__(self, nc, st):
        self.nc = nc
        self.st = st
        self.P = Prog(nc, st)
        self.uid = 0

    def sb(self, st, name, shape, dt):
        self.uid += 1
        nm = "%s_%d" % (name, self.uid)
        return st.enter_context(self.nc.sbuf_tensor(nm, list(shape), dt)), nm

    def ps(self, st, name, shape, dt):
        self.uid += 1
        nm = "PSUM_%s_%d" % (name, self.uid)
        return st.enter_context(self.nc.psum_tensor(nm, list(shape), dt)), nm

    def ring(self, st, name, shape, dt, n, psum=False):
        return Ring([(self.ps if psum else self.sb)(st, name, shape, dt) for _ in range(n)])


def run_lanes(gens, width):
    active = []
    it = iter(gens)
    more = True
    while True:
        while more and len(active) < width:
            try:
                active.append(next(it))
            except StopIteration:
                more = False
        if not active:
            break
        for g in list(active):
            try:
                next(g)
            except StopIteration:
                active.remove(g)


def tok_macros(t0, ntok, width=512):
    out = []
    o = 0
    while o < ntok:
        n = min(width, ntok - o)
        out.append((t0 + o, n))
        o += n
    return out


def stage_consts(kb, st):
    nc, P = kb.nc, kb.P
    identf, n_if = kb.sb(st, "identf", [128, 128], F32)
    ident, n_i = kb.sb(st, "ident", [128, 128], BF16)
    nh, n_nh = kb.sb(st, "neghalf", [128, 8], F32)
    P.op("gpsimd", lambda e: e.memset(identf[:], 1.0), writes=[n_if])
    P.op("gpsimd", lambda e: e.affine_select(out=identf[:], in_=identf[:], pattern=[[-1, 128]],
                                              compare_op=ALU.is_equal, fill=0.0, base=0, channel_multiplier=1),
         reads=[n_if], writes=[n_if])
    P.op("vector", lambda e: e.tensor_copy(out=ident[:], in_=identf[:]), reads=[n_if], writes=[n_i])
    P.op("gpsimd", lambda e: e.memset(nh[:], -0.5), writes=[n_nh])
    kb.ident, kb.n_ident = ident, n_i
    kb.identf, kb.n_identf = identf, n_if
    kb.nh, kb.n_nh = nh, n_nh
    NEGB = -30000.0
    for nm, pat, cm in (("masklo", [[-1, 128]], 1), ("maskhi", [[1, 128]], -1)):
        mf, n_mf = kb.sb(st, nm + "f", [128, 128], F32)
        mb, n_mb = kb.sb(st, nm, [128, 128], BF16)
        P.op("gpsimd", lambda e, mf=mf: e.memset(mf[:], 1.0), writes=[n_mf])
        P.op("gpsimd", lambda e, mf=mf, pat=pat, cm=cm: e.affine_select(out=mf[:], in_=mf[:], pattern=pat, compare_op=ALU.is_ge,
                                                                        fill=0.0, base=0, channel_multiplier=cm),
             reads=[n_mf], writes=[n_mf])
        P.op("vector", lambda e, mf=mf, mb=mb: e.tensor_copy(out=mb[:], in_=mf[:]), reads=[n_mf], writes=[n_mb])
        setattr(kb, nm, mb)
        setattr(kb, "n_" + nm, n_mb)
    nh2, n_nh2 = kb.sb(st, "neghalf2", [128, 16], F32)
    P.op("gpsimd", lambda e: e.memset(nh2[:], -0.5), writes=[n_nh2])
    kb.nh2, kb.n_nh2 = nh2, n_nh2


def rstd_from_ssq(kb, ssq, n_ssq, ncols, inv_n):
    P = kb.P
    P.op("vector", lambda e: e.tensor_scalar(out=ssq[:, 0:ncols], in0=ssq[:, 0:ncols], scalar1=inv_n, scalar2=EPS,
                                             op0=ALU.mult, op1=ALU.add), reads=[n_ssq], writes=[n_ssq])
    P.op("gpsimd", lambda e: e.tensor_tensor(out=ssq[:, 0:ncols], in0=ssq[:, 0:ncols], in1=kb.nh[:, 0:ncols],
                                             op=ALU.pow), reads=[n_ssq, kb.n_nh], writes=[n_ssq])


def stage_mod(kb, st, T, li, lst):
    nc, P = kb.nc, kb.P
    Afm, n_A = kb.sb(lst, "Afm", [128, 3, 2, 8], F32)
    Bfm, n_B = kb.sb(lst, "Bfm", [128, 3, 2, 8], F32)
    Gbc, n_G = kb.sb(lst, "Gbc", [128, 3, 2, 1024], F32)
    cf, n_cf = kb.sb(st, "cf", [128, 8, 2], F32)
    sc, n_sc = kb.sb(st, "sc", [128, 8, 2], F32)
    sbc, n_sbc = kb.sb(st, "sbc", [128, 2, 8, 128], F32)
    mbfm, n_mbfm = kb.sb(st, "mbfm", [128, 72], F32)
    gfm, n_gfm = kb.sb(st, "gfm", [128, 48], F32)
    modfm, n_modfm = kb.sb(st, "modfm", [128, 9, 8, 2], F32)
    wring = kb.ring(st, "modw", [128, 8, 512], F32, 2)
    bring = kb.ring(st, "modbb", [128, 512], F32, 2)
    gring = kb.ring(st, "gpost", [128, 512], F32, 2)
    tring = kb.ring(st, "modtmp", [128, 512], F32, 2)
    pfm, n_pfm = kb.ps(st, "pfm", [128, 9, 8, 2], F32)
    pbr = kb.ring(st, "pbc", [128, 512], F32, 2, psum=True)

    P.op("sync", lambda e: e.dma_start(out=cf[:], in_=T["cfm"][:, :, :]), writes=[n_cf], dma=True)
    P.op("sync", lambda e: e.dma_start(out=mbfm[:], in_=T["mod_b_fm"][li]), writes=[n_mbfm], dma=True)
    P.op("sync", lambda e: e.dma_start(out=gfm[:], in_=T["norm_g_fm"][li]), writes=[n_gfm], dma=True)
    P.op("scalar", lambda e: e.activation(out=sc[:], in_=cf[:], func=AF.Silu), reads=[n_cf], writes=[n_sc])
    for s in range(2):
        for kc in range(8):
            P.op("vector", lambda e, s=s, kc=kc: e.tensor_copy(out=sbc[:, s, kc, :],
                                                               in_=sc[:, kc, s:s + 1].to_broadcast([128, 128])),
                 reads=[n_sc], writes=[n_sbc])
    mw = T["mod_w"][li].rearrange("(kc p) n -> p kc n", p=128)
    wsub = [0.5, 1.0, 0.5]
    for slot in range(9):
        sub = slot // 3
        for half in range(2):
            col0 = slot * 1024 + half * 512
            (wt, n_wt) = wring.next()
            P.op("sync", lambda e, wt=wt, col0=col0: e.dma_start(out=wt[:], in_=mw[:, :, col0:col0 + 512]),
                 writes=[n_wt], dma=True)
            if slot % 3 != 2:
                def mm(e, wt=wt, slot=slot, half=half):
                    ins = None
                    for oc in range(4):
                        for kc in range(8):
                            ins = e.matmul(pfm[:, slot, half * 4 + oc, :], lhsT=wt[:, kc, oc * 128:(oc + 1) * 128],
                                           rhs=sc[:, kc, :], start=(kc == 0), stop=(kc == 7))
                    return ins
                P.op("tensor", mm, reads=[n_wt, n_sc], writes=[n_pfm])
            else:
                (bt, n_bt) = bring.next()
                (gt, n_gt) = gring.next()
                P.op("sync", lambda e, bt=bt, col0=col0: e.dma_start(
                    out=bt[:], in_=T["mod_b"][li, col0:col0 + 512].partition_broadcast(128)), writes=[n_bt], dma=True)
                P.op("sync", lambda e, gt=gt, sub=sub, half=half: e.dma_start(
                    out=gt[:], in_=T["norm_g"][li, 2 * sub + 1, half * 512:(half + 1) * 512].partition_broadcast(128)),
                    writes=[n_gt], dma=True)
                for s in range(2):
                    (pb, n_pb) = pbr.next()
                    (tt, n_tt) = tring.next()

                    def mm(e, wt=wt, s=s, pb=pb):
                        ins = None
                        for kc in range(8):
                            ins = e.matmul(pb[:], lhsT=sbc[:, s, kc, :], rhs=wt[:, kc, :], start=(kc == 0), stop=(kc == 7))
                        return ins
                    P.op("tensor", mm, reads=[n_wt, n_sbc], writes=[n_pb])
                    P.op("vector", lambda e, pb=pb, bt=bt, tt=tt: e.tensor_tensor(out=tt[:], in0=pb[:], in1=bt[:], op=ALU.add),
                         reads=[n_pb, n_bt], writes=[n_tt])
                    P.op("vector", lambda e, tt=tt, gt=gt, sub=sub, s=s, half=half: e.scalar_tensor_tensor(
                        out=Gbc[:, sub, s, half * 512:(half + 1) * 512], in0=tt[:], scalar=wsub[sub], in1=gt[:],
                        op0=ALU.mult, op1=ALU.mult), reads=[n_tt, n_gt], writes=[n_G])
    for slot in range(9):
        if slot % 3 == 2:
            continue
        for s in range(2):
            P.op("vector", lambda e, s=s, slot=slot: e.tensor_tensor(out=modfm[:, slot, :, s], in0=pfm[:, slot, :, s],
                                                                     in1=mbfm[:, slot * 8:slot * 8 + 8], op=ALU.add),
                 reads=[n_pfm, n_mbfm], writes=[n_modfm])
    for sub in range(3):
        for s in range(2):
            P.op("vector", lambda e, sub=sub, s=s: e.scalar_tensor_tensor(
                out=Afm[:, sub, s, :], in0=modfm[:, 3 * sub + 1, :, s], scalar=1.0,
                in1=gfm[:, (2 * sub) * 8:(2 * sub) * 8 + 8], op0=ALU.add, op1=ALU.mult),
                reads=[n_modfm, n_gfm], writes=[n_A])
            P.op("vector", lambda e, sub=sub, s=s: e.tensor_copy(out=Bfm[:, sub, s, :], in_=modfm[:, 3 * sub, :, s]),
                 reads=[n_modfm], writes=[n_B])
    return dict(Afm=Afm, n_A=n_A, Bfm=Bfm, n_B=n_B, Gbc=Gbc, n_G=n_G)


def stage_norm_T(kb, st, xsrc, tiles, M, sub, HT, n_HT, psum_ring=None, rings=(6, 2, 8)):
    nc, P = kb.nc, kb.P
    xr = kb.ring(st, "nx", [128, 1024], F32, rings[0])
    jr = kb.ring(st, "njunk", [128, 1024], BF16, rings[1])
    xnr = kb.ring(st, "nxn", [128, 1024], BF16, rings[2])
    sr = kb.ring(st, "nssq", [128, 8], F32, 2)
    ptr = psum_ring or kb.ring(st, "nptr", [128, 512], BF16, 2, psum=True)
    ev = 0
    for m0 in range(0, len(tiles), 4):
        grp = tiles[m0:m0 + 4]
        n = len(grp)
        s = 0 if grp[0] < 16 else 1
        assert all((t < 16) == (grp[0] < 16) for t in grp)
        (ssq, n_ssq) = sr.next()
        xts = []
        for j, tt in enumerate(grp):
            (xt, n_xt) = xr.next()
            (jk, n_jk) = jr.next()
            src, n_src = xsrc(tt)
            P.op("sync", lambda e, xt=xt, src=src: e.dma_start(out=xt[:], in_=src), reads=[n_src], writes=[n_xt], dma=True)
            P.op("scalar", lambda e, xt=xt, jk=jk, ssq=ssq, j=j: e.activation(out=jk[:], in_=xt[:], func=AF.Square,
                                                                               accum_out=ssq[:, j:j + 1]),
                 reads=[n_xt], writes=[n_jk, n_ssq])
            xts.append((xt, n_xt))
        rstd_from_ssq(kb, ssq, n_ssq, n, 1.0 / D)
        xns = []
        for j, tt in enumerate(grp):
            (xn, n_xn) = xnr.next()
            xt, n_xt = xts[j]
            if j % 2 == 0:
                P.op("vector", lambda e, xn=xn, xt=xt, ssq=ssq, j=j: e.tensor_scalar(
                    out=xn[:], in0=xt[:], scalar1=ssq[:, j:j + 1], scalar2=None, op0=ALU.mult),
                    reads=[n_xt, n_ssq], writes=[n_xn])
            else:
                P.op("scalar", lambda e, xn=xn, xt=xt, ssq=ssq, j=j: e.activation(
                    out=xn[:], in_=xt[:], func=AF.Identity, scale=ssq[:, j:j + 1]),
                    reads=[n_xt, n_ssq], writes=[n_xn])
            xns.append((xn, n_xn))
        for kc in range(8):
            (pt, n_pt) = ptr.next()

            def tr(e, pt=pt, kc=kc, xns=xns):
                ins = None
                for j, (xn, _) in enumerate(xns):
                    ins = e.transpose(out=pt[:, j * 128:(j + 1) * 128], in_=xn[:, kc * 128:(kc + 1) * 128],
                                      identity=kb.ident[:])
                return ins
            P.op("tensor", tr, reads=[nm for _, nm in xns] + [kb.n_ident], writes=[n_pt])
            c0 = m0 * 128
            if ev % 2 == 0:
                P.op("scalar", lambda e, pt=pt, kc=kc, c0=c0, n=n, s=s: e.activation(
                    out=HT[:, kc, c0:c0 + n * 128], in_=pt[:, 0:n * 128], func=AF.Identity,
                    scale=M["Afm"][:, sub, s, kc:kc + 1], bias=M["Bfm"][:, sub, s, kc:kc + 1]),
                    reads=[n_pt, M["n_A"], M["n_B"]], writes=[n_HT])
            else:
                P.op("vector", lambda e, pt=pt, kc=kc, c0=c0, n=n, s=s: e.tensor_scalar(
                    out=HT[:, kc, c0:c0 + n * 128], in0=pt[:, 0:n * 128], scalar1=M["Afm"][:, sub, s, kc:kc + 1],
                    scalar2=M["Bfm"][:, sub, s, kc:kc + 1], op0=ALU.mult, op1=ALU.add),
                    reads=[n_pt, M["n_A"], M["n_B"]], writes=[n_HT])
            ev += 1


class PostStage:
    def __init__(self, kb, st, bufs=2):
        self.kb = kb
        self.xr = kb.ring(st, "px", [128, 1024], F32, bufs)
        self.jr = kb.ring(st, "pjunk", [128, 1024], BF16, bufs)
        self.tr = kb.ring(st, "ptmp", [128, 1024], F32, bufs)
        self.sr = kb.ring(st, "pssq", [128, 8], F32, 6)

    def run(self, *a, **k):
        for _ in self.run_gen(*a, **k):
            pass

    def run_gen(self, y_ap_halves, y_names, tt, M, sub, xsrc, xdst, is_out=False):
        kb = self.kb
        P = kb.P
        s = 0 if tt < 16 else 1
        (xt, n_xt) = self.xr.next()
        (jk, n_jk) = self.jr.next()
        (tm, n_tm) = self.tr.next()
        (ssq, n_ssq) = self.sr.next()
        src, n_src = xsrc(tt)
        dst, n_dst = xdst(tt)
        P.op("sync", lambda e: e.dma_start(out=xt[:], in_=src), reads=[n_src], writes=[n_xt], dma=True)
        for h, yh in enumerate(y_ap_halves):
            P.op("scalar", lambda e, h=h, yh=yh: e.activation(out=jk[:, h * 512:(h + 1) * 512], in_=yh, func=AF.Square,
                                                             accum_out=ssq[:, h:h + 1]),
                 reads=[y_names[h]], writes=[n_jk, n_ssq])
        yield
        P.op("vector", lambda e: e.tensor_tensor(out=ssq[:, 2:3], in0=ssq[:, 0:1], in1=ssq[:, 1:2], op=ALU.add),
             reads=[n_ssq], writes=[n_ssq])
        yield
        P.op("vector", lambda e: e.tensor_scalar(out=ssq[:, 3:4], in0=ssq[:, 2:3], scalar1=1.0 / D, scalar2=EPS,
                                                 op0=ALU.mult, op1=ALU.add), reads=[n_ssq], writes=[n_ssq])
        yield
        P.op("gpsimd", lambda e: e.tensor_tensor(out=ssq[:, 4:5], in0=ssq[:, 3:4], in1=kb.nh[:, 0:1], op=ALU.pow),
             reads=[n_ssq, kb.n_nh], writes=[n_ssq])
        yield
        for h, yh in enumerate(y_ap_halves):
            P.op("vector", lambda e, h=h, yh=yh: e.scalar_tensor_tensor(
                out=tm[:, h * 512:(h + 1) * 512], in0=yh, scalar=ssq[:, 4:5],
                in1=M["Gbc"][:, sub, s, h * 512:(h + 1) * 512], op0=ALU.mult, op1=ALU.mult),
                reads=[y_names[h], n_ssq, M["n_G"]], writes=[n_tm])
        yield
        P.op("vector", lambda e: e.tensor_tensor(out=tm[:], in0=tm[:], in1=xt[:], op=ALU.add),
             reads=[n_tm, n_xt], writes=[n_tm])
        ev = P.op("sync", lambda e: e.dma_start(out=dst, in_=tm[:]), reads=[n_tm], writes=[n_dst], dma=True)
        if is_out:
            P.out_evs.append(ev)


def stage_ffn(kb, st, T, li, which, tiles, M, sub, xsrc, xdst, is_out=False):
    nc, P = kb.nc, kb.P
    nt = len(tiles)
    ntok = nt * 128
    HT, n_HT = kb.sb(st, "HT", [128, 8, ntok], BF16)
    Y, n_Y = kb.sb(st, "Y", [128, nt, 1024], F32)
    with ExitStack() as st2:
        stage_norm_T(kb, st2, xsrc, tiles, M, sub, HT, n_HT)
        P.barrier()
        P.flush()
    with ExitStack() as st3:
        FG = 256
        ngrp = DFF // FG
        wgr = kb.ring(st3, "wg", [128, 8, FG], BF16, 2)
        wur = kb.ring(st3, "wu", [128, 8, FG], BF16, 2)
        wdr = kb.ring(st3, "wd", [128, FG // 128, 1024], BF16, 2)
        actr = kb.ring(st3, "actT", [128, FG // 128, ntok], BF16, 2)
        sgr = kb.ring(st3, "sg", [128, 512], F32, 3)
        pgr = kb.ring(st3, "pg", [128, 512], F32, 2, psum=True)
        pur = kb.ring(st3, "pu", [128, 512], F32, 2, psum=True)
        pdr = kb.ring(st3, "pd", [128, 512], F32, 3, psum=True)
        wgd = T["ffn_w_gate"][li, which].rearrange("(kc p) n -> p kc n", p=128)
        wud = T["ffn_w_up"][li, which].rearrange("(kc p) n -> p kc n", p=128)
        wdd = T["ffn_w_down"][li, which].rearrange("(fc p) n -> p fc n", p=128)
        macros = tok_macros(0, ntok)
        for g in range(ngrp):
            f0 = g * FG
            (wg, n_wg) = wgr.next()
            (wu, n_wu) = wur.next()
            (wd, n_wd) = wdr.next()
            (act, n_act) = actr.next()
            P.op("gpsimd", lambda e, wg=wg, f0=f0: e.dma_start(out=wg[:], in_=wgd[:, :, f0:f0 + FG]), writes=[n_wg], dma=True)
            P.op("gpsimd", lambda e, wu=wu, f0=f0: e.dma_start(out=wu[:], in_=wud[:, :, f0:f0 + FG]), writes=[n_wu], dma=True)
            P.op("gpsimd", lambda e, wd=wd, f0=f0: e.dma_start(out=wd[:], in_=wdd[:, f0 // 128:(f0 + FG) // 128, :]),
                 writes=[n_wd], dma=True)
            for fc in range(FG // 128):
                for (t0, n) in macros:
                    (pg, n_pg) = pgr.next()
                    (pu, n_pu) = pur.next()
                    (sg, n_sg) = sgr.next()

                    def mm(e, w=wg, p=pg, fc=fc, t0=t0, n=n):
                        ins = None
                        for kc in range(8):
                            ins = e.matmul(p[:, 0:n], lhsT=w[:, kc, fc * 128:(fc + 1) * 128], rhs=HT[:, kc, t0:t0 + n],
                                           start=(kc == 0), stop=(kc == 7))
                        return ins
                    P.op("tensor", mm, reads=[n_wg, n_HT], writes=[n_pg])

                    def mm2(e, w=wu, p=pu, fc=fc, t0=t0, n=n):
                        ins = None
                        for kc in range(8):
                            ins = e.matmul(p[:, 0:n], lhsT=w[:, kc, fc * 128:(fc + 1) * 128], rhs=HT[:, kc, t0:t0 + n],
                                           start=(kc == 0), stop=(kc == 7))
                        return ins
                    P.op("tensor", mm2, reads=[n_wu, n_HT], writes=[n_pu])
                    P.op("scalar", lambda e, sg=sg, pg=pg, n=n: e.activation(out=sg[:, 0:n], in_=pg[:, 0:n], func=AF.Silu),
                         reads=[n_pg], writes=[n_sg])
                    P.op("vector", lambda e, act=act, sg=sg, pu=pu, fc=fc, t0=t0, n=n: e.tensor_tensor(
                        out=act[:, fc, t0:t0 + n], in0=sg[:, 0:n], in1=pu[:, 0:n], op=ALU.mult),
                        reads=[n_sg, n_pu], writes=[n_act + "_%d" % (t0 // 512)])
            for j in range(nt):
                for h in range(2):
                    (pd, n_pd) = pdr.next()

                    def mmd(e, pd=pd, act=act, wd=wd, j=j, h=h):
                        ins = None
                        nfc = FG // 128
                        for fc in range(nfc):
                            ins = e.matmul(pd[:], lhsT=act[:, fc, j * 128:(j + 1) * 128], rhs=wd[:, fc, h * 512:(h + 1) * 512],
                                           start=(fc == 0), stop=(fc == nfc - 1))
                        return ins
                    P.op("tensor", mmd, reads=[n_act + "_%d" % (j // 4), n_wd], writes=[n_pd])
                    yname = n_Y + "_%d_%d" % (j, h)
                    if g == 0:
                        P.op("scalar", lambda e, pd=pd, j=j, h=h: e.activation(out=Y[:, j, h * 512:(h + 1) * 512], in_=pd[:],
                                                                                func=AF.Identity),
                             reads=[n_pd], writes=[yname])
                    else:
                        P.op("vector", lambda e, pd=pd, j=j, h=h: e.tensor_tensor(
                            out=Y[:, j, h * 512:(h + 1) * 512], in0=Y[:, j, h * 512:(h + 1) * 512], in1=pd[:], op=ALU.add),
                            reads=[n_pd, yname], writes=[yname])
        P.barrier()
        P.flush()
    post = PostStage(kb, st, bufs=4)
    run_lanes((post.run_gen([Y[:, j, 0:512], Y[:, j, 512:1024]], [n_Y + "_%d_0" % j, n_Y + "_%d_1" % j], tt, M, sub, xsrc, xdst,
                            is_out=is_out) for j, tt in enumerate(tiles)), 4)
    P.barrier()
    P.flush()


class AttnCore:
    def __init__(self, kb, st, n_ps=3, n_pt=3, depth=1):
        self.kb = kb
        self.psr = kb.ring(st, "pS", [128, 512], F32, n_ps, psum=True)
        self.ptr = kb.ring(st, "PT", [128, 512], BF16, n_pt)
        self.pending = []
        self.depth = depth
        assert n_ps > depth and n_pt > depth

    def _submit(self, qk_fn, qk_reads, ncols, exp_scale, mults, pv_fn, pv_reads, pv_writes, after, wide_mult=None, nq=4,
                wide_flat=None):
        kb = self.kb
        P = kb.P
        (pS, n_pS) = self.psr.next()
        (PT, n_PT) = self.ptr.next()
        P.op("tensor", lambda e: qk_fn(e, pS), reads=qk_reads, writes=[n_pS])
        P.op("scalar", lambda e: e.activation(out=PT[:, 0:ncols], in_=pS[:, 0:ncols], func=AF.Exp, scale=exp_scale),
             reads=[n_pS], writes=[n_PT])
        for (i, m_ap, n_m) in mults:
            P.op("vector", lambda e, i=i, m_ap=m_ap: e.tensor_tensor(out=PT[:, i * 128:(i + 1) * 128], in0=PT[:, i * 128:(i + 1) * 128],
                                                                   in1=m_ap, op=ALU.mult), reads=[n_PT, n_m], writes=[n_PT])
        if wide_flat is not None:
            f_ap, n_f, k = wide_flat
            P.op("vector", lambda e: e.tensor_tensor(out=PT[:, 0:k * 128], in0=PT[:, 0:k * 128], in1=f_ap, op=ALU.mult),
                 reads=[n_PT, n_f], writes=[n_PT])
        if wide_mult is not None:
            m_ap, n_m = wide_mult
            P.op("vector", lambda e: e.tensor_tensor(out=PT[:, 0:nq * 128].rearrange("p (a b) -> p a b", b=128),
                                                     in0=PT[:, 0:nq * 128].rearrange("p (a b) -> p a b", b=128),
                                                     in1=m_ap.unsqueeze(1).to_broadcast([128, nq, 128]), op=ALU.mult),
                 reads=[n_PT, n_m], writes=[n_PT])
        self.pending.append((lambda e: pv_fn(e, PT), [n_PT] + list(pv_reads), list(pv_writes), after))
        while len(self.pending) > self.depth:
            self._emit(self.pending.pop(0))

    def _emit(self, pend):
        fn, rd, wr, after = pend
        self.kb.P.op("tensor", fn, reads=rd, writes=wr)
        if after is not None:
            after()

    def flush_pending(self):
        while self.pending:
            self._emit(self.pending.pop(0))

    def bank(self, blocks, exp_scale, after=None, wide=None):
        kb = self.kb
        nb = len(blocks)
        assert 1 <= nb <= 4

        def qk(e, pS):
            ins = None
            for i, b in enumerate(blocks):
                ins = e.matmul(pS[:, i * 128:(i + 1) * 128], lhsT=b["kT"], rhs=b["qT"], start=True, stop=True)
            return ins

        def pv(e, PT):
            ins = None
            for i, b in enumerate(blocks):
                ins = e.matmul(b["po"], lhsT=PT[:, i * 128:(i + 1) * 128], rhs=b["v"], start=b["start"], stop=b["stop"])
            return ins
        rd = []
        rdv = []
        wr = []
        mults = []
        for i, b in enumerate(blocks):
            rd += list(b["rd"])
            rdv += list(b["rdv"])
            wr.append(b["n_po"])
            if b.get("mult") is not None and not (wide is not None and i < wide[2]):
                mults.append((i, b["mult"], b["n_mult"]))
        self._submit(qk, rd, nb * 128, exp_scale, mults, pv, rdv, wr, after, wide_flat=wide)

    def bank_wide(self, kT, qT_wide, rd, v, rdv, pos, start, stop, exp_scale, nq=4, after=None, mult=None):
        def qk(e, pS):
            return e.matmul(pS[:, 0:nq * 128], lhsT=kT, rhs=qT_wide, start=True, stop=True)

        def pv(e, PT):
            ins = None
            for i in range(nq):
                ins = e.matmul(pos[i][0], lhsT=PT[:, i * 128:(i + 1) * 128], rhs=v, start=start, stop=stop)
            return ins
        self._submit(qk, list(rd), nq * 128, exp_scale, [], pv, rdv, [p[1] for p in pos], after, wide_mult=mult, nq=nq)


class OutProj:
    def __init__(self, kb, st, T, wo_dram, M, xsrc, xdst, post_bufs=1, n_py=2, n_pot=1, n_ot=2):
        self.kb = kb
        P = kb.P
        self.M = M
        self.xsrc, self.xdst = xsrc, xdst
        self.wo, self.n_wo = kb.sb(st, "wo", [128, 8, 1024], BF16)
        P.op("gpsimd", lambda e: e.dma_start(out=self.wo[:], in_=wo_dram.rearrange("(kc p) n -> p kc n", p=128)),
             writes=[self.n_wo], dma=True)
        self.otr = kb.ring(st, "OT", [128, 8, 128], BF16, n_ot)
        self.ptr = kb.ring(st, "pOT", [128, 1024], BF16, n_pot, psum=True)
        self.pyr = kb.ring(st, "pY", [128, 512], F32, n_py, psum=True)
        self.ysr = kb.ring(st, "Ysb", [128, 1024], F32, 2) if n_py < 2 else None
        self.post = PostStage(kb, st, bufs=post_bufs)

    def run(self, *a, **k):
        for _ in self.run_gen(*a, **k):
            pass

    def run_gen(self, ocat_ap, n_ocat, tt, is_out=False):
        kb = self.kb
        P = kb.P
        (OT, n_OT) = self.otr.next()
        (pt, n_pt) = self.ptr.next()

        def tr(e):
            ins = None
            for c in range(8):
                ins = e.transpose(out=pt[:, c * 128:(c + 1) * 128], in_=ocat_ap[:, c * 128:(c + 1) * 128], identity=kb.ident[:])
            return ins
        P.op("tensor", tr, reads=[n_ocat, kb.n_ident], writes=[n_pt])
        yield
        P.op("vector", lambda e: e.tensor_copy(out=OT[:].rearrange("p c t -> p (c t)"), in_=pt[:]), reads=[n_pt], writes=[n_OT])
        yield
        halves = []
        names = []
        for h in range(2):
            (py, n_py) = self.pyr.next()

            def mm(e, py=py, h=h):
                ins = None
                for c in range(8):
                    ins = e.matmul(py[:], lhsT=OT[:, c, :], rhs=self.wo[:, c, h * 512:(h + 1) * 512], start=(c == 0), stop=(c == 7))
                return ins
            P.op("tensor", mm, reads=[n_OT, self.n_wo], writes=[n_py])
            if self.ysr is not None:
                if h == 0:
                    (ys, n_ys) = self.ysr.next()
                P.op("scalar", lambda e, ys=ys, py=py, h=h: e.activation(out=ys[:, h * 512:(h + 1) * 512], in_=py[:], func=AF.Identity),
                     reads=[n_py], writes=[n_ys + "_%d" % h])
                halves.append(ys[:, h * 512:(h + 1) * 512])
                names.append(n_ys + "_%d" % h)
            else:
                halves.append(py[:])
                names.append(n_py)
        yield
        yield from self.post.run_gen(halves, names, tt, self.M, 1, self.xsrc, self.xdst, is_out=is_out)


def load_w_bf16(kb, st, name, dram_ap_pkn, ncols, piece=512):
    P = kb.P
    w, n_w = kb.sb(st, name, [128, 8, ncols], BF16)
    for c0 in range(0, ncols, piece):
        n = min(piece, ncols - c0)
        P.op("gpsimd", lambda e, c0=c0, n=n: e.dma_start(out=w[:, :, c0:c0 + n], in_=dram_ap_pkn[:, :, c0:c0 + n]),
             writes=[n_w + "_%d" % (c0 // piece)], dma=True)
    return w, n_w


def stage_mixer_d(kb, st, T, M, xsrc, xdst):
    nc, P = kb.nc, kb.P
    H, HKV, DH = 8, 2, 128
    QT, n_QT = kb.sb(st, "QT", [128, H, S], BF16)
    KT, n_KT = kb.sb(st, "KT", [128, HKV, NTOK], BF16)
    VA, n_VA = kb.sb(st, "VA", [128, 18, HKV, DH + 2], BF16)
    P.op("gpsimd", lambda e: e.memset(VA[:, :, :, DH:DH + 1], 1.0), writes=[n_VA])
    with ExitStack() as st2:
        HT, n_HT = kb.sb(st2, "HT", [128, 8, NTOK], BF16)
        with ExitStack() as st3:
            stage_norm_T(kb, st3, xsrc, list(range(18)), M, 1, HT, n_HT)
            P.barrier()
            P.flush()
        wq, n_wq = load_w_bf16(kb, st2, "wqkv", T["ga_w_qkv_p"].rearrange("(kc p) n -> p kc n", p=128), 1536)
        gain, n_gain = kb.sb(st2, "gain", [128, 10, 128], F32)
        cos, n_cos = kb.sb(st2, "cos", [128, 16, 64], F32)
        sin, n_sin = kb.sb(st2, "sin", [128, 16, 64], F32)
        P.op("sync", lambda e: e.dma_start(out=gain[:].rearrange("p a b -> p (a b)"),
                                           in_=T["ga_gain"][:].partition_broadcast(128)), writes=[n_gain], dma=True)
        P.op("sync", lambda e: e.dma_start(out=cos[:], in_=T["ga_cos"].rearrange("(t p) d -> p t d", p=128)), writes=[n_cos], dma=True)
        P.op("sync", lambda e: e.dma_start(out=sin[:], in_=T["ga_sin"].rearrange("(t p) d -> p t d", p=128)), writes=[n_sin], dma=True)
        ppr = kb.ring(st2, "pproj", [128, 512], F32, 3, psum=True)
        ptq = kb.ring(st2, "ptq", [128, 1024], BF16, 2, psum=True)
        qfr = kb.ring(st2, "qf", [128, 10, 128], F32, 3)
        sqr = kb.ring(st2, "qsq", [128, 10, 128], F32, 2)
        ssr = kb.ring(st2, "qss", [128, 16], F32, 3)
        rar = kb.ring(st2, "ra", [128, 10, 64], F32, 4)
        rbr = kb.ring(st2, "rb", [128, 10, 64], F32, 4)
        qrr = kb.ring(st2, "qr", [128, 10, 128], BF16, 3)
        def tile_gen(tt):
            lat = tt < 16
            (qf, n_qf) = qfr.next()
            (sq, n_sq) = sqr.next()
            (ss, n_ss) = ssr.next()
            (qr, n_qr) = qrr.next()
            pieces = ([(0, 0), (512, 4)] if lat else []) + [(1024, 8)]
            for (c0, h0) in pieces:
                (pp, n_pp) = ppr.next()

                def mm(e, pp=pp, c0=c0, tt=tt):
                    ins = None
                    for kc in range(8):
                        ins = e.matmul(pp[:], lhsT=HT[:, kc, tt * 128:(tt + 1) * 128], rhs=wq[:, kc, c0:c0 + 512],
                                       start=(kc == 0), stop=(kc == 7))
                    return ins
                P.op("tensor", mm, reads=[n_HT, n_wq + "_%d" % (c0 // 512)], writes=[n_pp])
                if c0 < 1024:
                    P.op("scalar", lambda e, pp=pp, qf=qf, h0=h0: e.activation(
                        out=qf[:, h0:h0 + 4, :].rearrange("p a b -> p (a b)"), in_=pp[:], func=AF.Identity),
                        reads=[n_pp], writes=[n_qf])
                else:
                    P.op("scalar", lambda e, pp=pp, qf=qf: e.activation(
                        out=qf[:, 8:10, :].rearrange("p a b -> p (a b)"), in_=pp[:, 0:256], func=AF.Identity),
                        reads=[n_pp], writes=[n_qf])
                    P.op("vector", lambda e, pp=pp, tt=tt: e.tensor_copy(
                        out=VA[:, tt, :, 0:DH], in_=pp[:, 256:512].rearrange("p (a b) -> p a b", b=DH)),
                        reads=[n_pp], writes=[n_VA])
            h_lo = 0 if lat else 8
            yield
            nh_ = 10 - h_lo
            P.op("vector", lambda e, qf=qf, sq=sq, h_lo=h_lo: e.tensor_tensor(out=sq[:, h_lo:10, :], in0=qf[:, h_lo:10, :],
                                                                            in1=qf[:, h_lo:10, :], op=ALU.mult),
                 reads=[n_qf], writes=[n_sq])
            P.op("vector", lambda e, sq=sq, ss=ss, h_lo=h_lo: e.tensor_reduce(out=ss[:, h_lo:10], in_=sq[:, h_lo:10, :],
                                                                            axis=AX.X, op=ALU.add),
                 reads=[n_sq], writes=[n_ss])
            yield
            P.op("vector", lambda e, ss=ss: e.tensor_scalar(out=ss[:, 0:10], in0=ss[:, 0:10], scalar1=1.0 / DH, scalar2=EPS,
                                                            op0=ALU.mult, op1=ALU.add), reads=[n_ss], writes=[n_ss])
            P.op("gpsimd", lambda e, ss=ss: e.tensor_tensor(out=ss[:, 0:10], in0=ss[:, 0:10], in1=kb.nh2[:, 0:10], op=ALU.pow),
                 reads=[n_ss, kb.n_nh2], writes=[n_ss])
            yield
            P.op("vector", lambda e, qf=qf, ss=ss, h_lo=h_lo, nh_=nh_: e.tensor_tensor(
                out=qf[:, h_lo:10, :], in0=qf[:, h_lo:10, :], in1=ss[:, h_lo:10].unsqueeze(2).to_broadcast([128, nh_, 128]),
                op=ALU.mult), reads=[n_qf, n_ss], writes=[n_qf])
            yield
            if lat:
                P.op("vector", lambda e, qf=qf: e.tensor_tensor(out=qf[:], in0=qf[:], in1=gain[:], op=ALU.mult),
                     reads=[n_qf, n_gain], writes=[n_qf])
                (ra, n_ra) = rar.next()
                (rb, n_rb) = rbr.next()
                cb = cos[:, tt, :].unsqueeze(1).to_broadcast([128, 10, 64])
                sb_ = sin[:, tt, :].unsqueeze(1).to_broadcast([128, 10, 64])
                x1 = qf[:, :, 0:64]
                x2 = qf[:, :, 64:128]
                P.op("vector", lambda e, ra=ra, x1=x1, cb=cb: e.tensor_tensor(out=ra[:], in0=x1, in1=cb, op=ALU.mult),
                     reads=[n_qf, n_cos], writes=[n_ra])
                P.op("vector", lambda e, rb=rb, x2=x2, sb_=sb_: e.tensor_tensor(out=rb[:], in0=x2, in1=sb_, op=ALU.mult),
                     reads=[n_qf, n_sin], writes=[n_rb])
                P.op("vector", lambda e, qr=qr, ra=ra, rb=rb: e.tensor_tensor(out=qr[:, :, 0:64], in0=ra[:], in1=rb[:], op=ALU.subtract),
                     reads=[n_ra, n_rb], writes=[n_qr])
                yield
                (ra2, n_ra2) = rar.next()
                (rb2, n_rb2) = rbr.next()
                P.op("vector", lambda e, ra2=ra2, x1=x1, sb_=sb_: e.tensor_tensor(out=ra2[:], in0=x1, in1=sb_, op=ALU.mult),
                     reads=[n_qf, n_sin], writes=[n_ra2])
                P.op("vector", lambda e, rb2=rb2, x2=x2, cb=cb: e.tensor_tensor(out=rb2[:], in0=x2, in1=cb, op=ALU.mult),
                     reads=[n_qf, n_cos], writes=[n_rb2])
                P.op("vector", lambda e, qr=qr, ra2=ra2, rb2=rb2: e.tensor_tensor(out=qr[:, :, 64:128], in0=ra2[:], in1=rb2[:], op=ALU.add),
                     reads=[n_ra2, n_rb2], writes=[n_qr])
            else:
                P.op("vector", lambda e, qf=qf, qr=qr: e.tensor_tensor(out=qr[:, 8:10, :], in0=qf[:, 8:10, :], in1=gain[:, 8:10, :],
                                                                     op=ALU.mult), reads=[n_qf, n_gain], writes=[n_qr])
            yield
            groups = ([(0, 8, "q")] if lat else []) + [(8, 2, "k")]
            for (h0, n, kind) in groups:
                (pt, n_pt) = ptq.next()

                def tr(e, pt=pt, qr=qr, h0=h0, n=n):
                    ins = None
                    for j in range(n):
                        ins = e.transpose(out=pt[:, j * 128:(j + 1) * 128], in_=qr[:, h0 + j, :], identity=kb.ident[:])
                    return ins
                P.op("tensor", tr, reads=[n_qr, kb.n_ident], writes=[n_pt])
                if kind == "q":
                    P.op("scalar", lambda e, pt=pt, tt=tt: e.activation(
                        out=QT[:, :, tt * 128:(tt + 1) * 128], in_=pt[:].rearrange("p (a b) -> p a b", b=128), func=AF.Identity),
                        reads=[n_pt], writes=[n_QT])
                else:
                    P.op("vector", lambda e, pt=pt, tt=tt: e.tensor_copy(
                        out=KT[:, :, tt * 128:(tt + 1) * 128], in_=pt[:, 0:256].rearrange("p (a b) -> p a b", b=128)),
                        reads=[n_pt], writes=[n_KT])

        run_lanes((tile_gen(tt) for tt in range(18)), 2)
        P.barrier()
        P.flush()
    import os
    if os.environ.get("MIX_STOP") == "1":
        return
    with ExitStack() as st2:
        core = AttnCore(kb, st2, n_ps=2, n_pt=3)
        op_ = OutProj(kb, st2, T, T["ga_w_o"][0], M, xsrc, xdst, n_py=1)
        ocr = kb.ring(st2, "Ocat", [128, 4, 1024], BF16, 2)
        pos_ = [kb.ps(st2, "pO", [128, 512], F32) for _ in range(4)]
        rcr = kb.ring(st2, "rc", [128, 4], F32, 8)
        stgr = kb.ring(st2, "ostage", [128, 4, DH + 2], F32, 2)
        scale = DH ** -0.5
        for mq in range(4):
            (oc, n_oc) = ocr.next()
            for h in range(H):
                g = h // (H // HKV)
                def after(h=h, oc=oc, n_oc=n_oc):
                    (stg, n_stg) = stgr.next()
                    for j in range(4):
                        (po, n_po) = pos_[j]
                        P.op("scalar", lambda e, j=j, po=po, stg=stg: e.activation(out=stg[:, j, 0:DH + 1], in_=po[:, 0:DH + 1],
                                                                               func=AF.Identity),
                             reads=[n_po], writes=[n_stg + "_%d" % j])
                    (rc, n_rc) = rcr.next()
                    sn = [n_stg + "_%d" % j for j in range(4)]
                    P.op("vector", lambda e, stg=stg, rc=rc: e.reciprocal(out=rc[:, 0:4], in_=stg[:, :, DH]), reads=sn, writes=[n_rc])
                    P.op("vector", lambda e, stg=stg, rc=rc, oc=oc, h=h: e.tensor_tensor(
                        out=oc[:, :, h * DH:(h + 1) * DH], in0=stg[:, :, 0:DH], in1=rc[:, 0:4].unsqueeze(2).to_broadcast([128, 4, DH]),
                        op=ALU.mult), reads=sn + [n_rc], writes=[n_oc + "_%d" % j for j in range(4)])
                for kt in range(18):
                    core.bank_wide(KT[:, g, kt * 128:(kt + 1) * 128], QT[:, h, mq * 512:(mq + 1) * 512], [n_KT, n_QT],
                                   VA[:, kt, g, 0:DH + 1], [n_VA], [(po[:, 0:DH + 1], n_po) for (po, n_po) in pos_],
                                   kt == 0, kt == 17, scale, after=(after if kt == 17 else None))
            core.flush_pending()
            for j in range(4):
                op_.run(oc[:, j, :], n_oc + "_%d" % j, mq * 4 + j)
        P.barrier()
        P.flush()


def normalize_head(kb, po, n_po, dv, oc_ap, n_oc, rcr, extra=None):
    P = kb.P
    (rc, n_rc) = rcr.next()
    if extra is not None:
        ex_ap, n_ex = extra
        P.op("vector", lambda e: e.tensor_tensor(out=rc[:, 1:2], in0=po[:, dv:dv + 1], in1=ex_ap, op=ALU.add),
             reads=[n_po, n_ex], writes=[n_rc])
        P.op("vector", lambda e: e.reciprocal(out=rc[:, 0:1], in_=rc[:, 1:2]), reads=[n_rc], writes=[n_rc])
    else:
        P.op("vector", lambda e: e.reciprocal(out=rc[:, 0:1], in_=po[:, dv:dv + 1]), reads=[n_po], writes=[n_rc])
    P.op("vector", lambda e: e.tensor_scalar(out=oc_ap, in0=po[:, 0:dv], scalar1=rc[:, 0:1], scalar2=None, op0=ALU.mult),
         reads=[n_po, n_rc], writes=[n_oc])


def stage_mixer_b(kb, st, T, M, xsrc, xdst):
    nc, P = kb.nc, kb.P
    H, HKV, DH = 16, 2, 64
    QT, n_QT = kb.sb(st, "QT2", [128, 8, NTOK], BF16)
    KT, n_KT = kb.sb(st, "KT2", [128, HKV, NTOK], BF16)
    VA, n_VA = kb.sb(st, "VA", [128, 18, HKV, DH + 2], BF16)
    P.op("gpsimd", lambda e: e.memset(VA[:, :, :, DH:DH + 1], 1.0), writes=[n_VA])
    with ExitStack() as st2:
        HT, n_HT = kb.sb(st2, "HT", [128, 8, NTOK], BF16)
        with ExitStack() as st3:
            stage_norm_T(kb, st3, xsrc, list(range(18)), M, 1, HT, n_HT)
            P.barrier()
            P.flush()
        wq, n_wq = load_w_bf16(kb, st2, "wqkv", T["sw_w_qkv_p"].rearrange("(kc p) n -> p kc n", p=128), 1280, piece=256)
        cos, n_cos = kb.sb(st2, "cos", [128, 16, 32], F32)
        sin, n_sin = kb.sb(st2, "sin", [128, 16, 32], F32)
        P.op("sync", lambda e: e.dma_start(out=cos[:], in_=T["sw_cos"].rearrange("(t p) d -> p t d", p=128)), writes=[n_cos], dma=True)
        P.op("sync", lambda e: e.dma_start(out=sin[:], in_=T["sw_sin"].rearrange("(t p) d -> p t d", p=128)), writes=[n_sin], dma=True)
        ppr = kb.ring(st2, "pproj", [128, 512], F32, 3, psum=True)
        ptq = kb.ring(st2, "ptq", [128, 1024], BF16, 2, psum=True)
        qfr = kb.ring(st2, "qf", [128, 18, 64], F32, 3)
        rar = kb.ring(st2, "ra", [128, 18, 32], F32, 4)
        rbr = kb.ring(st2, "rb", [128, 18, 32], F32, 4)
        qrr = kb.ring(st2, "qr", [128, 18, 64], BF16, 3)
        kdr = kb.ring(st2, "kd", [128, 2, 2, 64], BF16, 3)
        def tile_gen(tt):
            lat = tt < 16
            (qf, n_qf) = qfr.next()
            (qr, n_qr) = qrr.next()
            (kd, n_kd) = kdr.next()
            for (c0, ncol, h0) in [(0, 512, 0), (512, 512, 8), (1024, 256, 16)]:
                (pp, n_pp) = ppr.next()

                def mm(e, pp=pp, c0=c0, ncol=ncol, tt=tt):
                    ins = None
                    for kc in range(8):
                        ins = e.matmul(pp[:, 0:ncol], lhsT=HT[:, kc, tt * 128:(tt + 1) * 128], rhs=wq[:, kc, c0:c0 + ncol],
                                       start=(kc == 0), stop=(kc == 7))
                    return ins
                P.op("tensor", mm, reads=[n_HT] + [n_wq + "_%d" % i for i in range(c0 // 256, (c0 + ncol) // 256)], writes=[n_pp])
                if c0 < 1024:
                    P.op("scalar", lambda e, pp=pp, qf=qf, h0=h0: e.activation(
                        out=qf[:, h0:h0 + 8, :].rearrange("p a b -> p (a b)"), in_=pp[:], func=AF.Identity),
                        reads=[n_pp], writes=[n_qf])
                else:
                    P.op("scalar", lambda e, pp=pp, qf=qf: e.activation(
                        out=qf[:, 16:18, :].rearrange("p a b -> p (a b)"), in_=pp[:, 0:128], func=AF.Identity),
                        reads=[n_pp], writes=[n_qf])
                    P.op("scalar", lambda e, pp=pp, tt=tt: e.activation(
                        out=VA[:, tt, :, 0:DH], in_=pp[:, 128:256].rearrange("p (a b) -> p a b", b=DH), func=AF.Identity),
                        reads=[n_pp], writes=[n_VA])
            yield
            if lat:
                (ra, n_ra) = rar.next()
                (rb, n_rb) = rbr.next()
                cb = cos[:, tt, :].unsqueeze(1).to_broadcast([128, 18, 32])
                sb_ = sin[:, tt, :].unsqueeze(1).to_broadcast([128, 18, 32])
                x1 = qf[:, :, 0:32]
                x2 = qf[:, :, 32:64]
                P.op("vector", lambda e, ra=ra, x1=x1, cb=cb: e.tensor_tensor(out=ra[:], in0=x1, in1=cb, op=ALU.mult),
                     reads=[n_qf, n_cos], writes=[n_ra])
                P.op("vector", lambda e, rb=rb, x2=x2, sb_=sb_: e.tensor_tensor(out=rb[:], in0=x2, in1=sb_, op=ALU.mult),
                     reads=[n_qf, n_sin], writes=[n_rb])
                P.op("vector", lambda e, qr=qr, ra=ra, rb=rb: e.tensor_tensor(out=qr[:, :, 0:32], in0=ra[:], in1=rb[:], op=ALU.subtract),
                     reads=[n_ra, n_rb], writes=[n_qr])
                yield
                (ra2, n_ra2) = rar.next()
                (rb2, n_rb2) = rbr.next()
                P.op("vector", lambda e, ra2=ra2, x1=x1, sb_=sb_: e.tensor_tensor(out=ra2[:], in0=x1, in1=sb_, op=ALU.mult),
                     reads=[n_qf, n_sin], writes=[n_ra2])
                P.op("vector", lambda e, rb2=rb2, x2=x2, cb=cb: e.tensor_tensor(out=rb2[:], in0=x2, in1=cb, op=ALU.mult),
                     reads=[n_qf, n_cos], writes=[n_rb2])
                P.op("vector", lambda e, qr=qr, ra2=ra2, rb2=rb2: e.tensor_tensor(out=qr[:, :, 32:64], in0=ra2[:], in1=rb2[:], op=ALU.add),
                     reads=[n_ra2, n_rb2], writes=[n_qr])
            else:
                P.op("vector", lambda e, qr=qr, qf=qf: e.tensor_copy(out=qr[:], in_=qf[:]), reads=[n_qf], writes=[n_qr])
            yield
            for dup in range(2):
                P.op("scalar", lambda e, kd=kd, qr=qr, dup=dup: e.activation(out=kd[:, :, dup, :], in_=qr[:, 16:18, :], func=AF.Identity),
                     reads=[n_qr], writes=[n_kd])
            (pt, n_pt) = ptq.next()

            def tr(e, pt=pt, qr=qr):
                ins = None
                for p_ in range(8):
                    ins = e.transpose(out=pt[:, p_ * 128:(p_ + 1) * 128],
                                      in_=qr[:, 2 * p_:2 * p_ + 2, :].rearrange("p a b -> p (a b)"), identity=kb.ident[:])
                return ins
            P.op("tensor", tr, reads=[n_qr, kb.n_ident], writes=[n_pt])
            P.op("scalar", lambda e, pt=pt, tt=tt: e.activation(
                out=QT[:, :, tt * 128:(tt + 1) * 128], in_=pt[:].rearrange("p (a b) -> p a b", b=128), func=AF.Identity),
                reads=[n_pt], writes=[n_QT])
            (pt2, n_pt2) = ptq.next()

            def tr2(e, pt2=pt2, kd=kd):
                ins = None
                for g in range(2):
                    ins = e.transpose(out=pt2[:, g * 128:(g + 1) * 128], in_=kd[:, g, :, :].rearrange("p a b -> p (a b)"),
                                      identity=kb.ident[:])
                return ins
            P.op("tensor", tr2, reads=[n_kd, kb.n_ident], writes=[n_pt2])
            P.op("vector", lambda e, pt2=pt2, tt=tt: e.tensor_copy(
                out=KT[:, :, tt * 128:(tt + 1) * 128], in_=pt2[:, 0:256].rearrange("p (a b) -> p a b", b=128)),
                reads=[n_pt2], writes=[n_KT])
        run_lanes((tile_gen(tt) for tt in range(18)), 2)
        P.barrier()
        P.flush()
    with ExitStack() as st2:
        core = AttnCore(kb, st2, n_ps=2, n_pt=3)
        op_ = OutProj(kb, st2, T, T["sw_w_o"][0], M, xsrc, xdst, n_py=1)
        ocr = kb.ring(st2, "Ocat", [128, 1024], BF16, 2)
        pos_ = [kb.ps(st2, "pO", [128, 512], F32) for _ in range(4)]
        rcr = kb.ring(st2, "rc", [128, 8], F32, 4)
        stgr = kb.ring(st2, "ostage", [128, 4, DH + 2], F32, 2)
        esk, n_esk = kb.sb(st2, "esink", [128, 16], F32)
        P.op("sync", lambda e: e.dma_start(out=esk[:], in_=T["sw_sink"][0, :].partition_broadcast(128)), writes=[n_esk], dma=True)
        P.op("scalar", lambda e: e.activation(out=esk[:], in_=esk[:], func=AF.Exp), reads=[n_esk], writes=[n_esk])
        scale = DH ** -0.5
        for tt in range(18):
            (oc, n_oc) = ocr.next()
            if tt < 16:
                kl = []
                if tt - 1 >= 0:
                    kl.append((tt - 1, kb.masklo[:], kb.n_masklo))
                kl.append((tt, None, None))
                if tt + 1 <= 15:
                    kl.append((tt + 1, kb.maskhi[:], kb.n_maskhi))
                kl += [(16, None, None), (17, None, None)]
            else:
                kl = [(16, None, None), (17, None, None)]
            for g in range(HKV):
                for par in range(2):
                    b0 = par * 64
                    heads = [g * 8 + 2 * i + par for i in range(4)]

                    def after(g=g, par=par, oc=oc, n_oc=n_oc):
                        (stg, n_stg) = stgr.next()
                        for i in range(4):
                            (po, n_po) = pos_[i]
                            P.op("scalar", lambda e, i=i, po=po, stg=stg: e.activation(out=stg[:, i, 0:DH + 1], in_=po[:, 0:DH + 1],
                                                                                   func=AF.Identity),
                                 reads=[n_po], writes=[n_stg + "_%d" % i])
                        (rc, n_rc) = rcr.next()
                        esk_v = esk[:].rearrange("p (g i two) -> p g i two", g=2, two=2)[:, g, :, par]
                        oc_v = oc[:].rearrange("p (g i two d) -> p g i two d", g=2, two=2, d=DH)[:, g, :, par, :]
                        sn = [n_stg + "_%d" % i for i in range(4)]
                        P.op("vector", lambda e, stg=stg, rc=rc, esk_v=esk_v: e.tensor_tensor(out=rc[:, 0:4], in0=stg[:, :, DH], in1=esk_v,
                                                                                          op=ALU.add), reads=sn + [n_esk], writes=[n_rc])
                        P.op("vector", lambda e, rc=rc: e.reciprocal(out=rc[:, 4:8], in_=rc[:, 0:4]), reads=[n_rc], writes=[n_rc])
                        P.op("vector", lambda e, stg=stg, rc=rc, oc_v=oc_v: e.tensor_tensor(
                            out=oc_v, in0=stg[:, :, 0:DH], in1=rc[:, 4:8].unsqueeze(2).to_broadcast([128, 4, DH]), op=ALU.mult),
                            reads=sn + [n_rc], writes=[n_oc])
                    for idx, (kt, mk, n_mk) in enumerate(kl):
                        core.bank_wide(KT[b0:b0 + 64, g, kt * 128:(kt + 1) * 128],
                                       QT[b0:b0 + 64, g * 4:(g + 1) * 4, tt * 128:(tt + 1) * 128], [n_KT, n_QT],
                                       VA[:, kt, g, 0:DH + 1], [n_VA], [(po[:, 0:DH + 1], n_po) for (po, n_po) in pos_],
                                       idx == 0, idx == len(kl) - 1, scale,
                                       after=(after if idx == len(kl) - 1 else None),
                                       mult=((mk, n_mk) if mk is not None else None))
            core.flush_pending()
            op_.run(oc[:], n_oc, tt)
        P.barrier()
        P.flush()


NEG_BIAS = -30000.0


def na_structure():
    rows = S // GRID_W
    p = np.arange(128)
    combos = []
    keyl = []
    sig = {}
    for m in range(16):
        r = 2 * m + p // 64
        j = p % 64
        rs = np.clip(r - 4, 0, rows - 8)
        ws = np.clip(j - 8, 0, GRID_W - 16)
        lst = []
        for kt in range(16):
            kr = 2 * kt + p // 64
            kc = p % 64
            valid = ((kr[:, None] >= rs[None, :]) & (kr[:, None] < rs[None, :] + 8)
                     & (kc[:, None] >= ws[None, :]) & (kc[:, None] < ws[None, :] + 16))
            if not valid.any():
                continue
            ridx = np.clip(kr[:, None] - r[None, :] + 7, 0, 14)
            cidx = np.clip(kc[:, None] - j[None, :] + 15, 0, 30)
            ridx = np.where(valid, ridx, 0)
            cidx = np.where(valid, cidx, 0)
            key = (ridx.tobytes(), cidx.tobytes(), valid.tobytes())
            if key not in sig:
                sig[key] = len(combos)
                combos.append((ridx, cidx, valid))
            lst.append((kt, sig[key]))
        keyl.append(lst)
    from collections import Counter
    seqs = Counter(tuple(ci for (_, ci) in lst) for lst in keyl)
    common = list(seqs.most_common(1)[0][0])
    order = common + [i for i in range(len(combos)) if i not in common]
    remap = {old_i: new_i for new_i, old_i in enumerate(order)}
    combos = [combos[i] for i in order]
    keyl = [[(kt, remap[ci]) for (kt, ci) in lst] for lst in keyl]
    return keyl, combos


def na_bias_host(rpb):
    keyl, combos = na_structure()
    rpb = np.asarray(rpb, dtype=np.float32)[0]
    out = np.empty((16, 128, len(combos), 128), dtype=np.float32)
    for ci, (ridx, cidx, valid) in enumerate(combos):
        g = rpb[:, ridx, cidx]
        out[:, :, ci, :] = np.where(valid[None], g, np.float32(NEG_BIAS))
    return out


def stage_mixer_a(kb, st, T, M, xsrc, xdst):
    nc, P = kb.nc, kb.P
    H, DH = 16, 64
    keyl, combos = na_structure()
    NCMB = len(combos)
    BIG, n_BIG = kb.sb(st, "HT_OC", [128, 8 * NTOK], BF16)
    HT, n_HT = BIG[:].rearrange("p (c t) -> p c t", c=8), n_BIG + "_ht"
    OC, n_OC = BIG[:].rearrange("p (t d) -> p t d", t=18), n_BIG + "_oc"
    stA = ExitStack()
    QT, n_QT = kb.sb(stA, "QT2", [128, 8, NTOK], BF16)
    KT, n_KT = kb.sb(stA, "KT2", [128, 8, NTOK], BF16)
    VA, n_VA = kb.sb(stA, "VA", [128, 18, H, DH + 2], BF16)
    P.op("gpsimd", lambda e: e.memset(VA[:, :, :, DH:DH + 1], 1.0), writes=[n_VA])
    scale = DH ** -0.5
    with ExitStack() as st2:
        with ExitStack() as st3:
            stage_norm_T(kb, st3, xsrc, list(range(18)), M, 1, HT, n_HT, rings=(4, 1, 4))
            P.barrier()
            P.flush()
        wr = kb.ring(st2, "wqkv", [128, 8, 512], BF16, 2)
        ppr = kb.ring(st2, "pproj", [128, 512], F32, 4, psum=True)
        wd = T["na_w_qkv"][0].rearrange("(kc p) n -> p kc n", p=128)
        macros = tok_macros(0, NTOK)
        ev = 0
        for piece in range(6):
            (w, n_w) = wr.next()
            P.op("gpsimd", lambda e, w=w, piece=piece: e.dma_start(out=w[:], in_=wd[:, :, piece * 512:(piece + 1) * 512]),
                 writes=[n_w], dma=True)
            if piece < 4:
                dst, n_dst = (QT, n_QT) if piece < 2 else (KT, n_KT)
                for pl in range(4):
                    pr = (piece % 2) * 4 + pl
                    for (t0, n) in macros:
                        (pp, n_pp) = ppr.next()

                        def mm(e, pp=pp, w=w, pl=pl, t0=t0, n=n):
                            ins = None
                            for kc in range(8):
                                ins = e.matmul(pp[:, 0:n], lhsT=w[:, kc, pl * 128:(pl + 1) * 128], rhs=HT[:, kc, t0:t0 + n],
                                               start=(kc == 0), stop=(kc == 7))
                            return ins
                        P.op("tensor", mm, reads=[n_w, n_HT], writes=[n_pp])
                        if piece < 2:
                            P.op("scalar", lambda e, pp=pp, pr=pr, t0=t0, n=n: e.activation(
                                out=QT[:, pr, t0:t0 + n], in_=pp[:, 0:n], func=AF.Identity, scale=scale),
                                reads=[n_pp], writes=[n_QT])
                        else:
                            P.op("vector", lambda e, pp=pp, pr=pr, t0=t0, n=n: e.tensor_copy(out=KT[:, pr, t0:t0 + n], in_=pp[:, 0:n]),
                                 reads=[n_pp], writes=[n_KT])
            else:
                h0 = (piece - 4) * 8
                for tt in range(18):
                    (pp, n_pp) = ppr.next()

                    def mm(e, pp=pp, w=w, tt=tt):
                        ins = None
                        for kc in range(8):
                            ins = e.matmul(pp[:], lhsT=HT[:, kc, tt * 128:(tt + 1) * 128], rhs=w[:, kc, :], start=(kc == 0), stop=(kc == 7))
                        return ins
                    P.op("tensor", mm, reads=[n_w, n_HT], writes=[n_pp])
                    if ev % 2 == 0:
                        P.op("scalar", lambda e, pp=pp, tt=tt, h0=h0: e.activation(
                            out=VA[:, tt, h0:h0 + 8, 0:DH], in_=pp[:].rearrange("p (a b) -> p a b", b=DH), func=AF.Identity),
                            reads=[n_pp], writes=[n_VA])
                    else:
                        P.op("vector", lambda e, pp=pp, tt=tt, h0=h0: e.tensor_copy(
                            out=VA[:, tt, h0:h0 + 8, 0:DH], in_=pp[:].rearrange("p (a b) -> p a b", b=DH)),
                            reads=[n_pp], writes=[n_VA])
                    ev += 1
        P.barrier()
        P.flush()
    with ExitStack() as st2:
        core = AttnCore(kb, st2, n_ps=4, n_pt=4, depth=2)
        por = kb.ring(st2, "pO", [128, 512], F32, 2, psum=True)
        rcr = kb.ring(st2, "rc", [128, 4], F32, 4)
        br = kb.ring(st2, "nabias", [128, NCMB, 128], F32, 2)
        for h in range(H):
            (bt, n_bt) = br.next()
            P.op("sync", lambda e, bt=bt, h=h: e.dma_start(out=bt[:], in_=T["na_bias"][h]), writes=[n_bt], dma=True)
            P.op("scalar", lambda e, bt=bt: e.activation(out=bt[:], in_=bt[:], func=AF.Exp), reads=[n_bt], writes=[n_bt])
            b0 = (h % 2) * 64
            pr = h // 2
            for tt in range(18):
                if tt < 16:
                    kl = [(kt, bt[:, ci, :], n_bt) for (kt, ci) in keyl[tt]] + [(16, None, None), (17, None, None)]
                    cis = [ci for (_, ci) in keyl[tt]] + [None, None]
                else:
                    kl = [(16, None, None), (17, None, None)]
                    cis = [None, None]
                (po, n_po) = por.next()
                for k0 in range(0, len(kl), 4):
                    blocks = []
                    for idx in range(k0, min(k0 + 4, len(kl))):
                        kt, bias, n_bias = kl[idx]
                        blocks.append(dict(kT=KT[b0:b0 + 64, pr, kt * 128:(kt + 1) * 128], qT=QT[b0:b0 + 64, pr, tt * 128:(tt + 1) * 128],
                                           mult=bias, n_mult=n_bias, rd=[n_KT, n_QT],
                                           v=VA[:, kt, h, 0:DH + 1], rdv=[n_VA], po=po[:, 0:DH + 1], n_po=n_po,
                                           start=(idx == 0), stop=(idx == len(kl) - 1)))
                    last = (k0 + 4 >= len(kl))
                    bc = cis[k0:k0 + 4]
                    kpre = 0
                    while kpre < len(bc) and bc[kpre] is not None and bc[kpre] == bc[0] + kpre:
                        kpre += 1
                    wide = None
                    if kpre >= 2:
                        wide = (bt[:, bc[0]:bc[0] + kpre, :].rearrange("p a b -> p (a b)"), n_bt, kpre)
                    core.bank(blocks, 1.0, after=((lambda po=po, n_po=n_po, h=h, tt=tt: normalize_head(
                        kb, po, n_po, DH, OC[:, tt, h * DH:(h + 1) * DH], n_OC + "_%d" % tt, rcr)) if last else None), wide=wide)
        core.flush_pending()
        P.barrier()
        P.flush()
    stA.close()
    with ExitStack() as st2:
        op_ = OutProj(kb, st2, T, T["na_w_o"][0], M, xsrc, xdst, post_bufs=2, n_py=4, n_pot=2, n_ot=2)
        run_lanes((op_.run_gen(OC[:, tt, :], n_OC + "_%d" % tt, tt) for tt in range(18)), 2)
        P.barrier()
        P.flush()


def stage_mixer_c(kb, st, T, M, xsrc, xdst):
    nc, P = kb.nc, kb.P
    CW = 31
    HW_ = CW // 2
    HT, n_HT = kb.sb(st, "HT", [128, 8, NTOK], BF16)
    with ExitStack() as st3:
        stage_norm_T(kb, st3, xsrc, list(range(18)), M, 1, HT, n_HT, rings=(4, 1, 4))
        P.barrier()
        P.flush()
    w1, n_w1 = load_w_bf16(kb, st, "w1", T["cv_w_pw1"][0].rearrange("(kc p) n -> p kc n", p=128), 2048)
    w2, n_w2 = load_w_bf16(kb, st, "w2", T["cv_w_pw2"][0].rearrange("(kc p) n -> p kc n", p=128), 1024)
    cvf, n_cvf = kb.sb(st, "cvf", [128, 8, 36], F32)
    b2, n_b2 = kb.sb(st, "b2", [128, 1024], F32)
    onesm, n_ones = kb.sb(st, "onesm", [128, 128], F32)
    U, n_U = kb.sb(st, "U", [128, 8, 512 + 2 * HW_], BF16)
    dgr = kb.ring(st, "Dg", [128, CW, 128], BF16, 2)
    sqr_ = kb.ring(st, "SQc", [128, 512], F32, 2)
    pV, n_pV = kb.ps(st, "pV", [128, 512], F32)
    V, n_V = kb.sb(st, "V", [128, 8, 512], F32)
    Z, n_Z = kb.sb(st, "Z", [128, 8, 512], BF16)
    sgr = kb.ring(st, "sig", [128, 512 + 2 * HW_], F32, 2)
    msq, n_msq = kb.sb(st, "msq", [128, 512], F32)
    rstd, n_rstd = kb.sb(st, "rstd", [128, 512], F32)
    ysr = kb.ring(st, "Ysb", [128, 1024], F32, 2)
    pA, n_pA = kb.ps(st, "pA", [128, 1024], F32)
    pB, n_pB = kb.ps(st, "pB", [128, 1024], F32)
    pM, n_pM = kb.ps(st, "pM", [128, 512], F32)
    pQ, n_pQ = kb.ps(st, "pQ", [128, 512], F32)
    pyr = kb.ring(st, "pY", [128, 512], F32, 1, psum=True)
    post = PostStage(kb, st, bufs=1)
    P.op("sync", lambda e: e.dma_start(out=cvf[:], in_=T["cv_fm"][:, :, :]), writes=[n_cvf], dma=True)
    P.op("sync", lambda e: e.dma_start(out=b2[:], in_=T["cv_b_pw2"][0, :].partition_broadcast(128)), writes=[n_b2], dma=True)
    P.op("gpsimd", lambda e: e.memset(onesm[:], 1.0 / D), writes=[n_ones])
    segs = [(t0, 512, 0, S) for t0 in range(0, S, 512)] + [(S, C, S, S + C)]
    for (t0, n, seq0, seq1) in segs:
        lo = max(t0 - HW_, seq0)
        hi = min(t0 + n + HW_, seq1)
        w = hi - lo
        off = lo - (t0 - HW_)
        if off > 0 or off + w < n + 2 * HW_:
            P.op("gpsimd", lambda e, n=n: e.memset(U[:, :, 0:n + 2 * HW_], 0.0), writes=[n_U])
        pieces = [(0, min(w, 512))] + ([(512, w - 512)] if w > 512 else [])
        for c in range(8):
            (sg, n_sg) = sgr.next()
            for (pp, n_pp, cbase) in ((pA, n_pA, 0), (pB, n_pB, 1024)):
                def mm(e, pp=pp, cbase=cbase, c=c, lo=lo, pieces=pieces):
                    ins = None
                    for (a, ln) in pieces:
                        for kc in range(8):
                            ins = e.matmul(pp[:, a:a + ln], lhsT=w1[:, kc, cbase + c * 128:cbase + (c + 1) * 128],
                                           rhs=HT[:, kc, lo + a:lo + a + ln], start=(kc == 0), stop=(kc == 7))
                    return ins
                P.op("tensor", mm, reads=[n_HT, n_w1 + "_%d" % ((cbase + c * 128) // 512)], writes=[n_pp])
            P.op("scalar", lambda e, sg=sg, c=c, w=w: e.activation(out=sg[:, 0:w], in_=pB[:, 0:w], func=AF.Sigmoid,
                                                                 bias=cvf[:, c, 1:2], scale=1.0),
                 reads=[n_pB, n_cvf], writes=[n_sg])
            P.op("vector", lambda e, sg=sg, c=c, w=w, off=off: e.scalar_tensor_tensor(
                out=U[:, c, off:off + w], in0=pA[:, 0:w], scalar=cvf[:, c, 0:1], in1=sg[:, 0:w], op0=ALU.add, op1=ALU.mult),
                reads=[n_pA, n_cvf, n_sg], writes=[n_U])
            (dg, n_dg) = dgr.next()
            for j in range(CW):
                if j % 2 == 0:
                    P.op("scalar", lambda e, dg=dg, c=c, j=j: e.activation(out=dg[:, j, :], in_=kb.ident[:], func=AF.Identity,
                                                                         scale=cvf[:, c, 5 + j:6 + j]),
                         reads=[kb.n_ident, n_cvf], writes=[n_dg + "_%d" % j])
                else:
                    P.op("vector", lambda e, dg=dg, c=c, j=j: e.tensor_scalar(out=dg[:, j, :], in0=kb.ident[:], scalar1=cvf[:, c, 5 + j:6 + j],
                                                                           scalar2=None, op0=ALU.mult),
                         reads=[kb.n_ident, n_cvf], writes=[n_dg + "_%d" % j])

            def mmc(e, dg=dg, c=c, n=n):
                ins = None
                for j in range(CW):
                    ins = e.matmul(pV[:, 0:n], lhsT=dg[:, j, :], rhs=U[:, c, j:j + n], start=(j == 0), stop=(j == CW - 1))
                return ins
            P.op("tensor", mmc, reads=[n_dg + "_%d" % j for j in range(CW)] + [n_U], writes=[n_pV])
            P.op("scalar", lambda e, c=c, n=n: e.activation(out=V[:, c, 0:n], in_=pV[:, 0:n], func=AF.Identity, bias=cvf[:, c, 2:3],
                                                           scale=1.0), reads=[n_pV, n_cvf], writes=[n_V])
        def mmM(e, n=n):
            ins = None
            for c in range(8):
                ins = e.matmul(pM[:, 0:n], lhsT=onesm[:], rhs=V[:, c, 0:n], start=(c == 0), stop=(c == 7))
            return ins
        P.op("tensor", mmM, reads=[n_ones, n_V], writes=[n_pM])
        for c in range(8):
            (sqc, n_sqc) = sqr_.next()
            P.op("scalar", lambda e, c=c, n=n, sqc=sqc: e.activation(out=sqc[:, 0:n], in_=V[:, c, 0:n], func=AF.Square),
                 reads=[n_V], writes=[n_sqc])
            P.op("tensor", lambda e, c=c, n=n, sqc=sqc: e.matmul(pQ[:, 0:n], lhsT=onesm[:], rhs=sqc[:, 0:n], start=(c == 0), stop=(c == 7)),
                 reads=[n_ones, n_sqc], writes=[n_pQ])
        P.op("scalar", lambda e, n=n: e.activation(out=msq[:, 0:n], in_=pM[:, 0:n], func=AF.Square), reads=[n_pM], writes=[n_msq])
        P.op("vector", lambda e, n=n: e.tensor_tensor(out=rstd[:, 0:n], in0=pQ[:, 0:n], in1=msq[:, 0:n], op=ALU.subtract),
             reads=[n_pQ, n_msq], writes=[n_rstd])
        P.op("vector", lambda e, n=n: e.tensor_scalar(out=rstd[:, 0:n], in0=rstd[:, 0:n], scalar1=EPS, scalar2=None, op0=ALU.add),
             reads=[n_rstd], writes=[n_rstd])
        P.op("scalar", lambda e, n=n: e.activation(out=rstd[:, 0:n], in_=rstd[:, 0:n], func=AF.Sqrt), reads=[n_rstd], writes=[n_rstd])
        P.op("vector", lambda e, n=n: e.reciprocal(out=rstd[:, 0:n], in_=rstd[:, 0:n]), reads=[n_rstd], writes=[n_rstd])
        for c in range(8):
            P.op("vector", lambda e, c=c, n=n: e.tensor_tensor(out=V[:, c, 0:n], in0=V[:, c, 0:n], in1=pM[:, 0:n], op=ALU.subtract),
                 reads=[n_V, n_pM], writes=[n_V])
            P.op("vector", lambda e, c=c, n=n: e.tensor_tensor(out=V[:, c, 0:n], in0=V[:, c, 0:n], in1=rstd[:, 0:n], op=ALU.mult),
                 reads=[n_V, n_rstd], writes=[n_V])
            P.op("scalar", lambda e, c=c, n=n: e.activation(out=Z[:, c, 0:n], in_=V[:, c, 0:n], func=AF.Silu,
                                                           scale=cvf[:, c, 3:4], bias=cvf[:, c, 4:5]),
                 reads=[n_V, n_cvf], writes=[n_Z])
        for jt in range(n // 128):
            (ys, n_ys) = ysr.next()
            for h in range(2):
                (py, n_py) = pyr.next()

                def mmy(e, py=py, jt=jt, h=h):
                    ins = None
                    for c in range(8):
                        ins = e.matmul(py[:], lhsT=Z[:, c, jt * 128:(jt + 1) * 128], rhs=w2[:, c, h * 512:(h + 1) * 512],
                                       start=(c == 0), stop=(c == 7))
                    return ins
                P.op("tensor", mmy, reads=[n_Z, n_w2 + "_%d" % h], writes=[n_py])
                P.op("vector", lambda e, ys=ys, py=py, h=h: e.tensor_tensor(out=ys[:, h * 512:(h + 1) * 512], in0=py[:],
                                                                           in1=b2[:, h * 512:(h + 1) * 512], op=ALU.add),
                     reads=[n_py, n_b2], writes=[n_ys + "_%d" % h])
            post.run([ys[:, 0:512], ys[:, 512:1024]], [n_ys + "_0", n_ys + "_1"], t0 // 128 + jt, M, 1, xsrc, xdst)
    P.barrier()
    P.flush()


COMMON_SPECS = {
    "mod_w": [1024, 9216], "mod_b": [9216], "norm_g": [6, 1024], "mod_b_fm": [128, 72], "norm_g_fm": [128, 48],
    "ffn_w_gate": [2, 1024, 2816], "ffn_w_up": [2, 1024, 2816], "ffn_w_down": [2, 2816, 1024],
}
MIXER_SPECS = {
    0: {"na_w_qkv": [1, 1024, 3072], "na_w_o": [1, 1024, 1024], "na_bias": [16, 128, 9, 128]},
    1: {"sw_w_qkv_p": [1024, 1280], "sw_w_o": [1, 1024, 1024], "sw_sink": [1, 16], "sw_cos": [S, 32], "sw_sin": [S, 32]},
    2: {"cv_w_pw1": [1, 1024, 2048], "cv_w_pw2": [1, 1024, 1024], "cv_fm": [128, 8, 36], "cv_b_pw2": [1, 1024]},
    3: {"ga_w_qkv_p": [1024, 1536], "ga_w_o": [1, 1024, 1024], "ga_gain": [1280], "ga_cos": [S, 64], "ga_sin": [S, 64]},
}
CORE_SPECS = {"x": [S, D], "ctx": [C, D], "cfm": [128, 8, 2]}


def shared_specs(layers):
    sp = {k: [len(layers)] + v for k, v in COMMON_SPECS.items()}
    for li in layers:
        sp.update(MIXER_SPECS[li % 4])
    return sp


def shared_for(sh, layers):
    out = {}
    for k, shp in shared_specs(layers).items():
        a = sh[k]
        if k in COMMON_SPECS:
            a = np.ascontiguousarray(a[list(layers)])
        assert list(a.shape) == shp, (k, a.shape, shp)
        out[k] = a
    return out


def host_shared(inp):
    f = lambda a: np.ascontiguousarray(np.asarray(a, dtype=np.float32))
    sh = {}
    sh["mod_w"] = f(inp["mod_w"])
    sh["mod_b"] = f(inp["mod_b"])
    sh["norm_g"] = f(inp["norm_g"])
    sh["mod_b_fm"] = f(np.asarray(inp["mod_b"]).reshape(4, 72, 128).transpose(0, 2, 1))
    sh["norm_g_fm"] = f(np.asarray(inp["norm_g"]).reshape(4, 48, 128).transpose(0, 2, 1))
    for k in ("ffn_w_gate", "ffn_w_up", "ffn_w_down"):
        sh[k] = f(inp[k])
    sh["na_w_qkv"] = f(inp["na_w_qkv"])
    sh["na_w_o"] = f(inp["na_w_o"])
    sh["na_bias"] = na_bias_host(inp["na_rpb"])
    perm64 = np.concatenate([np.arange(0, 64, 2), np.arange(1, 64, 2)])
    w = np.asarray(inp["sw_w_qkv"])[0]
    cols = np.concatenate([h * 64 + perm64 for h in range(18)] + [np.arange(1152, 1280)])
    sh["sw_w_qkv_p"] = f(w[:, cols])
    sh["sw_w_o"] = f(inp["sw_w_o"])
    sh["sw_sink"] = f(inp["sw_sink"])
    sh["sw_cos"], sh["sw_sin"] = rope_tables(64)
    sh["cv_w_pw1"] = f(inp["cv_w_pw1"])
    sh["cv_w_pw2"] = f(inp["cv_w_pw2"])
    sh["cv_b_pw2"] = f(inp["cv_b_pw2"])
    b1 = np.asarray(inp["cv_b_pw1"])[0]
    vecs = [b1[:1024], b1[1024:], np.asarray(inp["cv_b_dw"])[0], np.asarray(inp["cv_ln_g"])[0], np.asarray(inp["cv_ln_b"])[0]]
    vecs += [np.asarray(inp["cv_w_dw"])[0][j] for j in range(31)]
    sh["cv_fm"] = f(np.stack([v.reshape(8, 128).T for v in vecs], axis=-1))
    perm = np.concatenate([np.arange(0, 128, 2), np.arange(1, 128, 2)])
    w = np.asarray(inp["ga_w_qkv"])[0]
    cols = np.concatenate([h * 128 + perm for h in range(10)] + [np.arange(1280, 1536)])
    sh["ga_w_qkv_p"] = f(w[:, cols])
    sh["ga_w_o"] = f(inp["ga_w_o"])
    qn = np.asarray(inp["ga_q_norm"])[0][perm]
    kn = np.asarray(inp["ga_k_norm"])[0][perm]
    sh["ga_gain"] = f(np.concatenate([qn] * 8 + [kn] * 2))
    cs, sn = rope_tables(128)
    sh["ga_cos"], sh["ga_sin"] = cs, sn
    return sh


def rope_tables(head_dim):
    t = np.arange(S)
    row = (t // GRID_W).astype(np.float32)
    col = (t % GRID_W).astype(np.float32)
    n = head_dim // 4
    freq = np.power(np.float32(10000.0), -(np.arange(n, dtype=np.float32) / np.float32(n))).astype(np.float32)
    ang = np.concatenate([row[:, None] * freq[None, :], col[:, None] * freq[None, :]], axis=-1).astype(np.float32)
    return np.cos(ang).astype(np.float32), np.sin(ang).astype(np.float32)


def host_core(inp, b):
    f = lambda a: np.ascontiguousarray(np.asarray(a, dtype=np.float32))
    c = np.asarray(inp["c"])[b].reshape(8, 128).T
    cc = np.asarray(inp["c_ctx"]).reshape(8, 128).T
    return {"x": f(inp["x"][b]), "ctx": f(inp["ctx"][b]), "cfm": f(np.stack([c, cc], axis=-1))}


def layer_plan(li):
    last = li == 3
    return [("mod", li), ("ffn", li, 0, True), ("mixer", li), ("ffn", li, 1, not last)]


def build_program(plan, layers):
    nc = bass.Bass("TRN2", target_bir_lowering=False)
    T = {}
    for k, shp in list(shared_specs(layers).items()) + list(CORE_SPECS.items()):
        T[k] = nc.dram_tensor(k, shp, F32, kind="ExternalInput").ap()
    out = nc.dram_tensor("out", [S, D], F32, kind="ExternalOutput").ap()
    outc = nc.dram_tensor("outc", [C, D], F32, kind="ExternalOutput").ap()
    XL = nc.dram_tensor("XL", [S, D], F32, kind="Internal").ap()
    XC = nc.dram_tensor("XC", [C, D], F32, kind="Internal").ap()

    def xs(tt):
        if tt < 16:
            return XL[tt * 128:(tt + 1) * 128, :], "XL_%d" % tt
        return XC[(tt - 16) * 128:(tt - 15) * 128, :], "XC_%d" % (tt - 16)

    with ExitStack() as st:
        kb = KB(nc, st)
        P = kb.P
        stage_consts(kb, st)
        for tt in range(18):
            dst, n_dst = xs(tt)
            src = T["x"][tt * 128:(tt + 1) * 128, :] if tt < 16 else T["ctx"][(tt - 16) * 128:(tt - 15) * 128, :]
            P.op("sync", lambda e, dst=dst, src=src: e.dma_start(out=dst, in_=src), writes=[n_dst], dma=True)
        M = None
        lst = None
        for step in plan:
            lloc = list(layers).index(step[1])
            if step[0] == "mod":
                if lst is not None:
                    P.barrier()
                    P.flush()
                    lst.close()
                lst = ExitStack()
                with ExitStack() as st2:
                    M = stage_mod(kb, st2, T, lloc, lst)
                    P.barrier()
                    P.flush()
            elif step[0] == "ffn":
                which = step[2]
                tiles = list(range(18)) if step[3] else list(range(16))
                with ExitStack() as st2:
                    stage_ffn(kb, st2, T, lloc, which, tiles, M, 0 if which == 0 else 2, xs, xs)
            elif step[0] == "mixer":
                kind = step[1] % 4
                with ExitStack() as st2:
                    [stage_mixer_a, stage_mixer_b, stage_mixer_c, stage_mixer_d][kind](kb, st2, T, M, xs, xs)
            else:
                raise ValueError(step)
        for tt in range(18):
            src, n_src = xs(tt)
            dst = out[tt * 128:(tt + 1) * 128, :] if tt < 16 else outc[(tt - 16) * 128:(tt - 15) * 128, :]
            ev = P.op("sync", lambda e, dst=dst, src=src: e.dma_start(out=dst, in_=src), reads=[n_src], dma=True)
            P.out_evs.append(ev)
        waits = {}
        for ev in P.out_evs:
            P._need("sync", ev, waits)
        P.ops["sync"].append((None, list(waits.items()), None))
        P.barrier()
        P.flush()
        if lst is not None:
            lst.close()
    return nc


FUSED = True


def kernel(**inputs):
    sh = host_shared(inputs)
    cores = [host_core(inputs, b) for b in range(8)]
    groups = [[0, 1, 2, 3]] if FUSED else [[0], [1], [2], [3]]
    for layers in groups:
        plan = []
        for li in layers:
            plan += layer_plan(li)
        nc = build_program(plan, layers)
        shl = shared_for(sh, layers)
        res = run_bass_kernel_spmd(nc, [{**shl, **cores[b]} for b in range(8)], core_ids=list(range(8)))
        for b in range(8):
            cores[b]["x"] = np.ascontiguousarray(res.results[b]["out"], dtype=np.float32)
            cores[b]["ctx"] = np.ascontiguousarray(res.results[b]["outc"], dtype=np.float32)
    return np.stack([cores[b]["x"] for b in range(8)], axis=0).astype(np.float32)
```
